# Optimizing a Trainium2 kernel written in Bass

```python
import math
import jax, jax.numpy as jnp
from jax import lax
import numpy as np

D_MODEL = 2048
BATCH = 1
SEQ = 8192
DEPTH = 1
DEC_BATCH = 8
DEC_SEQ = 32
PAST_LEN = 2048

CHUNK = 64
H_A = 8
DK_A = 64
DV_A = 2 * DK_A
QK_A = H_A * 2 * DK_A
W_A = H_A * DV_A
H_B = 8
DH_B = 128
W_B = H_B * DH_B
BAND_CHUNKS = 8
BAND_PAST = BAND_CHUNKS * CHUNK
REL_CLIP = 128
T5_BUCKETS = 32
T5_MAX_DIST = 128
Q_BLOCK = 128
EPS = 1e-6

kernel_name = "diff_chunkband_hybrid_stream_step"


def _split_points():
    sizes = (QK_A, QK_A, W_A, W_A, W_B, W_B, W_B, W_B, D_MODEL, D_MODEL)
    pts, acc = [], 0
    for s in sizes[:-1]:
        acc += s
        pts.append(acc)
    return pts


def _in_width():
    return 2 * QK_A + 2 * W_A + 4 * W_B + 2 * D_MODEL


def _rms_norm(x, g):
    xf = x.astype(jnp.float32)
    y = xf * lax.rsqrt(jnp.mean(xf * xf, axis=-1, keepdims=True) + EPS)
    return (y * g.astype(jnp.float32)).astype(x.dtype)


def _t5_bucket(rel):
    half = T5_BUCKETS // 2
    max_exact = half // 2
    ret = jnp.where(rel > 0, half, 0)
    n = jnp.abs(rel)
    nf = jnp.maximum(n, 1).astype(jnp.float32)
    large = max_exact + (jnp.log(nf / max_exact) / math.log(T5_MAX_DIST / max_exact)
                         * (half - max_exact)).astype(jnp.int32)
    large = jnp.minimum(large, half - 1)
    return ret + jnp.where(n < max_exact, n, large)


def _in_proj(h, w_in):
    B, S = h.shape[0], h.shape[1]
    y = jnp.einsum('bsd,de->bse', h, w_in)
    q_a, k_a, v_a, z_a, q_b, k_b, v_b, z_b, g_a, g_b = jnp.split(y, _split_points(), axis=-1)
    return (q_a.reshape(B, S, H_A, 2, DK_A), k_a.reshape(B, S, H_A, 2, DK_A),
            v_a.reshape(B, S, H_A, DV_A), z_a,
            q_b.reshape(B, S, H_B, DH_B), k_b.reshape(B, S, H_B, DH_B),
            v_b.reshape(B, S, H_B, DH_B), z_b, g_a, g_b)


def _diff_core(q, k, v, qpos, kpos, t5_bias, lam, subln_g, lam_init):
    s = jnp.einsum('bqhmd,bkhmd->bhmqk', q, k, preferred_element_type=jnp.float32) * (DK_A ** -0.5)
    bias = jnp.transpose(t5_bias[_t5_bucket(kpos[None, :] - qpos[:, None])], (2, 0, 1))
    mask = (kpos[None, :] // CHUNK) <= (qpos[:, None] // CHUNK)
    s = jnp.where(mask, s + bias[None, :, None].astype(jnp.float32), -jnp.inf)
    p = jax.nn.softmax(s, axis=-1)
    attn = p[:, :, 0] - lam * p[:, :, 1]
    o = jnp.einsum('bhqk,bkhd->bqhd', attn.astype(v.dtype), v, preferred_element_type=jnp.float32)
    o = _rms_norm(o, subln_g) * (1.0 - lam_init)
    return o.astype(q.dtype)


def _band_core(q, k, v, qpos, kpos, rel_bias):
    s = jnp.einsum('bqhd,bkhd->bhqk', q, k, preferred_element_type=jnp.float32) * (DH_B ** -0.5)
    rel = jnp.clip(kpos[None, :] - qpos[:, None], -REL_CLIP, REL_CLIP) + REL_CLIP
    bias = rel_bias[:, rel].astype(jnp.float32)
    qc = qpos // CHUNK
    kc = kpos // CHUNK
    mask = ((kpos[None, :] >= 0) & (kc[None, :] <= qc[:, None])
            & (kc[None, :] >= qc[:, None] - BAND_CHUNKS))
    p = jax.nn.softmax(jnp.where(mask, s + bias[None], -jnp.inf), axis=-1)
    o = jnp.einsum('bhqk,bkhd->bqhd', p.astype(v.dtype), v, preferred_element_type=jnp.float32)
    return o.astype(q.dtype)


def _merge(o_a, z_a, o_b, z_b, g_a, g_b, w_o_a, w_o_b, w_out):
    B, S = o_a.shape[0], o_a.shape[1]
    y_a = jnp.einsum('bse,ed->bsd', o_a.reshape(B, S, W_A) * jax.nn.silu(z_a), w_o_a)
    y_b = jnp.einsum('bse,ed->bsd', o_b.reshape(B, S, W_B) * jax.nn.silu(z_b), w_o_b)
    m = jax.nn.sigmoid(g_a) * y_a + jax.nn.sigmoid(g_b) * y_b
    return jnp.einsum('bsd,de->bse', m, w_out)


def _lambda(lq1, lk1, lq2, lk2, lam_init):
    f = jnp.float32
    return (jnp.exp(jnp.sum(lq1.astype(f) * lk1.astype(f)))
            - jnp.exp(jnp.sum(lq2.astype(f) * lk2.astype(f))) + lam_init)


def _layer_prompt(x, pre_g, post_g, w_in, t5_bias, lam, lam_init, subln_g, rel_bias, w_o_a, w_o_b, w_out):
    B, S = x.shape[0], x.shape[1]
    h = _rms_norm(x, pre_g)
    q_a, k_a, v_a, z_a, q_b, k_b, v_b, z_b, g_a, g_b = _in_proj(h, w_in)
    nb = S // Q_BLOCK
    qa_blocks = jnp.moveaxis(q_a.reshape(B, nb, Q_BLOCK, H_A, 2, DK_A), 1, 0)
    kpos = jnp.arange(S, dtype=jnp.int32)

    def a_block(args):
        qi, i = args
        qpos = i * Q_BLOCK + jnp.arange(Q_BLOCK, dtype=jnp.int32)
        return _diff_core(qi, k_a, v_a, qpos, kpos, t5_bias, lam, subln_g, lam_init)

    o_a = lax.map(a_block, (qa_blocks, jnp.arange(nb, dtype=jnp.int32)))
    o_a = jnp.moveaxis(o_a, 0, 1).reshape(B, S, H_A, DV_A)
    nc = S // CHUNK
    band = BAND_PAST + CHUNK
    kp = jnp.pad(k_b, ((0, 0), (BAND_PAST, 0), (0, 0), (0, 0)))
    vp = jnp.pad(v_b, ((0, 0), (BAND_PAST, 0), (0, 0), (0, 0)))
    qb_chunks = jnp.moveaxis(q_b.reshape(B, nc, CHUNK, H_B, DH_B), 1, 0)

    def b_chunk(args):
        qi, n = args
        start = n * CHUNK
        kb = lax.dynamic_slice_in_dim(kp, start, band, axis=1)
        vb = lax.dynamic_slice_in_dim(vp, start, band, axis=1)
        qpos = start + jnp.arange(CHUNK, dtype=jnp.int32)
        kpos_b = start - BAND_PAST + jnp.arange(band, dtype=jnp.int32)
        return _band_core(qi, kb, vb, qpos, kpos_b, rel_bias)

    o_b = lax.map(b_chunk, (qb_chunks, jnp.arange(nc, dtype=jnp.int32)))
    o_b = jnp.moveaxis(o_b, 0, 1).reshape(B, S, H_B, DH_B)
    y = x + _rms_norm(_merge(o_a, z_a, o_b, z_b, g_a, g_b, w_o_a, w_o_b, w_out), post_g)
    tail = min(BAND_PAST, S)
    return y, k_a.reshape(B, S, 2 * H_A, DK_A), v_a, k_b[:, S - tail:], v_b[:, S - tail:]


def _layer_sample(x, ck_a, cv_a, ck_b, cv_b, pre_g, post_g, w_in, t5_bias, lam, lam_init, subln_g,
                  rel_bias, w_o_a, w_o_b, w_out):
    B, S = x.shape[0], x.shape[1]
    past = ck_a.shape[1]
    win = ck_b.shape[1]
    h = _rms_norm(x, pre_g)
    q_a, k_a, v_a, z_a, q_b, k_b, v_b, z_b, g_a, g_b = _in_proj(h, w_in)
    qpos = past + jnp.arange(S, dtype=jnp.int32)
    k_full = jnp.concatenate([ck_a.reshape(B, past, H_A, 2, DK_A), k_a], axis=1)
    v_full = jnp.concatenate([cv_a, v_a], axis=1)
    kpos = jnp.arange(past + S, dtype=jnp.int32)
    o_a = _diff_core(q_a, k_full, v_full, qpos, kpos, t5_bias, lam, subln_g, lam_init)
    kb = jnp.concatenate([ck_b, k_b], axis=1)
    vb = jnp.concatenate([cv_b, v_b], axis=1)
    kpos_b = past - win + jnp.arange(win + S, dtype=jnp.int32)
    o_b = _band_core(q_b, kb, vb, qpos, kpos_b, rel_bias)
    y = x + _rms_norm(_merge(o_a, z_a, o_b, z_b, g_a, g_b, w_o_a, w_o_b, w_out), post_g)
    return y, k_a.reshape(B, S, 2 * H_A, DK_A), v_a, kb[:, S:], vb[:, S:]


def setup_inputs(seed: int = 0) -> dict:
    key = jax.random.key(seed)
    ks = jax.random.split(key, 20)
    f = jnp.float32
    wb = min(BAND_PAST, PAST_LEN)
    nrm = lambda k, shp, sc: jax.random.normal(k, shp, f) * sc
    return {
        "x_prompt": nrm(ks[0], (BATCH, SEQ, D_MODEL), 1.0),
        "x_sample": nrm(ks[1], (DEC_BATCH, DEC_SEQ, D_MODEL), 1.0),
        "cache_k_a": nrm(ks[2], (DEPTH, DEC_BATCH, PAST_LEN, 2 * H_A, DK_A), 1.0),
        "cache_v_a": nrm(ks[3], (DEPTH, DEC_BATCH, PAST_LEN, H_A, DV_A), 1.0),
        "cache_k_b": nrm(ks[4], (DEPTH, DEC_BATCH, wb, H_B, DH_B), 1.0),
        "cache_v_b": nrm(ks[5], (DEPTH, DEC_BATCH, wb, H_B, DH_B), 1.0),
        "t5_bias": nrm(ks[6], (T5_BUCKETS, H_A), 0.5),
        "pre_norm": 1.0 + nrm(ks[7], (DEPTH, D_MODEL), 0.02),
        "post_norm": 1.0 + nrm(ks[8], (DEPTH, D_MODEL), 0.02),
        "w_in": nrm(ks[9], (DEPTH, D_MODEL, _in_width()), D_MODEL ** -0.5),
        "lambda_q1": nrm(ks[10], (DEPTH, DK_A), 0.1),
        "lambda_k1": nrm(ks[11], (DEPTH, DK_A), 0.1),
        "lambda_q2": nrm(ks[12], (DEPTH, DK_A), 0.1),
        "lambda_k2": nrm(ks[13], (DEPTH, DK_A), 0.1),
        "subln_a": 1.0 + nrm(ks[14], (DEPTH, DV_A), 0.02),
        "rel_bias_b": nrm(ks[15], (DEPTH, H_B, 2 * REL_CLIP + 1), 0.5),
        "w_o_a": nrm(ks[16], (DEPTH, W_A, D_MODEL), W_A ** -0.5),
        "w_o_b": nrm(ks[17], (DEPTH, W_B, D_MODEL), W_B ** -0.5),
        "w_out": nrm(ks[18], (DEPTH, D_MODEL, D_MODEL), D_MODEL ** -0.5),
    }


def reference(x_prompt, x_sample, cache_k_a, cache_v_a, cache_k_b, cache_v_b, t5_bias, pre_norm,
              post_norm, w_in, lambda_q1, lambda_k1, lambda_q2, lambda_k2, subln_a, rel_bias_b,
              w_o_a, w_o_b, w_out):
    yp, ys = x_prompt, x_sample
    kap, vap, kbp, vbp, kas, vas, kbs, vbs = [], [], [], [], [], [], [], []
    for l in range(DEPTH):
        lam_init = 0.8 - 0.6 * math.exp(-0.3 * l)
        lam = _lambda(lambda_q1[l], lambda_k1[l], lambda_q2[l], lambda_k2[l], lam_init)
        yp, ka, va, kb, vb = _layer_prompt(yp, pre_norm[l], post_norm[l], w_in[l], t5_bias, lam, lam_init,
                                           subln_a[l], rel_bias_b[l], w_o_a[l], w_o_b[l], w_out[l])
        kap.append(ka); vap.append(va); kbp.append(kb); vbp.append(vb)
        ys, ka, va, kb, vb = _layer_sample(ys, cache_k_a[l], cache_v_a[l], cache_k_b[l], cache_v_b[l],
                                           pre_norm[l], post_norm[l], w_in[l], t5_bias, lam, lam_init,
                                           subln_a[l], rel_bias_b[l], w_o_a[l], w_o_b[l], w_out[l])
        kas.append(ka); vas.append(va); kbs.append(kb); vbs.append(vb)
    return (yp, ys, jnp.stack(kap), jnp.stack(vap), jnp.stack(kbp), jnp.stack(vbp),
            jnp.stack(kas), jnp.stack(vas), jnp.stack(kbs), jnp.stack(vbs))
```

```python
import numpy as np
import ml_dtypes
import concourse.bass as bass
import concourse.mybir as mybir
from concourse.bass_utils import run_bass_kernel_spmd

F32 = mybir.dt.float32
BF16 = mybir.dt.bfloat16
AF = mybir.ActivationFunctionType
ALU = mybir.AluOpType

NCORES = 8
D = 2048
SEQ = 8192
ROWS = 1024
NS = 32
OWN = 1152
NT_OWN = 9
PAST = 2048
WIN = 12288
NEG = -30000.0
EPS = 1e-6
LAM_INIT = 0.2
FA_LEN = 1152
GB_LEN = 768
ENGS = ("pe", "act", "dve", "pool", "sp")


class Prog:
    def __init__(self):
        self.ops = {e: [] for e in ENGS}
        self.cnt = {e: 0 for e in ENGS}
        self.last_w = {}
        self.readers = {}
        self.waited = {e: {} for e in ENGS}
        self.dma_cnt = {}

    def _deps(self, eng, reads, writes):
        deps = {}
        def add(tok):
            if tok is None:
                return
            s, v = tok
            if s == eng and eng == "pe":
                return
            if s.startswith("dma:"):
                v = 16 * self.dma_cnt[s[4:]]
            if deps.get(s, 0) < v:
                deps[s] = v
        for k in reads:
            add(self.last_w.get(k))
        for k in writes:
            add(self.last_w.get(k))
            for t in self.readers.get(k, ()):
                add(t)
        waits = []
        for s, v in deps.items():
            if self.waited[eng].get(s, 0) < v:
                self.waited[eng][s] = v
                waits.append((s, v))
        return waits

    def op(self, eng, fn, reads=(), writes=(), tag=None):
        waits = self._deps(eng, reads, writes)
        if tag is not None:
            self.dma_cnt[tag] = self.dma_cnt.get(tag, 0) + 1
            tok = ("dma:" + tag, 16 * self.dma_cnt[tag])
        else:
            self.cnt[eng] += 1
            tok = (eng, self.cnt[eng])
        self.ops[eng].append((waits, fn, tok))
        for k in writes:
            self.last_w[k] = tok
            self.readers[k] = []
        for k in reads:
            self.readers.setdefault(k, []).append(tok)
        return tok

    def barrier(self):
        allt = [(e, self.cnt[e]) for e in ENGS if self.cnt[e] > 0]
        allt += [("dma:" + t, 16 * c) for t, c in self.dma_cnt.items()]
        for e in ENGS:
            waits = []
            for s, v in allt:
                if self.waited[e].get(s, 0) < v:
                    self.waited[e][s] = v
                    waits.append((s, v))
            if waits:
                self.ops[e].append((waits, None, None))
        self.last_w = {}
        self.readers = {}

    def sem_names(self):
        names = [e for e in ENGS if e != "sp"]
        names += ["dma:" + t for t in self.dma_cnt]
        return names


class Arena:
    def __init__(self, ap_f32, nwords):
        self.ap = ap_f32
        self.n = nwords
        self.off = 0
        self.marks = []

    def push(self):
        self.marks.append(self.off)

    def pop(self):
        self.off = self.marks.pop()

    def f32(self, n):
        assert self.off + n <= self.n, ("arena overflow", self.off, n, self.n)
        a = self.ap[:, self.off:self.off + n]
        self.off += n
        return a

    def bf16(self, n):
        w = (n + 1) // 2
        return self.f32(w).bitcast(BF16)

    def top_bf16(self, n):
        w = (n + 1) // 2
        assert self.n - w >= self.off
        self.n -= w
        return self.ap[:, self.n:self.n + w].bitcast(BF16)


def _t5_bucket_np(rel):
    rel = np.asarray(rel, np.int64)
    half = 16
    max_exact = 8
    ret = np.where(rel > 0, half, 0)
    n = np.abs(rel)
    nf = np.maximum(n, 1).astype(np.float32)
    large = max_exact + (np.log(nf / np.float32(max_exact)) / np.float32(np.log(128 / max_exact))
                         * np.float32(half - max_exact)).astype(np.int32)
    large = np.minimum(large, half - 1)
    return ret + np.where(n < max_exact, n, large)


def build():
    nc = bass.Bass("TRN2", target_bir_lowering=False)

    def din(name, shape, dt=F32):
        return nc.dram_tensor(name, list(shape), dt, kind="ExternalInput").ap()

    def dout(name, shape):
        return nc.dram_tensor(name, list(shape), F32, kind="ExternalOutput").ap()

    xf = din("xf", [SEQ, D])
    xo = din("xo", [OWN, D])
    xh = din("xh", [512, D])
    w_in = din("w_in", [D, WIN])
    w_oa = din("w_oa", [1024, D])
    w_ob = din("w_ob", [1024, D])
    w_out = din("w_out", [D, D])
    cka = din("cka", [PAST, 1024])
    cva = din("cva", [PAST, 1024])
    ckb = din("ckb", [512, 1024])
    cvb = din("cvb", [512, 1024])
    pre = din("pre", [1, D])
    post = din("post", [1, D])
    subln = din("subln", [1, 128])
    lam4 = din("lam4", [1, 256])
    t5 = din("t5", [32, 8])
    relbT = din("relbT", [384, 8])
    c_ident = din("c_ident", [128, 128])
    c_antij = din("c_antij", [128, 128])
    c_oha = din("c_oha", [32, FA_LEN])
    c_ohb = din("c_ohb", [384, GB_LEN])
    c_maskA = din("c_maskA", [5, 128, 512])
    c_maskB = din("c_maskB", [5, 128, 128])
    c_visA = din("c_visA", [128, 128])
    c_visB = din("c_visB", [128, 16])

    o_y = dout("y", [1056, D])
    o_ka = dout("ka", [1056, 1024])
    o_va = dout("va", [1056, 1024])
    o_kbp = dout("kbp", [512, 1024])
    o_vbp = dout("vbp", [512, 1024])
    o_kbs = dout("kbs", [512, 1024])
    o_vbs = dout("vbs", [512, 1024])

    KTs = nc.dram_tensor("KTs", [8, 128, 2 * SEQ], BF16).ap()
    Vs = nc.dram_tensor("Vs", [8, 128, 128, 128], BF16).ap()
    fa_s = nc.dram_tensor("fa_s", [8, FA_LEN], F32).ap()
    gb_s = nc.dram_tensor("gb_s", [8, GB_LEN], F32).ap()

    P = Prog()
    pid_holder = {}

    ARENA_WORDS = 52000
    with (
        nc.sbuf_tensor("arena", [128, ARENA_WORDS], F32) as arena_t,
        nc.psum_tensor("psA", [128, 1024], F32) as psA,
        nc.psum_tensor("psB", [128, 1024], F32) as psB,
        nc.psum_tensor("psO", [128, 1536], F32) as psO,
        nc.psum_tensor("psT", [128, 1024], BF16) as psT,
    ):
        AR = Arena(arena_t[:, :], ARENA_WORDS)
        banks = []
        for nm, t, nb in (("A", psA, 2), ("B", psB, 2), ("O", psO, 3)):
            for i in range(nb):
                banks.append((t[:, 512 * i:512 * (i + 1)], ("ps", nm, i)))
        bank_rr = [0]

        def next_bank():
            b = banks[bank_rr[0] % len(banks)]
            bank_rr[0] += 1
            return b
        KT_T = ("ps", "T", 0)

        def dma(q, out, in_, reads, writes, tag):
            P.op(q, lambda E: E.dma_start(out=out, in_=in_), reads, writes, tag=tag)

        def mm(out, lhsT, rhs, start, stop, reads, writes):
            P.op("pe", lambda E: E.matmul(out, lhsT, rhs, start=start, stop=stop,
                                          skip_group_check=True), reads, writes)

        def tr(out, in_, ident, reads, writes):
            P.op("pe", lambda E: E.transpose(out, in_, ident), reads, writes)

        def act(out, in_, func, reads, writes, bias=None, scale=None, accum=None):
            kw = {}
            if bias is not None:
                kw["bias"] = bias
            if scale is not None:
                kw["scale"] = scale
            if accum is not None:
                kw["accum_out"] = accum
            P.op("act", lambda E: E.activation(out, in_, func, **kw), reads, writes)

        def ts(eng, out, in0, s1, s2, op0, op1, reads, writes, accum=None):
            if accum is None:
                P.op(eng, lambda E: E.tensor_scalar(out, in0, s1, s2, op0, op1), reads, writes)
            else:
                P.op(eng, lambda E: E.tensor_scalar(out, in0, s1, s2, op0, op1, accum_out=accum),
                     reads, writes)

        def tt(eng, out, in0, in1, op, reads, writes):
            P.op(eng, lambda E: E.tensor_tensor(out, in0, in1, op), reads, writes)

        def stt(out, in0, scalar, in1, op0, op1, reads, writes):
            P.op("dve", lambda E: E.scalar_tensor_tensor(out, in0, scalar, in1, op0, op1), reads, writes)

        def cp(eng, out, in_, reads, writes):
            if eng == "act":
                P.op("act", lambda E: E.copy(out, in_), reads, writes)
            else:
                P.op(eng, lambda E: E.tensor_copy(out, in_), reads, writes)

        def memset(eng, ap, val, writes):
            P.op(eng, lambda E: E.memset(ap, val), (), writes)

        evac_rr = [0]

        def evac(out, in_, reads, writes):
            e = ("act", "dve")[evac_rr[0] % 2]
            evac_rr[0] += 1
            cp(e, out, in_, reads, writes)

        ident_f = AR.f32(128)
        antij = AR.f32(128)
        ident_b = AR.bf16(128)
        gvec = AR.f32(2048)
        sublnb = AR.f32(128)
        lamt = AR.f32(256)
        lamtmp = AR.f32(64)
        lamc = AR.f32(8)
        visA = AR.f32(128)
        visB = AR.f32(16)
        sublnc = AR.f32(2)
        ones_f = AR.f32(128)
        onesm_f = AR.f32(128)
        ones_b = AR.bf16(128)

        dma("sp", ident_f, c_ident, (), ["ident_f"], "c0")
        dma("sp", antij, c_antij, (), ["antij"], "c0")
        dma("sp", visA, c_visA, (), ["visA"], "c0")
        dma("sp", visB, c_visB, (), ["visB"], "c0")
        dma("sp", gvec, pre.partition_broadcast(128), (), ["gvec"], "c0")
        dma("sp", sublnb, subln.partition_broadcast(128), (), ["sublnb"], "c0")
        dma("sp", lamt, lam4.partition_broadcast(128), (), ["lamt"], "c0")
        dma("sp", sublnc[:, 0:1], subln.rearrange("o d -> d o"), (), ["sublnc"], "c0")
        AR.push()
        t5_sb = AR.f32(8)
        oha_sb = AR.f32(FA_LEN)
        rb_sb = AR.f32(24)
        ohb_sb = AR.f32(3 * GB_LEN)
        vec_sb = AR.f32(FA_LEN)
        dma("sp", t5_sb[0:32, :], t5, (), ["t5_sb"], "c0")
        dma("sp", oha_sb[0:32, :], c_oha, (), ["oha_sb"], "c0")
        dma("sp", rb_sb.rearrange("p (c h) -> p c h", c=3), relbT.rearrange("(c p) h -> p c h", p=128),
            (), ["rb_sb"], "c0")
        dma("sp", ohb_sb.rearrange("p (c n) -> p c n", c=3), c_ohb.rearrange("(c p) n -> p c n", p=128),
            (), ["ohb_sb"], "c0")
        P.barrier()
        cp("dve", ident_b, ident_f, ["ident_f"], ["ident_b"])
        ts("dve", sublnb, sublnb, 1.0 - LAM_INIT, None, ALU.mult, ALU.bypass, ["sublnb"], ["sublnb"])
        ts("dve", sublnc[:, 0:1], sublnc[:, 0:1], 1.0 - LAM_INIT, None, ALU.mult, ALU.bypass, ["sublnc"], ["sublnc"])
        memset("dve", ones_f, 1.0, ["ones_f"])
        memset("dve", onesm_f, 1.0 / 128, ["onesm_f"])
        memset("dve", ones_b, 1.0, ["ones_b"])
        for i in range(2):
            P.op("dve", (lambda i: lambda E: E.scalar_tensor_tensor(
                lamtmp, lamt[:, 128 * i:128 * i + 64], 1.0, lamt[:, 128 * i + 64:128 * i + 128],
                ALU.mult, ALU.mult, accum_out=lamc[:, i:i + 1]))(i),
                ["lamt"], ["lamtmp", "lamc"])
        act(lamc[:, 2:4], lamc[:, 0:2], AF.Exp, ["lamc"], ["lamc"])
        tt("dve", lamc[:, 4:5], lamc[:, 2:3], lamc[:, 3:4], ALU.subtract, ["lamc"], ["lamc"])
        ts("dve", lamc[:, 5:6], lamc[:, 4:5], LAM_INIT, -1.0, ALU.add, ALU.mult, ["lamc"], ["lamc"])
        neg_lam = lamc[:, 5:6]

        for j in range(3):
            bk, bkey = next_bank()
            mm(bk[0:8, 0:384], t5_sb[0:32, 0:8], oha_sb[0:32, 384 * j:384 * (j + 1)], True, True,
               ["t5_sb", "oha_sb"], [bkey])
            cp("dve", vec_sb[0:8, 384 * j:384 * (j + 1)], bk[0:8, 0:384], [bkey], ["vec_sb"])
        dma("sp", fa_s, vec_sb[0:8, 0:FA_LEN], ["vec_sb"], ["fa_s"], "c1")
        for j in range(2):
            bk, bkey = next_bank()
            for c in range(3):
                mm(bk[0:8, 0:384], rb_sb[:, 8 * c:8 * c + 8], ohb_sb[:, GB_LEN * c + 384 * j:GB_LEN * c + 384 * (j + 1)],
                   c == 0, c == 2, ["rb_sb", "ohb_sb"], [bkey])
            cp("dve", vec_sb[0:8, 384 * j:384 * (j + 1)], bk[0:8, 0:384], [bkey, "fa_s"], ["vec_sb"])
        dma("sp", gb_s, vec_sb[0:8, 0:GB_LEN], ["vec_sb"], ["gb_s"], "c1")
        P.barrier()
        AR.pop()

        def norm_jobs(src, ntiles, hT, hT_key, tagbase, xs, hb, ssq, junk):
            hTv = hT.rearrange("p (c t) -> p c t", c=16)
            jobs = []

            def mk(t, part):
                def job():
                    sl = t % len(xs)
                    xk = ("xs", tagbase, sl)
                    hk = ("hb", tagbase, t % 2)
                    sk = ("ssq", tagbase, t % 2)
                    sq = ssq[:, 2 * (t % 2):2 * (t % 2) + 2]
                    if part == "norm":
                        dma("sp", xs[sl], src[128 * t:128 * (t + 1), :], (), [xk], "x%s%d" % (tagbase, sl))
                        P.op("dve", lambda E: E.scalar_tensor_tensor(
                            junk, xs[sl], 1.0, xs[sl], ALU.mult, ALU.mult, accum_out=sq[:, 0:1]),
                            [xk], [("junk", tagbase), sk])
                        act(sq[:, 0:1], sq[:, 0:1], AF.Ln, [sk], [sk], bias=EPS, scale=1.0 / D)
                        act(sq[:, 1:2], sq[:, 0:1], AF.Exp, [sk], [sk], scale=-0.5)
                        stt(hb[t % 2], xs[sl], sq[:, 1:2], gvec, ALU.mult, ALU.mult, [xk, sk, "gvec"], [hk])
                        return
                    half = part
                    for j in range(8):
                        dc = 8 * half + j
                        tr(psT[:, 128 * j:128 * (j + 1)], hb[t % 2][:, 128 * dc:128 * (dc + 1)], ident_b,
                           [hk, "ident_b"], [KT_T])
                    evac(hTv[:, 8 * half:8 * half + 8, 128 * t:128 * (t + 1)],
                         psT.rearrange("p (c t) -> p c t", c=8), [KT_T], [(hT_key, t)])
                return job
            for t in range(ntiles):
                if t == 0:
                    jobs.append(mk(0, "norm"))
                if t + 1 < ntiles:
                    jobs.append(mk(t + 1, "norm"))
                jobs.append(mk(t, 0))
                jobs.append(mk(t, 1))
            return jobs

        def norm_tiles(src, ntiles, hT, hT_key, tagbase, xs, hb, ssq, junk):
            for j in norm_jobs(src, ntiles, hT, hT_key, tagbase, xs, hb, ssq, junk):
                j()

        AR.push()
        Wkv = AR.bf16(16 * 2048)
        wst = [AR.f32(2048) for _ in range(2)]
        xs1 = [AR.f32(2048) for _ in range(3)]
        hb1 = [AR.bf16(2048) for _ in range(2)]
        ssq1 = AR.f32(4)
        junk1 = AR.bf16(2048)
        hTb = [AR.bf16(16 * 512) for _ in range(2)]
        ktb = [AR.bf16(8 * 512) for _ in range(2)]
        vbk = [AR.bf16(4 * 1024) for _ in range(2)]
        Wkvv = Wkv.rearrange("p (c n) -> p c n", c=16)
        for dc in range(16):
            sl = dc % 2
            dma("pool", Wkvv[:, dc, :], w_in[128 * dc:128 * (dc + 1), 1024:3072], (), [("Wkv", dc)], "wkv")
        Wkeys = [("Wkv", dc) for dc in range(16)]
        KTsv = KTs.rearrange("h p n -> p h n")
        def p1_jobs(b):
            return norm_jobs(xf[512 * b:512 * (b + 1), :], 4, hTb[b % 2], ("hTb", b % 2), "p1", xs1, hb1, ssq1, junk1)
        for j in p1_jobs(0):
            j()
        for b in range(16):
            hT = hTb[b % 2]
            hTkey = ("hTb", b % 2)
            pending = p1_jobs(b + 1) if b + 1 < 16 else []
            hTv = hT.rearrange("p (c t) -> p c t", c=16)
            hkeys = [(hTkey, t) for t in range(4)]
            kt = ktb[b % 2].rearrange("p (h t) -> p h t", h=8)
            ktk = ("ktb", b % 2)
            vb_ = vbk[b % 2].rearrange("p (t n) -> p t n", t=4)
            vk = ("vbk", b % 2)
            gcount = [0]

            def after_group():
                gcount[0] += 1
                if pending:
                    pending.pop(0)()
            for h in range(8):
                bk, bkey = next_bank()
                for dc in range(16):
                    mm(bk, Wkvv[:, dc, 128 * h:128 * (h + 1)], hTv[:, dc, :], dc == 0, dc == 15,
                       hkeys + [Wkeys[dc]], [bkey])
                evac(kt[:, h, :], bk, [bkey], [(ktk, h)])
                after_group()
            for rep in range(2):
                dma("pool", KTsv[:, :, rep * SEQ + 512 * b:rep * SEQ + 512 * (b + 1)], kt,
                    [(ktk, h) for h in range(8)], [("KTs", b, rep)], "kts%d" % (b % 2))
            for t in range(4):
                for half in range(2):
                    bk, bkey = next_bank()
                    for dc in range(16):
                        mm(bk, hTv[:, dc, 128 * t:128 * (t + 1)], Wkvv[:, dc, 1024 + 512 * half:1024 + 512 * (half + 1)],
                           dc == 0, dc == 15, hkeys + [Wkeys[dc]], [bkey])
                    evac(vb_[:, t, 512 * half:512 * (half + 1)], bk, [bkey], [(vk, t, half)])
                    after_group()
                for rep in range(2):
                    dma("pool", Vs[:, :, 64 * rep + 4 * b + t, :].rearrange("h p d -> p h d"),
                        vb_[:, t, :].rearrange("p (h d) -> p h d", h=8),
                        [(vk, t, 0), (vk, t, 1)], [("Vs", b, t, rep)], "vs%d" % (b % 2))
            while pending:
                pending.pop(0)()
        P.barrier()
        AR.pop()

        ozTa = AR.bf16(8 * OWN)
        ozTav = ozTa.rearrange("p (h t) -> p h t", h=8)

        def load_wgroup(wsrc_cols, ncol, wst_f, wbf, key, tag, nk=16, stkey=None):
            half = nk // 2
            for hh in range(2):
                dma("pool", wbf[:, hh * half * ncol:(hh + 1) * half * ncol].rearrange("p (c n) -> p c n", c=half),
                    wsrc_cols[128 * half * hh:128 * half * (hh + 1), :].rearrange("(c p) n -> p c n", p=128),
                    (), [(key, "bf", hh)], tag)
            return [(key, "bf", 0), (key, "bf", 1)]

        def own_prep(AR, with_halo):
            hTo = AR.bf16(16 * OWN)
            hTh = AR.bf16(16 * 512) if with_halo else None
            AR.push()
            xs = [AR.f32(2048) for _ in range(2)]
            hb = [AR.bf16(2048) for _ in range(2)]
            ssq = AR.f32(4)
            junk = AR.bf16(2048)
            norm_tiles(xo, NT_OWN, hTo, "hTo", "own", xs, hb, ssq, junk)
            if with_halo:
                norm_tiles(xh, 4, hTh, "hTh", "own", xs, hb, ssq, junk)
            P.barrier()
            AR.pop()
            return hTo, hTh

        tok_blocks = [(0, 512), (512, 512), (1024, 128)]

        AR.push()
        qaT = AR.bf16(8 * OWN)
        szga = AR.bf16(8 * OWN)
        kaTs = AR.bf16(8 * 32)
        vas = AR.bf16(8 * 130)
        qaTv = qaT.rearrange("p (h t) -> p h t", h=8)
        szgaTv = szga.rearrange("p (h t) -> p h t", h=8)
        kaTsv = kaTs.rearrange("p (h t) -> p h t", h=8)
        vasv = vas.rearrange("p (h d) -> p h d", h=8)
        memset("pool", vasv[:, :, 128:130], 1.0, ["vas1"])

        AR.push()
        hTo, _ = own_prep(AR, False)
        hTov = hTo.rearrange("p (c t) -> p c t", c=16)
        hokeys = [("hTo", t) for t in range(NT_OWN)]
        GC = 256
        wstf = [AR.f32(16 * GC) for _ in range(2)]
        wbfs = [AR.bf16(16 * GC) for _ in range(2)]
        ost = [AR.f32(GC) for _ in range(3)]
        sgt = [AR.f32(GC) for _ in range(2)]
        ost_i = [0]

        def feat_major(wbf, wkeys, ncol, col0, dst_fn, scale, tokv, tokkeys, blocks, evac_fn=None):
            wv = wbf.rearrange("p (c n) -> p c n", c=16)
            for cc in range(ncol // 128):
                for (t0, n) in blocks:
                    bk, bkey = next_bank()
                    for dc in range(16):
                        mm(bk[:, 0:n], wv[:, dc, 128 * cc:128 * (cc + 1)], tokv[:, dc, t0:t0 + n], dc == 0, dc == 15,
                           wkeys + tokkeys, [bkey])
                    dst, dkey = dst_fn(col0 + 128 * cc, t0, n)
                    if evac_fn is not None:
                        evac_fn(dst, dkey, bk[:, 0:n], bkey, n)
                    elif scale is None:
                        evac(dst, bk[:, 0:n], [bkey], [dkey])
                    else:
                        ts("dve", dst, bk[:, 0:n], scale, None, ALU.mult, ALU.bypass, [bkey], [dkey])

        def tok_major(wbf, wkeys, ncol, tokv, tokkeys, ntiles, sink):
            wv = wbf.rearrange("p (c n) -> p c n", c=16)
            for t in range(ntiles):
                bk, bkey = next_bank()
                for dc in range(16):
                    mm(bk[:, 0:ncol], tokv[:, dc, 128 * t:128 * (t + 1)], wv[:, dc, :], dc == 0, dc == 15,
                       wkeys + tokkeys, [bkey])
                sink(t, bk[:, 0:ncol], bkey)

        def out_rows(dst, col0, ncol, src_ap, skey, t, nrows_total):
            r0 = 128 * t
            n = min(128, nrows_total - r0)
            if n <= 0:
                return
            dma("sp", dst[r0:r0 + n, col0:col0 + ncol], src_ap[0:n, :], [skey], [("out", id(dst), t, col0)], "outst%d" % skey[1])

        def silu_to(dst, psum, bkey, dkey, mulb=None, mkey=None):
            if mulb is None:
                act(dst, psum, AF.Silu, [bkey], [dkey])
                return
            s_ = sgt[ost_i[0] % 2]
            sk = ("sgt", ost_i[0] % 2)
            ost_i[0] += 1
            n = psum.shape[-1]
            act(s_[:, 0:n], psum, AF.Silu, [bkey], [sk])
            tt("dve", dst, s_[:, 0:n], mulb, ALU.mult, [sk, mkey], [dkey])

        gi = [0]

        def wgroup(col0):
            sl = gi[0] % 2
            gi[0] += 1
            keys = load_wgroup(w_in[:, col0:col0 + GC], GC, wstf[sl], wbfs[sl], ("wg", sl), "wg%d" % sl)
            return wbfs[sl], keys

        sgt2 = [AR.f32(512) for _ in range(2)]

        for g in range(4096 // GC):
            col0 = g * GC
            wbf, wkeys = wgroup(col0)
            if col0 < 1024:
                feat_major(wbf, wkeys, GC, col0,
                           lambda c, t0, n: (qaTv[:, c // 128, t0:t0 + n], ("qaT", c // 128, t0)),
                           0.125, hTov, hokeys, tok_blocks)
            elif col0 < 2048:
                c1 = col0 - 1024

                def sink(t, ps, bkey, c1=c1):
                    o = ost[ost_i[0] % 3]
                    ok = ("ost", ost_i[0] % 3)
                    ost_i[0] += 1
                    evac(o[:, 0:GC], ps, [bkey], [ok])
                    out_rows(o_ka, c1, GC, o, ok, t, 1056)
                tok_major(wbf, wkeys, GC, hTov, hokeys, NT_OWN, sink)
                feat_major(wbf, wkeys, GC, c1,
                           lambda c, t0, n: (kaTsv[:, c // 128, 0:32], ("kaTs", c // 128)),
                           None, hTov, hokeys, [(1024, 32)])
            elif col0 < 3072:
                c1 = col0 - 2048

                def sink(t, ps, bkey, c1=c1):
                    o = ost[ost_i[0] % 3]
                    ok = ("ost", ost_i[0] % 3)
                    ost_i[0] += 1
                    evac(o[:, 0:GC], ps, [bkey], [ok])
                    out_rows(o_va, c1, GC, o, ok, t, 1056)
                    if t == 8:
                        cp("pool", vasv[0:32, c1 // 128:c1 // 128 + 2, 0:128],
                           o[0:32, 0:GC].rearrange("p (h d) -> p h d", h=2), [ok], [("vas", c1)])
                tok_major(wbf, wkeys, GC, hTov, hokeys, NT_OWN, sink)
            else:
                c1 = col0 - 3072

                def zevac(dst, dkey, ps, bkey, n):
                    s_ = sgt2[ost_i[0] % 2]
                    sk = ("sgt2", ost_i[0] % 2)
                    ost_i[0] += 1
                    act(s_[:, 0:n], ps, AF.Silu, [bkey], [sk])
                    ts("dve", dst, s_[:, 0:n], sublnc[:, 0:1], None, ALU.mult, ALU.bypass, [sk, "sublnc"], [dkey])
                feat_major(wbf, wkeys, GC, c1,
                           lambda c, t0, n: (szgaTv[:, c // 128, t0:t0 + n], ("szgaT", c // 128, t0)),
                           None, hTov, hokeys, tok_blocks, evac_fn=zevac)
        P.barrier()
        AR.pop()

        def attention_unit(nmaps, nq, qT_fn, tiles, Pbufs, tmpb, epilogue, ukey, abase=0):
            nsb = (nq + 127) // 128
            T = len(tiles)
            Sps = [(psA, [("ps", "A", 0), ("ps", "A", 1)]), (psB, [("ps", "B", 0), ("ps", "B", 1)])]
            Okeys = [("ps", "O", 0), ("ps", "O", 1), ("ps", "O", 2)]

            def qk(t):
                tl = tiles[t]
                sp, skeys = Sps[t % 2]
                for m in range(nmaps):
                    mm(sp[0:tl["nk"], 512 * m:512 * m + nq], tl["kT"](m), qT_fn(m), True, True,
                       tl["keys"], [skeys[m]])

            def softmax_exp(t):
                tl = tiles[t]
                nk = tl["nk"]
                sp, skeys = Sps[t % 2]
                pb = Pbufs[t % len(Pbufs)]
                pk = ("Pb", t % len(Pbufs))
                pbv = pb.rearrange("p (m q) -> p m q", m=2)
                spv = sp.rearrange("p (m q) -> p m q", m=2)
                if tl["bias"] is not None:
                    tb = tmpb[t % len(tmpb)]
                    tk = ("tmpb", t % len(tmpb))
                    tbv = tb.rearrange("p (m q) -> p m q", m=2)
                    for m in range(nmaps):
                        tt("dve", tbv[0:nk, m, 0:nq], spv[0:nk, m, 0:nq], tl["bias"][0:nk, 0:nq], ALU.add,
                           [skeys[m], tl["biaskey"]], [tk])
                    act(pbv[0:nk, 0:nmaps, 0:nq], tbv[0:nk, 0:nmaps, 0:nq], AF.Exp, [tk, "visA", "visB"], [pk],
                        bias=tl["vis"])
                else:
                    act(pbv[0:nk, 0:nmaps, 0:nq], spv[0:nk, 0:nmaps, 0:nq], AF.Exp,
                        skeys[0:nmaps] + ["visA", "visB"], [pk], bias=tl["vis"])

            def pv(t):
                tl = tiles[t]
                nk = tl["nk"]
                pb = Pbufs[t % len(Pbufs)]
                pk = ("Pb", t % len(Pbufs))
                pbv = pb.rearrange("p (m q) -> p m q", m=2)
                for m in range(nmaps):
                    for sb in range(nsb):
                        a = abase + m * nsb + sb
                        nqs = min(128, nq - 128 * sb)
                        first_in_bank = (t == 0 and a % 3 == 0)
                        mm(psO[0:nqs, 512 * (a // 3) + 130 * (a % 3):512 * (a // 3) + 130 * (a % 3) + 130],
                           pbv[0:nk, m, 128 * sb:128 * sb + nqs], tl["v"], first_in_bank, t == T - 1,
                           [pk] + tl["keys"], [Okeys[a // 3]])

            qk(0)
            for t in range(T):
                if t + 1 < T:
                    qk(t + 1)
                softmax_exp(t)
                if t >= 1:
                    pv(t - 1)
            pv(T - 1)
            epilogue(nsb, Okeys, abase)

        def attention_unit_T(nq, qT_fn, tiles, Pbufs, tmpb, lacc, eA, eB, h, row0):
            T = len(tiles)
            Sps = [(psA, [("ps", "A", 0), ("ps", "A", 1)]), (psB, [("ps", "B", 0), ("ps", "B", 1)])]
            Okeys = [("ps", "O", 0), ("ps", "O", 1)]
            laccv = lacc.rearrange("p (m q) -> p m q", m=2)
            lk = "lacc"
            Lps = [psO[:, 1024:1536], psT.bitcast(F32)]
            Lkeys = [("ps", "O", 2), KT_T]
            dve_started = [False]

            def qk(t):
                tl = tiles[t]
                sp, skeys = Sps[t % 2]
                for m in range(2):
                    mm(sp[0:tl["nk"], 512 * m:512 * m + nq], tl["kT"](m), qT_fn(m), True, True,
                       tl["keys"], [skeys[m]])

            def softmax_exp(t):
                tl = tiles[t]
                nk = tl["nk"]
                sp, skeys = Sps[t % 2]
                pb = Pbufs[t % len(Pbufs)]
                pk = ("Pb", t % len(Pbufs))
                pbv = pb.rearrange("p (m q) -> p m q", m=2)
                spv = sp.rearrange("p (m q) -> p m q", m=2)
                if tl["bias"] is not None:
                    tb = tmpb[t % len(tmpb)]
                    tk = ("tmpb", t % len(tmpb))
                    tbv = tb.rearrange("p (m q) -> p m q", m=2)
                    for m in range(2):
                        tt("dve", tbv[0:nk, m, 0:nq], spv[0:nk, m, 0:nq], tl["bias"][0:nk, 0:nq], ALU.add,
                           [skeys[m], tl["biaskey"]], [tk])
                    act(pbv[0:nk, :, 0:nq], tbv[0:nk, :, 0:nq], AF.Exp, [tk, "visA", "visB"], [pk], bias=tl["vis"])
                else:
                    act(pbv[0:nk, :, 0:nq], spv[0:nk, :, 0:nq], AF.Exp, skeys + ["visA", "visB"], [pk], bias=tl["vis"])

            def pv(t):
                tl = tiles[t]
                nk = tl["nk"]
                pb = Pbufs[t % len(Pbufs)]
                pk = ("Pb", t % len(Pbufs))
                pbv = pb.rearrange("p (m q) -> p m q", m=2)
                on_pe = (t % 3 == 0)
                if on_pe:
                    for m in range(2):
                        mm(Lps[m][:, 0:nq], ones_b[0:nk, :], pbv[0:nk, m, 0:nq], t == 0, False,
                           [pk, "ones_b"], [Lkeys[m]])
                elif not dve_started[0]:
                    dve_started[0] = True
                    cp("dve", laccv[0:nk, :, 0:nq], pbv[0:nk, :, 0:nq], [pk], [lk])
                else:
                    tt("dve", laccv[0:nk, :, 0:nq], laccv[0:nk, :, 0:nq], pbv[0:nk, :, 0:nq], ALU.add, [pk, lk], [lk])
                for m in range(2):
                    mm(psO[:, 512 * m:512 * m + nq], tl["v"][0:nk, 0:128], pbv[0:nk, m, 0:nq], t == 0, t == T - 1,
                       [pk] + tl["keys"], [Okeys[m]])

            qk(0)
            for t in range(T):
                if t + 1 < T:
                    qk(t + 1)
                softmax_exp(t)
                if t >= 1:
                    pv(t - 1)
            pv(T - 1)
            sp, skeys = Sps[T % 2]
            spv = sp.rearrange("p (m q) -> p m q", m=2)
            eAv = eA.rearrange("p (m q) -> p m q", m=2)
            eBv = eB.rearrange("p (m q) -> p m q", m=2)
            for m in range(2):
                mm(Lps[m][:, 0:nq], ones_f, laccv[:, m, 0:nq], False, True, [lk, "ones_f"], [Lkeys[m]])
            for m in range(2):
                act(eAv[:, m, 0:nq], Lps[m][:, 0:nq], AF.Ln, [Lkeys[m]], ["eA"])
            act(eAv[:, :, 0:nq], eAv[:, :, 0:nq], AF.Exp, ["eA"], ["eA"], scale=-1.0)
            tt("dve", eBv[:, 1, 0:nq], psO[:, 512:512 + nq], eAv[:, 1, 0:nq], ALU.mult, [Okeys[1], "eA"], ["eB"])
            tt("dve", eBv[:, 0, 0:nq], psO[:, 0:nq], eAv[:, 0, 0:nq], ALU.mult, [Okeys[0], "eA"], ["eB"])
            stt(eBv[:, 0, 0:nq], eBv[:, 1, 0:nq], neg_lam, eBv[:, 0, 0:nq], ALU.mult, ALU.add, ["eB", "lamc"], ["eB"])
            tt("dve", eBv[:, 1, 0:nq], eBv[:, 0, 0:nq], eBv[:, 0, 0:nq], ALU.mult, ["eB"], ["eB"])
            mm(sp[:, 0:nq], onesm_f, eBv[:, 1, 0:nq], True, True, ["eB", "onesm_f"], [skeys[0]])
            act(eAv[:, 0, 0:nq], sp[:, 0:nq], AF.Ln, [skeys[0]], ["eA"], bias=EPS)
            act(eAv[:, 0, 0:nq], eAv[:, 0, 0:nq], AF.Exp, ["eA"], ["eA"], scale=-0.5)
            tt("dve", eBv[:, 0, 0:nq], eBv[:, 0, 0:nq], eAv[:, 0, 0:nq], ALU.mult, ["eB", "eA"], ["eB"])
            szk = [("szgaT", h, b0) for b0 in (0, 512, 1024) if b0 < row0 + nq and b0 + 512 > row0]
            tt("dve", ozTav[:, h, row0:row0 + nq], eBv[:, 0, 0:nq], szgaTv[:, h, row0:row0 + nq], ALU.mult,
               ["eB"] + szk, [("ozTa", h, row0)])

        def acc_ap(a, n):
            return psO[0:n, 512 * (a // 3) + 130 * (a % 3):512 * (a // 3) + 130 * (a % 3) + 130]

        def make_epilogue_A(h, row0, nq, osb, ep):
            def epilogue(nsb, Okeys):
                for sb in range(nsb):
                    n = min(128, nq - 128 * sb)
                    a0, a1 = sb, nsb + sb
                    k = ("ep", sb % 2)
                    e = ep[sb % 2]
                    o1 = osb[sb % 2]
                    cp("act", o1[0:n, 0:130], acc_ap(a0, n), [Okeys[a0 // 3]], [k])
                    cp("act", o1[0:n, 130:260], acc_ap(a1, n), [Okeys[a1 // 3]], [k])
                    P.op("dve", lambda E, e=e, o1=o1, n=n: E.reciprocal(e[0:n, 0:1], o1[0:n, 128:129]), [k], [k])
                    P.op("dve", lambda E, e=e, o1=o1, n=n: E.reciprocal(e[0:n, 1:2], o1[0:n, 258:259]), [k], [k])
                    tt("dve", e[0:n, 1:2], e[0:n, 1:2], neg_lam[0:n, :], ALU.mult, [k, "lamc"], [k])
                    ts("dve", o1[0:n, 130:258], o1[0:n, 130:258], e[0:n, 1:2], None, ALU.mult, ALU.bypass, [k], [k])
                    stt(o1[0:n, 0:128], o1[0:n, 0:128], e[0:n, 0:1], o1[0:n, 130:258], ALU.mult, ALU.add, [k], [k])
                    P.op("dve", lambda E, e=e, o1=o1, n=n: E.scalar_tensor_tensor(
                        o1[0:n, 130:258], o1[0:n, 0:128], 1.0, o1[0:n, 0:128], ALU.mult, ALU.mult,
                        accum_out=e[0:n, 2:3]), [k], [k])
                    act(e[0:n, 2:3], e[0:n, 2:3], AF.Ln, [k], [k], bias=EPS, scale=1.0 / 128)
                    act(e[0:n, 3:4], e[0:n, 2:3], AF.Exp, [k], [k], scale=-0.5)
                    r = row0 + 128 * sb
                    tix, rr = r // 128, r % 128
                    ob = o1[:, 260:324].bitcast(BF16)
                    stt(ob[0:n, :], o1[0:n, 0:128], e[0:n, 3:4], szgav[rr:rr + n, tix, 128 * h:128 * (h + 1)],
                        ALU.mult, ALU.mult, [k, ("szga", tix, (128 * h) // GC * GC)], [k])
                    tr(psT[:, 0:n], ob[0:n, :], ident_b[0:n, 0:n], [k, "ident_b"], [KT_T])
                    cp("act", ozTav[:, h, r:r + n], psT[:, 0:n], [KT_T], [("ozTa", h, r)])
            return epilogue

        def load_bias_tile(dstb, dkey, hk, hkkey, src, offset, nq, mask_ap, maskkey, nk=128):
            hsrc = bass.AP(tensor=src.tensor, offset=src.offset + offset, ap=[[1, 128], [1, nq]])
            dma("sp", hk[:, 0:nq], hsrc, ["fa_s", "gb_s"], [hkkey], "hk")
            bk, bkey = banks[6]
            mm(bk[:, 0:nq], antij, hk[:, 0:nq], True, True, [hkkey, "antij"], [bkey])
            if mask_ap is None:
                cp("dve", dstb[0:nk, 0:nq], bk[0:nk, 0:nq], [bkey], [dkey])
            else:
                tt("dve", dstb[0:nk, 0:nq], bk[0:nk, 0:nq], mask_ap[0:nk, 0:nq], ALU.add, [bkey, maskkey], [dkey])

        AR.push()
        Pbufs = [AR.bf16(1024) for _ in range(4)]
        tmpb = [AR.f32(1024) for _ in range(2)]
        lacc = AR.f32(1024)
        eA = AR.f32(1024)
        eB = AR.f32(1024)
        hkb = AR.f32(512)
        maskA = AR.f32(5 * 512)
        maskAv = maskA.rearrange("p (s q) -> p s q", s=5)
        dma("sp", maskAv, c_maskA.rearrange("s p q -> p s q"), (), ["maskA"], "c0")
        AR.push()
        biasA = [AR.f32(5 * 512) for _ in range(2)]
        KTw = [AR.bf16(64 * 128) for _ in range(2)]
        Vw = [AR.bf16(64 * 130) for _ in range(2)]
        for i in range(2):
            memset("pool", Vw[i].rearrange("p (t d) -> p t d", t=64)[:, :, 128:130], 1.0, [("Vw1", i)])

        def p3_dyn(E):
            if "pid" not in pid_holder:
                pid_holder["pid"] = E.partition_id()
            return pid_holder["pid"]

        u = 0
        for h in range(8):
            bA = biasA[h % 2]
            bAv = bA.rearrange("p (s q) -> p s q", s=5)
            for s in range(5):
                load_bias_tile(bAv[:, s, :], ("biasA", h % 2, s), hkb, "hkb", fa_s[h:h + 1, :], 512 - 128 * s, 512,
                               maskAv[:, s, :], "maskA")
            for qb in range(2):
                sl = u % 2
                ktw = KTw[sl]
                vw = Vw[sl].rearrange("p (t d) -> p t d", t=64)

                dma("pool", ktw, KTs[h, :, 512 * qb:512 * qb + 64 * 128], (), [("KTw", sl)], "ktw%d" % sl)
                for half in range(2):
                    dma("pool", vw[:, 32 * half:32 * half + 32, 0:128],
                        Vs[h, :, 4 * qb + 32 * half:4 * qb + 32 * half + 32, :], (), [("Vw", sl, half)],
                        "vw%d" % sl)
                tiles = []
                for s in range(64):
                    tiles.append(dict(
                        kT=(lambda m, s=s, ktw=ktw: ktw[64 * m:64 * m + 64, 128 * s:128 * (s + 1)]),
                        v=vw[:, s, :], nk=128,
                        bias=(bAv[:, s, :] if s < 5 else None), biaskey=("biasA", h % 2, s),
                        vis=visA[:, 64 * qb + s:64 * qb + s + 1],
                        keys=[("KTw", sl), ("Vw", sl, s // 32), ("Vw1", sl)]))
                r0 = 512 * qb
                ukey = ("qaT", h, r0)
                attention_unit_T(512, (lambda m, h=h, r0=r0: qaTv[64 * m:64 * m + 64, h, r0:r0 + 512]),
                                 tiles, Pbufs, tmpb, lacc, eA, eB, h, r0)
                u += 1
        P.barrier()
        AR.pop()

        AR.push()
        KTc = AR.bf16(8 * PAST)
        Vc = AR.bf16(16 * 8 * 130)
        cst = [AR.f32(1024) for _ in range(2)]
        bsA = [AR.f32(2 * 32) for _ in range(2)]
        KTcv = KTc.rearrange("p (h n) -> p h n", h=8)
        Vcv = Vc.rearrange("p (t h d) -> p t h d", t=16, h=8)
        memset("pool", Vcv[:, :, :, 128:130], 1.0, ["Vc1"])
        psTf = [b for b in banks]
        for t in range(16):
            sl = t % 2
            dma("sp", cst[sl], cka[128 * t:128 * (t + 1), :], (), [("cst", sl)], "cst%d" % sl)
            for hh in range(2):
                bk, bkey = next_bank()
                for j in range(4):
                    h = 4 * hh + j
                    tr(bk[:, 128 * j:128 * (j + 1)], cst[sl][:, 128 * h:128 * (h + 1)], ident_f, [("cst", sl), "ident_f"],
                       [bkey])
                evac(KTcv[:, 4 * hh:4 * hh + 4, 128 * t:128 * (t + 1)], bk.rearrange("p (h n) -> p h n", h=4),
                     [bkey], [("KTc", t)])
        for t in range(16):
            sl = t % 2
            dma("sp", cst[sl], cva[128 * t:128 * (t + 1), :], (), [("cst", sl)], "cst%d" % sl)
            cp("pool", Vcv[:, t, :, 0:128], cst[sl].rearrange("p (h d) -> p h d", h=8), [("cst", sl)], [("Vc", t)])
        for h in range(8):
            bs = bsA[h % 2].rearrange("p (s q) -> p s q", s=2)
            load_bias_tile(bs[:, 0, :], ("bsA", h % 2, 0), hkb, "hkb", fa_s[h:h + 1, :], 512, 32, None, None)
            load_bias_tile(bs[:, 1, :], ("bsA", h % 2, 1), hkb, "hkb", fa_s[h:h + 1, :], 384, 32, None, None, nk=32)
            tiles = []
            for t in range(16):
                tiles.append(dict(
                    kT=(lambda m, t=t, h=h: KTcv[64 * m:64 * m + 64, h, 128 * t:128 * (t + 1)]),
                    v=Vcv[:, t, h, :], nk=128, bias=(bs[:, 0, :] if t == 15 else None), biaskey=("bsA", h % 2, 0),
                    vis=visB[:, 15:16], keys=[("KTc", t), ("Vc", t), "Vc1"]))
            tiles.append(dict(
                kT=(lambda m, h=h: kaTsv[64 * m:64 * m + 64, h, 0:32]),
                v=vasv[0:32, h, :], nk=32, bias=bs[:, 1, :], biaskey=("bsA", h % 2, 1),
                vis=visB[0:32, 15:16], keys=[("kaTs", h), ("vas", (128 * h) // GC * GC), "vas1"]))
            attention_unit_T(32, (lambda m, h=h: qaTv[64 * m:64 * m + 64, h, 1024:1056]),
                             tiles, Pbufs, tmpb, lacc, eA, eB, h, 1024)
        P.barrier()
        AR.pop()
        AR.pop()
        AR.pop()

        AR.push()
        qbT = AR.bf16(8 * OWN)
        szb = AR.bf16(NT_OWN * 1024)
        kbT = AR.bf16(8 * 1664)
        vbb = AR.bf16(13 * 8 * 130)
        qbTv = qbT.rearrange("p (h t) -> p h t", h=8)
        szbv = szb.rearrange("p (t n) -> p t n", t=NT_OWN)
        kbTv = kbT.rearrange("p (h t) -> p h t", h=8)
        vbv = vbb.rearrange("p (t h d) -> p t h d", t=13, h=8)
        memset("pool", vbv[:, :, :, 128:130], 1.0, ["vb1"])

        AR.push()
        hTo, hTh = own_prep(AR, True)
        hTov = hTo.rearrange("p (c t) -> p c t", c=16)
        hThv = hTh.rearrange("p (c t) -> p c t", c=16)
        hokeys = [("hTo", t) for t in range(NT_OWN)]
        hhkeys = [("hTh", t) for t in range(4)]
        GC = 128
        wstf = [AR.f32(16 * GC) for _ in range(2)]
        wbfs = [AR.bf16(16 * GC) for _ in range(2)]
        ost = [AR.f32(GC) for _ in range(3)]
        sgt = [AR.f32(GC) for _ in range(2)]
        qscale = float(128 ** -0.5)
        for g in range(4096 // GC):
            col0 = 4096 + g * GC
            c1 = (g * GC) % 1024
            wbf, wkeys = wgroup(col0)
            if g * GC < 1024:
                feat_major(wbf, wkeys, GC, c1,
                           lambda c, t0, n: (qbTv[:, c // 128, t0:t0 + n], ("qbT", c // 128, t0)),
                           qscale, hTov, hokeys, tok_blocks)
            elif g * GC < 2048:
                feat_major(wbf, wkeys, GC, c1,
                           lambda c, t0, n: (kbTv[:, c // 128, t0:t0 + n], ("kbT", c // 128, t0)),
                           None, hThv, hhkeys, [(0, 512)])
                feat_major(wbf, wkeys, GC, c1,
                           lambda c, t0, n: (kbTv[:, c // 128, 512 + t0:512 + t0 + n], ("kbT", c // 128, 512 + t0)),
                           None, hTov, hokeys, tok_blocks)

                def sink(t, ps, bkey, c1=c1):
                    if t < 4:
                        return
                    o = ost[ost_i[0] % 3]
                    ok = ("ost", ost_i[0] % 3)
                    ost_i[0] += 1
                    evac(o[:, 0:GC], ps, [bkey], [ok])
                    if t < 8:
                        dma("sp", o_kbp[128 * (t - 4):128 * (t - 3), c1:c1 + GC], o[:, 0:GC], [ok],
                            [("okbp", t, c1)], "outst%d" % ok[1])
                    else:
                        dma("sp", o_kbs[480:512, c1:c1 + GC], o[0:32, 0:GC], [ok], [("okbs", c1)], "outst%d" % ok[1])
                tok_major(wbf, wkeys, GC, hTov, hokeys, NT_OWN, sink)
            elif g * GC < 3072:
                def sinkh(t, ps, bkey, c1=c1):
                    evac(vbv[:, t, c1 // 128:c1 // 128 + GC // 128, 0:128], ps.rearrange("p (h d) -> p h d", h=GC // 128),
                         [bkey], [("vb", t, c1)])
                tok_major(wbf, wkeys, GC, hThv, hhkeys, 4, sinkh)

                def sink(t, ps, bkey, c1=c1):
                    o = ost[ost_i[0] % 3]
                    ok = ("ost", ost_i[0] % 3)
                    ost_i[0] += 1
                    evac(o[:, 0:GC], ps, [bkey], [ok])
                    cp("pool", vbv[:, 4 + t, c1 // 128:c1 // 128 + GC // 128, 0:128],
                       o[:, 0:GC].rearrange("p (h d) -> p h d", h=GC // 128), [ok], [("vb", 4 + t, c1)])
                    if 4 <= t < 8:
                        dma("sp", o_vbp[128 * (t - 4):128 * (t - 3), c1:c1 + GC], o[:, 0:GC], [ok],
                            [("ovbp", t, c1)], "outst%d" % ok[1])
                    elif t == 8:
                        dma("sp", o_vbs[480:512, c1:c1 + GC], o[0:32, 0:GC], [ok], [("ovbs", c1)], "outst%d" % ok[1])
                tok_major(wbf, wkeys, GC, hTov, hokeys, NT_OWN, sink)
            else:
                def sink(t, ps, bkey, c1=c1):
                    silu_to(szbv[:, t, c1:c1 + GC], ps, bkey, ("szb", t, c1))
                tok_major(wbf, wkeys, GC, hTov, hokeys, NT_OWN, sink)
        dma("pool", o_kbs[0:480, :], ckb[32:512, :], (), [("okbs_c",)], "outc")
        dma("pool", o_vbs[0:480, :], cvb[32:512, :], (), [("ovbs_c",)], "outc")
        P.barrier()
        AR.pop()

        ozTb = AR.top_bf16(8 * OWN)
        ozTbv = ozTb.rearrange("p (h t) -> p h t", h=8)
        AR.push()
        Pbufs = [AR.bf16(1024) for _ in range(4)]
        tmpb = [AR.f32(1024) for _ in range(2)]
        osb = [AR.f32(324) for _ in range(2)]
        ep = [AR.f32(4) for _ in range(2)]
        hkb = AR.f32(512)
        maskB = AR.f32(5 * 128)
        maskBv = maskB.rearrange("p (s q) -> p s q", s=5)
        dma("sp", maskBv, c_maskB.rearrange("s p q -> p s q"), (), ["maskB"], "c0")
        biasB = [AR.f32(5 * 128) for _ in range(2)]
        bsB = [AR.f32(5 * 32) for _ in range(2)]
        KbTc = AR.bf16(8 * 512)
        Vbc = AR.bf16(4 * 8 * 130)
        cst = [AR.f32(1024) for _ in range(2)]
        KbTcv = KbTc.rearrange("p (h n) -> p h n", h=8)
        Vbcv = Vbc.rearrange("p (t h d) -> p t h d", t=4, h=8)
        memset("pool", Vbcv[:, :, :, 128:130], 1.0, ["Vbc1"])
        for t in range(4):
            sl = t % 2
            dma("sp", cst[sl], ckb[128 * t:128 * (t + 1), :], (), [("cst", sl)], "cst%d" % sl)
            for hh in range(2):
                bk, bkey = next_bank()
                for j in range(4):
                    h = 4 * hh + j
                    tr(bk[:, 128 * j:128 * (j + 1)], cst[sl][:, 128 * h:128 * (h + 1)], ident_f, [("cst", sl), "ident_f"],
                       [bkey])
                evac(KbTcv[:, 4 * hh:4 * hh + 4, 128 * t:128 * (t + 1)], bk.rearrange("p (h n) -> p h n", h=4),
                     [bkey], [("KbTc", t)])
        for t in range(4):
            sl = t % 2
            dma("sp", cst[sl], cvb[128 * t:128 * (t + 1), :], (), [("cst", sl)], "cst%d" % sl)
            cp("pool", Vbcv[:, t, :, 0:128], cst[sl].rearrange("p (h d) -> p h d", h=8), [("cst", sl)], [("Vbc", t)])

        def make_epilogue_B(h, row0, nq):
            def epilogue(nsb, Okeys, abase):
                for sb in range(nsb):
                    n = min(128, nq - 128 * sb)
                    slot = (abase // 3 + sb) % 2
                    k = ("ep", slot)
                    e = ep[slot]
                    o1 = osb[slot]
                    cp("act", o1[0:n, 0:130], acc_ap(abase + sb, n), [Okeys[(abase + sb) // 3]], [k])
                    P.op("dve", lambda E, e=e, o1=o1, n=n: E.reciprocal(e[0:n, 0:1], o1[0:n, 128:129]), [k], [k])
                    r = row0 + 128 * sb
                    tix, rr = r // 128, r % 128
                    ob = o1[:, 260:324].bitcast(BF16)
                    stt(ob[0:n, :], o1[0:n, 0:128], e[0:n, 0:1], szbv[rr:rr + n, tix, 128 * h:128 * (h + 1)],
                        ALU.mult, ALU.mult, [k, ("szb", tix, (128 * h) // GC * GC)], [k])
                    tr(psT[:, 0:n], ob[0:n, :], ident_b[0:n, 0:n], [k, "ident_b"], [KT_T])
                    cp("act", ozTbv[:, h, r:r + n], psT[:, 0:n], [KT_T], [("ozTb", h, r)])
            return epilogue

        def kb_keys(h, c0, n):
            ks = []
            for (a, b_) in ((0, 512), (512, 1024), (1024, 1536), (1536, 1664)):
                if c0 < b_ and c0 + n > a:
                    ks.append(("kbT", h, a))
            return ks

        for h in range(8):
            bB = biasB[h % 2].rearrange("p (s q) -> p s q", s=5)
            bS = bsB[h % 2].rearrange("p (s q) -> p s q", s=5)
            for s in range(5):
                load_bias_tile(bB[:, s, :], ("biasB", h % 2, s), hkb, "hkb", gb_s[h:h + 1, :], 512 - 128 * s, 128,
                               maskBv[:, s, :], "maskB")
            for s in range(4):
                load_bias_tile(bS[:, s, :], ("bsB", h % 2, s), hkb, "hkb", gb_s[h:h + 1, :], 512 - 128 * s, 32,
                               None, None)
            load_bias_tile(bS[:, 4, :], ("bsB", h % 2, 4), hkb, "hkb", gb_s[h:h + 1, :], 0, 32, None, None, nk=32)
            for p in range(8):
                tiles = []
                for s in range(5):
                    w = p + s
                    tiles.append(dict(
                        kT=(lambda m, w=w, h=h: kbTv[:, h, 128 * w:128 * (w + 1)]),
                        v=vbv[:, w, h, :], nk=128, bias=bB[:, s, :], biaskey=("biasB", h % 2, s),
                        vis=visB[:, w:w + 1],
                        keys=kb_keys(h, 128 * w, 128) + [("vb", w, (128 * h) // GC * GC), "vb1"]))
                r0 = 128 * p
                attention_unit(1, 128, (lambda m, h=h, r0=r0: qbTv[:, h, r0:r0 + 128]),
                               tiles, Pbufs, tmpb, make_epilogue_B(h, r0, 128), ("qbT", h, (r0 // 512) * 512))
            tiles = []
            for t in range(4):
                tiles.append(dict(
                    kT=(lambda m, t=t, h=h: KbTcv[:, h, 128 * t:128 * (t + 1)]),
                    v=Vbcv[:, t, h, :], nk=128, bias=bS[:, t, :], biaskey=("bsB", h % 2, t),
                    vis=visB[:, 15:16], keys=[("KbTc", t), ("Vbc", t), "Vbc1"]))
            tiles.append(dict(
                kT=(lambda m, h=h: kbTv[:, h, 512 + 1024:512 + 1056]),
                v=vbv[0:32, 12, h, :], nk=32, bias=bS[:, 4, :], biaskey=("bsB", h % 2, 4),
                vis=visB[0:32, 15:16], keys=[("kbT", h, 1536), ("vb", 12, (128 * h) // GC * GC), "vb1"]))
            attention_unit(1, 32, (lambda m, h=h: qbTv[:, h, 1024:1056]),
                           tiles, Pbufs, tmpb, make_epilogue_B(h, 1024, 32), ("qbT", h, 1024))
        P.barrier()
        AR.pop()
        AR.pop()

        AR.push()
        hTo, _ = own_prep(AR, False)
        hTov = hTo.rearrange("p (c t) -> p c t", c=16)
        hokeys = [("hTo", t) for t in range(NT_OWN)]
        dma("sp", gvec, post.partition_broadcast(128), (), ["gvec"], "c0")
        mT = AR.bf16(16 * OWN)
        mTv = mT.rearrange("p (c t) -> p c t", c=16)
        AR.push()
        wgst = [AR.f32(16 * 128) for _ in range(2)]
        wgbf = [AR.bf16(16 * 128) for _ in range(4)]
        wost = [AR.f32(8 * 128) for _ in range(2)]
        wobf = [AR.bf16(8 * 128) for _ in range(4)]
        sga = [AR.f32(512) for _ in range(2)]
        sgb = [AR.f32(512) for _ in range(2)]
        ya = [AR.f32(512) for _ in range(2)]
        k5 = [0]
        for cc in range(16):
            wk = {}
            for gi_, (c0, nm) in enumerate(((8192, "ga"), (10240, "gb"))):
                sl = (2 * cc + gi_) % 2
                sl4 = (2 * cc + gi_) % 4
                wk[nm] = (wgbf[sl4], load_wgroup(w_in[:, c0 + 128 * cc:c0 + 128 * (cc + 1)], 128, wgst[sl], wgbf[sl4],
                                                 ("wg5", sl4), "wg5%d" % sl4))
            for gi_, (wsrc, nm) in enumerate(((w_oa, "oa"), (w_ob, "ob"))):
                sl = (2 * cc + gi_) % 2
                sl4 = (2 * cc + gi_) % 4
                wk[nm] = (wobf[sl4], load_wgroup(wsrc[:, 128 * cc:128 * (cc + 1)], 128, wost[sl], wobf[sl4],
                                                 ("wo5", sl4), "wo5%d" % sl4, nk=8))
            for (t0, n) in tok_blocks:
                i2 = k5[0] % 2
                k5[0] += 1
                sig = {}
                for nm, sbuf_ in (("ga", sga[i2]), ("gb", sgb[i2])):
                    wbf, wkeys = wk[nm]
                    wv = wbf.rearrange("p (c n) -> p c n", c=16)
                    bk, bkey = next_bank()
                    for dc in range(16):
                        mm(bk[:, 0:n], wv[:, dc, :], hTov[:, dc, t0:t0 + n], dc == 0, dc == 15, wkeys + hokeys, [bkey])
                    sk = ("sg", nm, i2)
                    act(sbuf_[:, 0:n], bk[:, 0:n], AF.Sigmoid, [bkey], [sk])
                    sig[nm] = (sbuf_, sk)
                yk = ("ya", i2)
                for nm, ozv, oznm, gnm in (("oa", ozTav, "ozTa", "ga"), ("ob", ozTbv, "ozTb", "gb")):
                    wbf, wkeys = wk[nm]
                    wv = wbf.rearrange("p (c n) -> p c n", c=8)
                    bk, bkey = next_bank()
                    for h in range(8):
                        mm(bk[:, 0:n], wv[:, h, :], ozv[:, h, t0:t0 + n], h == 0, h == 7, wkeys, [bkey])
                    sb_, sk = sig[gnm]
                    if nm == "oa":
                        tt("dve", ya[i2][:, 0:n], bk[:, 0:n], sb_[:, 0:n], ALU.mult, [bkey, sk], [yk])
                    else:
                        tt("dve", sb_[:, 0:n], bk[:, 0:n], sb_[:, 0:n], ALU.mult, [bkey, sk], [sk])
                        tt("dve", mTv[:, cc, t0:t0 + n], sb_[:, 0:n], ya[i2][:, 0:n], ALU.add, [sk, yk], [("mT", cc, t0)])
        P.barrier()
        AR.pop()
        AR.n = ARENA_WORDS
        wout_bf_full = AR.bf16(16 * 2048)
        wost2 = [AR.f32(2048) for _ in range(2)]
        woutv = wout_bf_full.rearrange("p (c n) -> p c n", c=16)
        for dc in range(16):
            sl = dc % 2
            dma("pool", woutv[:, dc, :], w_out[128 * dc:128 * (dc + 1), :], (), [("wout", dc)], "wout")
        wokeys = [("wout", dc) for dc in range(16)]
        stage = hTo.bitcast(F32)
        yrow = [stage[:, 0:2048], stage[:, 2048:4096]]
        xrow = [stage[:, 4096:6144], stage[:, 6144:8192]]
        sq5 = stage[:, 8192:8200]
        junk5 = AR.f32(1024)
        for t in range(NT_OWN):
            i2 = t % 2
            nrow = min(128, 1056 - 128 * t)
            yk = ("yrow", i2)
            xk = ("xrow", i2)
            dma("sp", xrow[i2], xo[128 * t:128 * (t + 1), :], (), [xk], "xr%d" % i2)
            for cg in range(4):
                bk, bkey = next_bank()
                for kc in range(16):
                    mm(bk, mTv[:, kc, 128 * t:128 * (t + 1)], woutv[:, kc, 512 * cg:512 * (cg + 1)], kc == 0, kc == 15,
                       wokeys, [bkey])
                evac(yrow[i2][:, 512 * cg:512 * (cg + 1)], bk, [bkey], [yk])
            sq = sq5[:, 4 * i2:4 * i2 + 2]
            sk = ("sq5", i2)
            for hh in range(2):
                P.op("dve", lambda E, i2=i2, hh=hh: E.scalar_tensor_tensor(
                    junk5, yrow[i2][:, 1024 * hh:1024 * (hh + 1)], 1.0, yrow[i2][:, 1024 * hh:1024 * (hh + 1)],
                    ALU.mult, ALU.mult, accum_out=sq5[:, 4 * i2 + 2 + hh:4 * i2 + 3 + hh]), [yk], ["junk5", sk])
            tt("dve", sq[:, 0:1], sq5[:, 4 * i2 + 2:4 * i2 + 3], sq5[:, 4 * i2 + 3:4 * i2 + 4], ALU.add, [sk], [sk])
            act(sq[:, 0:1], sq[:, 0:1], AF.Ln, [sk], [sk], bias=EPS, scale=1.0 / D)
            act(sq[:, 1:2], sq[:, 0:1], AF.Exp, [sk], [sk], scale=-0.5)
            stt(yrow[i2], yrow[i2], sq[:, 1:2], gvec, ALU.mult, ALU.mult, [yk, sk, "gvec"], [yk])
            tt("pool", yrow[i2], yrow[i2], xrow[i2], ALU.add, [yk, xk], [yk])
            dma("sp", o_y[128 * t:128 * t + nrow, :], yrow[i2][0:nrow, :], [yk], [("oy", t)], "outy")
        P.barrier()
        AR.pop()

        P.barrier()
        sem_names = P.sem_names()
        sem_ctx = [nc.semaphore("s%d" % i) for i in range(len(sem_names))]
        sems = {}
        import contextlib
        with contextlib.ExitStack() as stack:
            for nm, c in zip(sem_names, sem_ctx):
                sems[nm] = stack.enter_context(c)
            block = stack.enter_context(nc.Block())

            def replay(E, eng):
                for waits, fn, tok in P.ops[eng]:
                    for s, v in waits:
                        E.wait_ge(sems[s], v)
                    if fn is None:
                        continue
                    ins = fn(E)
                    ins.then_inc(sems[tok[0]], 16 if tok[0].startswith("dma:") else 1)

            @block.tensor
            def _(E):
                replay(E, "pe")

            @block.scalar
            def _(E):
                replay(E, "act")

            @block.vector
            def _(E):
                replay(E, "dve")

            @block.gpsimd
            def _(E):
                replay(E, "pool")

            @block.sync
            def _(E):
                replay(E, "sp")
    return nc


def _constants():
    c = {}
    c["c_ident"] = np.eye(128, dtype=np.float32)
    c["c_antij"] = np.ascontiguousarray(np.eye(128, dtype=np.float32)[::-1])
    u = np.arange(FA_LEN)
    d = u - 511
    bk = _t5_bucket_np(-d)
    oh = np.zeros((32, FA_LEN), np.float32)
    oh[bk, u] = 1.0
    oh[15, :] -= 1.0
    c["c_oha"] = oh
    v = np.arange(GB_LEN)
    idx = np.clip(127 - v, -128, 128) + 128
    ohb = np.zeros((384, GB_LEN), np.float32)
    ohb[idx, v] = 1.0
    c["c_ohb"] = ohb
    i = np.arange(128)[:, None]
    j = np.arange(512)[None, :]
    mA = np.zeros((5, 128, 512), np.float32)
    for s in range(5):
        krel = (s - 1) * 128 + i
        mA[s] = np.where(np.floor_divide(krel, 64) > (j // 64), NEG, 0.0)
    c["c_maskA"] = mA
    j2 = np.arange(128)[None, :]
    mB = np.zeros((5, 128, 128), np.float32)
    for s in range(5):
        kc = (128 * s + i) // 64 - 8
        qc = j2 // 64
        ok = (kc <= qc) & (kc >= qc - 8)
        mB[s] = np.where(ok, 0.0, NEG)
    c["c_maskB"] = mB
    return c


def _vis_tables(core):
    visA = np.zeros((128, 128), np.float32)
    for qb in range(2):
        T0 = 8 * core + 4 * qb
        for s in range(64):
            tile = T0 - 1 + s
            if tile >= 64:
                visible = True
            else:
                visible = (0 <= tile <= T0 + 3)
            visA[:, 64 * qb + s] = 0.0 if visible else NEG
    visB = np.zeros((128, 16), np.float32)
    if core == 0:
        visB[:, 0:4] = NEG
    return visA, visB


_NC_CACHE = {}


def kernel(x_prompt, x_sample, cache_k_a, cache_v_a, cache_k_b, cache_v_b, t5_bias, pre_norm, post_norm,
           w_in, lambda_q1, lambda_k1, lambda_q2, lambda_k2, subln_a, rel_bias_b, w_o_a, w_o_b, w_out):
    f = lambda a: np.ascontiguousarray(np.asarray(a, dtype=np.float32))
    xp = f(x_prompt)[0]
    xs = f(x_sample)
    consts = _constants()
    relbT = np.zeros((384, 8), np.float32)
    relbT[:257] = f(rel_bias_b)[0].T
    lam4 = np.concatenate([f(lambda_q1)[0], f(lambda_k1)[0], f(lambda_q2)[0], f(lambda_k2)[0]])[None, :]
    shared = {
        "w_in": f(w_in)[0], "w_oa": f(w_o_a)[0], "w_ob": f(w_o_b)[0], "w_out": f(w_out)[0],
        "pre": f(pre_norm), "post": f(post_norm), "subln": f(subln_a), "lam4": np.ascontiguousarray(lam4),
        "t5": f(t5_bias), "relbT": relbT,
    }
    shared.update(consts)
    in_maps = []
    for c in range(NCORES):
        xo = np.zeros((OWN, D), np.float32)
        xo[:ROWS] = xp[ROWS * c:ROWS * (c + 1)]
        xo[ROWS:ROWS + NS] = xs[c]
        xh = np.zeros((512, D), np.float32)
        if c > 0:
            xh[:] = xp[ROWS * c - 512:ROWS * c]
        visA, visB = _vis_tables(c)
        m = dict(shared)
        m.update({
            "xf": np.ascontiguousarray(np.roll(xp, -128 * (8 * c - 1), axis=0)),
            "xo": xo, "xh": xh,
            "cka": f(cache_k_a)[0, c].reshape(PAST, 1024), "cva": f(cache_v_a)[0, c].reshape(PAST, 1024),
            "ckb": f(cache_k_b)[0, c].reshape(512, 1024), "cvb": f(cache_v_b)[0, c].reshape(512, 1024),
            "c_visA": visA, "c_visB": visB,
        })
        in_maps.append(m)
    if "nc" not in _NC_CACHE:
        _NC_CACHE["nc"] = build()
    nc = _NC_CACHE["nc"]
    res = run_bass_kernel_spmd(nc, in_maps, core_ids=list(range(NCORES)))
    R = res.results
    y_prompt = np.concatenate([R[c]["y"][:ROWS] for c in range(NCORES)], 0)[None]
    y_sample = np.stack([R[c]["y"][ROWS:ROWS + NS] for c in range(NCORES)], 0)
    kap = np.concatenate([R[c]["ka"][:ROWS] for c in range(NCORES)], 0).reshape(1, 1, SEQ, 16, 64)
    vap = np.concatenate([R[c]["va"][:ROWS] for c in range(NCORES)], 0).reshape(1, 1, SEQ, 8, 128)
    kbp = R[NCORES - 1]["kbp"].reshape(1, 1, 512, 8, 128)
    vbp = R[NCORES - 1]["vbp"].reshape(1, 1, 512, 8, 128)
    kas = np.stack([R[c]["ka"][ROWS:ROWS + NS] for c in range(NCORES)], 0).reshape(1, NCORES, NS, 16, 64)
    vas = np.stack([R[c]["va"][ROWS:ROWS + NS] for c in range(NCORES)], 0).reshape(1, NCORES, NS, 8, 128)
    kbs = np.stack([R[c]["kbs"] for c in range(NCORES)], 0).reshape(1, NCORES, 512, 8, 128)
    vbs = np.stack([R[c]["vbs"] for c in range(NCORES)], 0).reshape(1, NCORES, 512, 8, 128)
    out = (y_prompt, y_sample, kap, vap, kbp, vbp, kas, vas, kbs, vbs)
    return tuple(np.ascontiguousarray(o.astype(np.float32)) for o in out)
```

```python
import numpy as np
import ml_dtypes
import concourse.bass as bass
import concourse.mybir as mybir
from concourse.bass_utils import run_bass_kernel_spmd

F32 = mybir.dt.float32
BF16 = mybir.dt.bfloat16
AF = mybir.ActivationFunctionType
ALU = mybir.AluOpType

NCORES = 8
D = 2048
SEQ = 8192
ROWS = 1024
NS = 32
OWN = 1152
NT_OWN = 9
PAST = 2048
WIN = 12288
NEG = -30000.0
EPS = 1e-6
LAM_INIT = 0.2
FA_LEN = 1152
GB_LEN = 768
ENGS = ("pe", "act", "dve", "pool", "sp")


class Prog:
    def __init__(self):
        self.ops = {e: [] for e in ENGS}
        self.cnt = {e: 0 for e in ENGS}
        self.last_w = {}
        self.readers = {}
        self.waited = {e: {} for e in ENGS}
        self.dma_cnt = {}

    def _deps(self, eng, reads, writes):
        deps = {}
        def add(tok):
            if tok is None:
                return
            s, v = tok
            if s == eng and eng == "pe":
                return
            if s.startswith("dma:"):
                v = 16 * self.dma_cnt[s[4:]]
            if deps.get(s, 0) < v:
                deps[s] = v
        for k in reads:
            add(self.last_w.get(k))
        for k in writes:
            add(self.last_w.get(k))
            for t in self.readers.get(k, ()):
                add(t)
        waits = []
        for s, v in deps.items():
            if self.waited[eng].get(s, 0) < v:
                self.waited[eng][s] = v
                waits.append((s, v))
        return waits

    def op(self, eng, fn, reads=(), writes=(), tag=None):
        waits = self._deps(eng, reads, writes)
        if tag is not None:
            self.dma_cnt[tag] = self.dma_cnt.get(tag, 0) + 1
            tok = ("dma:" + tag, 16 * self.dma_cnt[tag])
        else:
            self.cnt[eng] += 1
            tok = (eng, self.cnt[eng])
        self.ops[eng].append((waits, fn, tok))
        for k in writes:
            self.last_w[k] = tok
            self.readers[k] = []
        for k in reads:
            self.readers.setdefault(k, []).append(tok)
        return tok

    def barrier(self):
        allt = [(e, self.cnt[e]) for e in ENGS if self.cnt[e] > 0]
        allt += [("dma:" + t, 16 * c) for t, c in self.dma_cnt.items()]
        for e in ENGS:
            waits = []
            for s, v in allt:
                if self.waited[e].get(s, 0) < v:
                    self.waited[e][s] = v
                    waits.append((s, v))
            if waits:
                self.ops[e].append((waits, None, None))
        self.last_w = {}
        self.readers = {}

    def sem_names(self):
        names = [e for e in ENGS if e != "sp"]
        names += ["dma:" + t for t in self.dma_cnt]
        return names


class Arena:
    def __init__(self, ap_f32, nwords):
        self.ap = ap_f32
        self.n = nwords
        self.off = 0
        self.marks = []

    def push(self):
        self.marks.append(self.off)

    def pop(self):
        self.off = self.marks.pop()

    def f32(self, n):
        assert self.off + n <= self.n, ("arena overflow", self.off, n, self.n)
        a = self.ap[:, self.off:self.off + n]
        self.off += n
        return a

    def bf16(self, n):
        w = (n + 1) // 2
        return self.f32(w).bitcast(BF16)

    def top_bf16(self, n):
        w = (n + 1) // 2
        assert self.n - w >= self.off
        self.n -= w
        return self.ap[:, self.n:self.n + w].bitcast(BF16)


def _t5_bucket_np(rel):
    rel = np.asarray(rel, np.int64)
    half = 16
    max_exact = 8
    ret = np.where(rel > 0, half, 0)
    n = np.abs(rel)
    nf = np.maximum(n, 1).astype(np.float32)
    large = max_exact + (np.log(nf / np.float32(max_exact)) / np.float32(np.log(128 / max_exact))
                         * np.float32(half - max_exact)).astype(np.int32)
    large = np.minimum(large, half - 1)
    return ret + np.where(n < max_exact, n, large)


def build():
    nc = bass.Bass("TRN2", target_bir_lowering=False)

    def din(name, shape, dt=F32):
        return nc.dram_tensor(name, list(shape), dt, kind="ExternalInput").ap()

    def dout(name, shape):
        return nc.dram_tensor(name, list(shape), F32, kind="ExternalOutput").ap()

    xf = din("xf", [SEQ, D])
    xo = din("xo", [OWN, D])
    xh = din("xh", [512, D])
    w_in = din("w_in", [D, WIN])
    w_oa = din("w_oa", [1024, D])
    w_ob = din("w_ob", [1024, D])
    w_out = din("w_out", [D, D])
    cka = din("cka", [PAST, 1024])
    cva = din("cva", [PAST, 1024])
    ckb = din("ckb", [512, 1024])
    cvb = din("cvb", [512, 1024])
    pre = din("pre", [1, D])
    post = din("post", [1, D])
    subln = din("subln", [1, 128])
    lam4 = din("lam4", [1, 256])
    t5 = din("t5", [32, 8])
    relbT = din("relbT", [384, 8])
    c_ident = din("c_ident", [128, 128])
    c_antij = din("c_antij", [128, 128])
    c_oha = din("c_oha", [32, FA_LEN])
    c_ohb = din("c_ohb", [384, GB_LEN])
    c_maskA = din("c_maskA", [5, 128, 512])
    c_maskB = din("c_maskB", [5, 128, 128])
    c_visA = din("c_visA", [128, 128])
    c_visB = din("c_visB", [128, 16])

    o_y = dout("y", [1056, D])
    o_ka = dout("ka", [1056, 1024])
    o_va = dout("va", [1056, 1024])
    o_kbp = dout("kbp", [512, 1024])
    o_vbp = dout("vbp", [512, 1024])
    o_kbs = dout("kbs", [512, 1024])
    o_vbs = dout("vbs", [512, 1024])

    KTs = nc.dram_tensor("KTs", [8, 128, 2 * SEQ], BF16).ap()
    Vs = nc.dram_tensor("Vs", [8, 128, 128, 128], BF16).ap()
    fa_s = nc.dram_tensor("fa_s", [8, FA_LEN], F32).ap()
    gb_s = nc.dram_tensor("gb_s", [8, GB_LEN], F32).ap()

    P = Prog()
    pid_holder = {}

    ARENA_WORDS = 53000
    with (
        nc.sbuf_tensor("arena", [128, ARENA_WORDS], F32) as arena_t,
        nc.psum_tensor("psA", [128, 1024], F32) as psA,
        nc.psum_tensor("psB", [128, 1024], F32) as psB,
        nc.psum_tensor("psO", [128, 1536], F32) as psO,
        nc.psum_tensor("psT", [128, 1024], BF16) as psT,
    ):
        AR = Arena(arena_t[:, :], ARENA_WORDS)
        banks = []
        for nm, t, nb in (("A", psA, 2), ("B", psB, 2), ("O", psO, 3)):
            for i in range(nb):
                banks.append((t[:, 512 * i:512 * (i + 1)], ("ps", nm, i)))
        bank_rr = [0]

        def next_bank():
            b = banks[bank_rr[0] % len(banks)]
            bank_rr[0] += 1
            return b
        KT_T = ("ps", "T", 0)

        def dma(q, out, in_, reads, writes, tag):
            P.op(q, lambda E: E.dma_start(out=out, in_=in_), reads, writes, tag=tag)

        def mm(out, lhsT, rhs, start, stop, reads, writes):
            P.op("pe", lambda E: E.matmul(out, lhsT, rhs, start=start, stop=stop,
                                          skip_group_check=True), reads, writes)

        def tr(out, in_, ident, reads, writes):
            P.op("pe", lambda E: E.transpose(out, in_, ident), reads, writes)

        def act(out, in_, func, reads, writes, bias=None, scale=None, accum=None):
            kw = {}
            if bias is not None:
                kw["bias"] = bias
            if scale is not None:
                kw["scale"] = scale
            if accum is not None:
                kw["accum_out"] = accum
            P.op("act", lambda E: E.activation(out, in_, func, **kw), reads, writes)

        def ts(eng, out, in0, s1, s2, op0, op1, reads, writes, accum=None):
            if accum is None:
                P.op(eng, lambda E: E.tensor_scalar(out, in0, s1, s2, op0, op1), reads, writes)
            else:
                P.op(eng, lambda E: E.tensor_scalar(out, in0, s1, s2, op0, op1, accum_out=accum),
                     reads, writes)

        def tt(eng, out, in0, in1, op, reads, writes):
            P.op(eng, lambda E: E.tensor_tensor(out, in0, in1, op), reads, writes)

        def stt(out, in0, scalar, in1, op0, op1, reads, writes):
            P.op("dve", lambda E: E.scalar_tensor_tensor(out, in0, scalar, in1, op0, op1), reads, writes)

        def cp(eng, out, in_, reads, writes):
            if eng == "act":
                P.op("act", lambda E: E.copy(out, in_), reads, writes)
            else:
                P.op(eng, lambda E: E.tensor_copy(out, in_), reads, writes)

        def memset(eng, ap, val, writes):
            P.op(eng, lambda E: E.memset(ap, val), (), writes)

        evac_rr = [0]

        def evac(out, in_, reads, writes):
            e = ("act", "dve")[evac_rr[0] % 2]
            evac_rr[0] += 1
            cp(e, out, in_, reads, writes)

        ident_f = AR.f32(128)
        antij = AR.f32(128)
        ident_b = AR.bf16(128)
        gvec = AR.f32(2048)
        sublnb = AR.f32(128)
        lamt = AR.f32(256)
        lamtmp = AR.f32(64)
        lamc = AR.f32(8)
        visA = AR.f32(128)
        visB = AR.f32(16)
        sublnc = AR.f32(2)
        ones_f = AR.f32(128)
        onesm_f = AR.f32(128)
        ones_b = AR.bf16(128)

        dma("sp", ident_f, c_ident, (), ["ident_f"], "c0")
        dma("sp", antij, c_antij, (), ["antij"], "c0")
        dma("sp", visA, c_visA, (), ["visA"], "c0")
        dma("sp", visB, c_visB, (), ["visB"], "c0")
        dma("sp", gvec, pre.partition_broadcast(128), (), ["gvec"], "c0")
        dma("sp", sublnb, subln.partition_broadcast(128), (), ["sublnb"], "c0")
        dma("sp", lamt, lam4.partition_broadcast(128), (), ["lamt"], "c0")
        dma("sp", sublnc[:, 0:1], subln.rearrange("o d -> d o"), (), ["sublnc"], "c0")
        AR.push()
        t5_sb = AR.f32(8)
        oha_sb = AR.f32(FA_LEN)
        rb_sb = AR.f32(24)
        ohb_sb = AR.f32(3 * GB_LEN)
        vec_sb = AR.f32(FA_LEN)
        dma("sp", t5_sb[0:32, :], t5, (), ["t5_sb"], "c0")
        dma("sp", oha_sb[0:32, :], c_oha, (), ["oha_sb"], "c0")
        dma("sp", rb_sb.rearrange("p (c h) -> p c h", c=3), relbT.rearrange("(c p) h -> p c h", p=128),
            (), ["rb_sb"], "c0")
        dma("sp", ohb_sb.rearrange("p (c n) -> p c n", c=3), c_ohb.rearrange("(c p) n -> p c n", p=128),
            (), ["ohb_sb"], "c0")
        P.barrier()
        cp("dve", ident_b, ident_f, ["ident_f"], ["ident_b"])
        ts("dve", sublnb, sublnb, 1.0 - LAM_INIT, None, ALU.mult, ALU.bypass, ["sublnb"], ["sublnb"])
        ts("dve", sublnc[:, 0:1], sublnc[:, 0:1], 1.0 - LAM_INIT, None, ALU.mult, ALU.bypass, ["sublnc"], ["sublnc"])
        memset("dve", ones_f, 1.0, ["ones_f"])
        memset("dve", onesm_f, 1.0 / 128, ["onesm_f"])
        memset("dve", ones_b, 1.0, ["ones_b"])
        for i in range(2):
            P.op("dve", (lambda i: lambda E: E.scalar_tensor_tensor(
                lamtmp, lamt[:, 128 * i:128 * i + 64], 1.0, lamt[:, 128 * i + 64:128 * i + 128],
                ALU.mult, ALU.mult, accum_out=lamc[:, i:i + 1]))(i),
                ["lamt"], ["lamtmp", "lamc"])
        act(lamc[:, 2:4], lamc[:, 0:2], AF.Exp, ["lamc"], ["lamc"])
        tt("dve", lamc[:, 4:5], lamc[:, 2:3], lamc[:, 3:4], ALU.subtract, ["lamc"], ["lamc"])
        ts("dve", lamc[:, 5:6], lamc[:, 4:5], LAM_INIT, -1.0, ALU.add, ALU.mult, ["lamc"], ["lamc"])
        neg_lam = lamc[:, 5:6]

        for j in range(3):
            bk, bkey = next_bank()
            mm(bk[0:8, 0:384], t5_sb[0:32, 0:8], oha_sb[0:32, 384 * j:384 * (j + 1)], True, True,
               ["t5_sb", "oha_sb"], [bkey])
            cp("dve", vec_sb[0:8, 384 * j:384 * (j + 1)], bk[0:8, 0:384], [bkey], ["vec_sb"])
        dma("sp", fa_s, vec_sb[0:8, 0:FA_LEN], ["vec_sb"], ["fa_s"], "c1")
        for j in range(2):
            bk, bkey = next_bank()
            for c in range(3):
                mm(bk[0:8, 0:384], rb_sb[:, 8 * c:8 * c + 8], ohb_sb[:, GB_LEN * c + 384 * j:GB_LEN * c + 384 * (j + 1)],
                   c == 0, c == 2, ["rb_sb", "ohb_sb"], [bkey])
            cp("dve", vec_sb[0:8, 384 * j:384 * (j + 1)], bk[0:8, 0:384], [bkey, "fa_s"], ["vec_sb"])
        dma("sp", gb_s, vec_sb[0:8, 0:GB_LEN], ["vec_sb"], ["gb_s"], "c1")
        P.barrier()
        AR.pop()

        def norm_jobs(src, ntiles, hT, hT_key, tagbase, xs, hb, ssq, junk):
            hTv = hT.rearrange("p (c t) -> p c t", c=16)
            jobs = []

            def mk(t, part):
                def job():
                    sl = t % len(xs)
                    xk = ("xs", tagbase, sl)
                    hk = ("hb", tagbase, t % 2)
                    sk = ("ssq", tagbase, t % 2)
                    sq = ssq[:, 2 * (t % 2):2 * (t % 2) + 2]
                    if part == "norm":
                        dma("sp", xs[sl], src[128 * t:128 * (t + 1), :], (), [xk], "x%s%d" % (tagbase, sl))
                        P.op("dve", lambda E: E.scalar_tensor_tensor(
                            junk, xs[sl], 1.0, xs[sl], ALU.mult, ALU.mult, accum_out=sq[:, 0:1]),
                            [xk], [("junk", tagbase), sk])
                        act(sq[:, 0:1], sq[:, 0:1], AF.Ln, [sk], [sk], bias=EPS, scale=1.0 / D)
                        act(sq[:, 1:2], sq[:, 0:1], AF.Exp, [sk], [sk], scale=-0.5)
                        stt(hb[t % 2], xs[sl], sq[:, 1:2], gvec, ALU.mult, ALU.mult, [xk, sk, "gvec"], [hk])
                        return
                    half = part
                    for j in range(8):
                        dc = 8 * half + j
                        tr(psT[:, 128 * j:128 * (j + 1)], hb[t % 2][:, 128 * dc:128 * (dc + 1)], ident_b,
                           [hk, "ident_b"], [KT_T])
                    evac(hTv[:, 8 * half:8 * half + 8, 128 * t:128 * (t + 1)],
                         psT.rearrange("p (c t) -> p c t", c=8), [KT_T], [(hT_key, t)])
                return job
            for t in range(ntiles):
                if t == 0:
                    jobs.append(mk(0, "norm"))
                if t + 1 < ntiles:
                    jobs.append(mk(t + 1, "norm"))
                jobs.append(mk(t, 0))
                jobs.append(mk(t, 1))
            return jobs

        def norm_tiles(src, ntiles, hT, hT_key, tagbase, xs, hb, ssq, junk):
            for j in norm_jobs(src, ntiles, hT, hT_key, tagbase, xs, hb, ssq, junk):
                j()

        AR.push()
        Wkv = AR.bf16(16 * 2048)
        wst = [AR.f32(2048) for _ in range(2)]
        xs1 = [AR.f32(2048) for _ in range(3)]
        hb1 = [AR.bf16(2048) for _ in range(2)]
        ssq1 = AR.f32(4)
        junk1 = AR.bf16(2048)
        hTb = [AR.bf16(16 * 512) for _ in range(2)]
        ktb = [AR.bf16(8 * 512) for _ in range(2)]
        vbk = [AR.bf16(4 * 1024) for _ in range(2)]
        Wkvv = Wkv.rearrange("p (c n) -> p c n", c=16)
        for dc in range(16):
            sl = dc % 2
            dma("pool", Wkvv[:, dc, :], w_in[128 * dc:128 * (dc + 1), 1024:3072], (), [("Wkv", dc)], "wkv")
        Wkeys = [("Wkv", dc) for dc in range(16)]
        KTsv = KTs.rearrange("h p n -> p h n")
        def p1_jobs(b):
            return norm_jobs(xf[512 * b:512 * (b + 1), :], 4, hTb[b % 2], ("hTb", b % 2), "p1", xs1, hb1, ssq1, junk1)
        for j in p1_jobs(0):
            j()
        for b in range(16):
            hT = hTb[b % 2]
            hTkey = ("hTb", b % 2)
            pending = p1_jobs(b + 1) if b + 1 < 16 else []
            hTv = hT.rearrange("p (c t) -> p c t", c=16)
            hkeys = [(hTkey, t) for t in range(4)]
            kt = ktb[b % 2].rearrange("p (h t) -> p h t", h=8)
            ktk = ("ktb", b % 2)
            vb_ = vbk[b % 2].rearrange("p (t n) -> p t n", t=4)
            vk = ("vbk", b % 2)
            gcount = [0]

            def after_group():
                gcount[0] += 1
                if pending:
                    pending.pop(0)()
            for h in range(8):
                bk, bkey = next_bank()
                for dc in range(16):
                    mm(bk, Wkvv[:, dc, 128 * h:128 * (h + 1)], hTv[:, dc, :], dc == 0, dc == 15,
                       hkeys + [Wkeys[dc]], [bkey])
                evac(kt[:, h, :], bk, [bkey], [(ktk, h)])
                after_group()
            for rep in range(2):
                dma("pool", KTsv[:, :, rep * SEQ + 512 * b:rep * SEQ + 512 * (b + 1)], kt,
                    [(ktk, h) for h in range(8)], [("KTs", b, rep)], "kts%d" % (b % 2))
            for t in range(4):
                for half in range(2):
                    bk, bkey = next_bank()
                    for dc in range(16):
                        mm(bk, hTv[:, dc, 128 * t:128 * (t + 1)], Wkvv[:, dc, 1024 + 512 * half:1024 + 512 * (half + 1)],
                           dc == 0, dc == 15, hkeys + [Wkeys[dc]], [bkey])
                    evac(vb_[:, t, 512 * half:512 * (half + 1)], bk, [bkey], [(vk, t, half)])
                    after_group()
                for rep in range(2):
                    dma("pool", Vs[:, :, 64 * rep + 4 * b + t, :].rearrange("h p d -> p h d"),
                        vb_[:, t, :].rearrange("p (h d) -> p h d", h=8),
                        [(vk, t, 0), (vk, t, 1)], [("Vs", b, t, rep)], "vs%d" % (b % 2))
            while pending:
                pending.pop(0)()
        P.barrier()
        AR.pop()

        ozTa = AR.bf16(8 * OWN)
        ozTav = ozTa.rearrange("p (h t) -> p h t", h=8)

        def load_wgroup(wsrc_cols, ncol, wst_f, wbf, key, tag, nk=16, stkey=None):
            half = nk // 2
            for hh in range(2):
                dma("pool", wbf[:, hh * half * ncol:(hh + 1) * half * ncol].rearrange("p (c n) -> p c n", c=half),
                    wsrc_cols[128 * half * hh:128 * half * (hh + 1), :].rearrange("(c p) n -> p c n", p=128),
                    (), [(key, "bf", hh)], tag)
            return [(key, "bf", 0), (key, "bf", 1)]

        def own_prep(AR, with_halo):
            hTo = AR.bf16(16 * OWN)
            hTh = AR.bf16(16 * 512) if with_halo else None
            AR.push()
            xs = [AR.f32(2048) for _ in range(2)]
            hb = [AR.bf16(2048) for _ in range(2)]
            ssq = AR.f32(4)
            junk = AR.bf16(2048)
            norm_tiles(xo, NT_OWN, hTo, "hTo", "own", xs, hb, ssq, junk)
            if with_halo:
                norm_tiles(xh, 4, hTh, "hTh", "own", xs, hb, ssq, junk)
            P.barrier()
            AR.pop()
            return hTo, hTh

        tok_blocks = [(0, 512), (512, 512), (1024, 128)]

        AR.push()
        qaT = AR.bf16(8 * OWN)
        szga = AR.bf16(8 * OWN)
        kaTs = AR.bf16(8 * 32)
        vas = AR.bf16(8 * 130)
        qaTv = qaT.rearrange("p (h t) -> p h t", h=8)
        szgaTv = szga.rearrange("p (h t) -> p h t", h=8)
        kaTsv = kaTs.rearrange("p (h t) -> p h t", h=8)
        vasv = vas.rearrange("p (h d) -> p h d", h=8)
        memset("pool", vasv[:, :, 128:130], 1.0, ["vas1"])

        AR.push()
        hTo, _ = own_prep(AR, False)
        hTov = hTo.rearrange("p (c t) -> p c t", c=16)
        hokeys = [("hTo", t) for t in range(NT_OWN)]
        GC = 256
        wstf = [AR.f32(16 * GC) for _ in range(2)]
        wbfs = [AR.bf16(16 * GC) for _ in range(2)]
        ost = [AR.f32(GC) for _ in range(3)]
        sgt = [AR.f32(GC) for _ in range(2)]
        ost_i = [0]

        def feat_major(wbf, wkeys, ncol, col0, dst_fn, scale, tokv, tokkeys, blocks, evac_fn=None):
            wv = wbf.rearrange("p (c n) -> p c n", c=16)
            for cc in range(ncol // 128):
                for (t0, n) in blocks:
                    bk, bkey = next_bank()
                    for dc in range(16):
                        mm(bk[:, 0:n], wv[:, dc, 128 * cc:128 * (cc + 1)], tokv[:, dc, t0:t0 + n], dc == 0, dc == 15,
                           wkeys + tokkeys, [bkey])
                    dst, dkey = dst_fn(col0 + 128 * cc, t0, n)
                    if evac_fn is not None:
                        evac_fn(dst, dkey, bk[:, 0:n], bkey, n)
                    elif scale is None:
                        evac(dst, bk[:, 0:n], [bkey], [dkey])
                    else:
                        ts("dve", dst, bk[:, 0:n], scale, None, ALU.mult, ALU.bypass, [bkey], [dkey])

        def tok_major(wbf, wkeys, ncol, tokv, tokkeys, ntiles, sink):
            wv = wbf.rearrange("p (c n) -> p c n", c=16)
            for t in range(ntiles):
                bk, bkey = next_bank()
                for dc in range(16):
                    mm(bk[:, 0:ncol], tokv[:, dc, 128 * t:128 * (t + 1)], wv[:, dc, :], dc == 0, dc == 15,
                       wkeys + tokkeys, [bkey])
                sink(t, bk[:, 0:ncol], bkey)

        def out_rows(dst, col0, ncol, src_ap, skey, t, nrows_total):
            r0 = 128 * t
            n = min(128, nrows_total - r0)
            if n <= 0:
                return
            dma("sp", dst[r0:r0 + n, col0:col0 + ncol], src_ap[0:n, :], [skey], [("out", id(dst), t, col0)], "outst%d" % skey[1])

        def silu_to(dst, psum, bkey, dkey, mulb=None, mkey=None):
            if mulb is None:
                act(dst, psum, AF.Silu, [bkey], [dkey])
                return
            s_ = sgt[ost_i[0] % 2]
            sk = ("sgt", ost_i[0] % 2)
            ost_i[0] += 1
            n = psum.shape[-1]
            act(s_[:, 0:n], psum, AF.Silu, [bkey], [sk])
            tt("dve", dst, s_[:, 0:n], mulb, ALU.mult, [sk, mkey], [dkey])

        gi = [0]

        def wgroup(col0):
            sl = gi[0] % 2
            gi[0] += 1
            keys = load_wgroup(w_in[:, col0:col0 + GC], GC, wstf[sl], wbfs[sl], ("wg", sl), "wg%d" % sl)
            return wbfs[sl], keys

        sgt2 = [AR.f32(512) for _ in range(2)]

        for g in range(4096 // GC):
            col0 = g * GC
            wbf, wkeys = wgroup(col0)
            if col0 < 1024:
                feat_major(wbf, wkeys, GC, col0,
                           lambda c, t0, n: (qaTv[:, c // 128, t0:t0 + n], ("qaT", c // 128, t0)),
                           0.125, hTov, hokeys, tok_blocks)
            elif col0 < 2048:
                c1 = col0 - 1024

                def sink(t, ps, bkey, c1=c1):
                    o = ost[ost_i[0] % 3]
                    ok = ("ost", ost_i[0] % 3)
                    ost_i[0] += 1
                    evac(o[:, 0:GC], ps, [bkey], [ok])
                    out_rows(o_ka, c1, GC, o, ok, t, 1056)
                tok_major(wbf, wkeys, GC, hTov, hokeys, NT_OWN, sink)
                feat_major(wbf, wkeys, GC, c1,
                           lambda c, t0, n: (kaTsv[:, c // 128, 0:32], ("kaTs", c // 128)),
                           None, hTov, hokeys, [(1024, 32)])
            elif col0 < 3072:
                c1 = col0 - 2048

                def sink(t, ps, bkey, c1=c1):
                    o = ost[ost_i[0] % 3]
                    ok = ("ost", ost_i[0] % 3)
                    ost_i[0] += 1
                    evac(o[:, 0:GC], ps, [bkey], [ok])
                    out_rows(o_va, c1, GC, o, ok, t, 1056)
                    if t == 8:
                        cp("pool", vasv[0:32, c1 // 128:c1 // 128 + 2, 0:128],
                           o[0:32, 0:GC].rearrange("p (h d) -> p h d", h=2), [ok], [("vas", c1)])
                tok_major(wbf, wkeys, GC, hTov, hokeys, NT_OWN, sink)
            else:
                c1 = col0 - 3072

                def zevac(dst, dkey, ps, bkey, n):
                    s_ = sgt2[ost_i[0] % 2]
                    sk = ("sgt2", ost_i[0] % 2)
                    ost_i[0] += 1
                    act(s_[:, 0:n], ps, AF.Silu, [bkey], [sk])
                    ts("dve", dst, s_[:, 0:n], sublnc[:, 0:1], None, ALU.mult, ALU.bypass, [sk, "sublnc"], [dkey])
                feat_major(wbf, wkeys, GC, c1,
                           lambda c, t0, n: (szgaTv[:, c // 128, t0:t0 + n], ("szgaT", c // 128, t0)),
                           None, hTov, hokeys, tok_blocks, evac_fn=zevac)
        P.barrier()
        AR.pop()

        def attention_unit(nmaps, nq, qT_fn, tiles, Pbufs, tmpb, epilogue, ukey, abase=0):
            nsb = (nq + 127) // 128
            T = len(tiles)
            Sps = [(psA, [("ps", "A", 0), ("ps", "A", 1)]), (psB, [("ps", "B", 0), ("ps", "B", 1)])]
            Okeys = [("ps", "O", 0), ("ps", "O", 1), ("ps", "O", 2)]

            def qk(t):
                tl = tiles[t]
                sp, skeys = Sps[t % 2]
                for m in range(nmaps):
                    mm(sp[0:tl["nk"], 512 * m:512 * m + nq], tl["kT"](m), qT_fn(m), True, True,
                       tl["keys"], [skeys[m]])

            def softmax_exp(t):
                tl = tiles[t]
                nk = tl["nk"]
                sp, skeys = Sps[t % 2]
                pb = Pbufs[t % len(Pbufs)]
                pk = ("Pb", t % len(Pbufs))
                pbv = pb.rearrange("p (m q) -> p m q", m=2)
                spv = sp.rearrange("p (m q) -> p m q", m=2)
                if tl["bias"] is not None:
                    tb = tmpb[t % len(tmpb)]
                    tk = ("tmpb", t % len(tmpb))
                    tbv = tb.rearrange("p (m q) -> p m q", m=2)
                    for m in range(nmaps):
                        tt("dve", tbv[0:nk, m, 0:nq], spv[0:nk, m, 0:nq], tl["bias"][0:nk, 0:nq], ALU.add,
                           [skeys[m], tl["biaskey"]], [tk])
                    act(pbv[0:nk, 0:nmaps, 0:nq], tbv[0:nk, 0:nmaps, 0:nq], AF.Exp, [tk, "visA", "visB"], [pk],
                        bias=tl["vis"])
                else:
                    act(pbv[0:nk, 0:nmaps, 0:nq], spv[0:nk, 0:nmaps, 0:nq], AF.Exp,
                        skeys[0:nmaps] + ["visA", "visB"], [pk], bias=tl["vis"])

            def pv(t):
                tl = tiles[t]
                nk = tl["nk"]
                pb = Pbufs[t % len(Pbufs)]
                pk = ("Pb", t % len(Pbufs))
                pbv = pb.rearrange("p (m q) -> p m q", m=2)
                for m in range(nmaps):
                    for sb in range(nsb):
                        a = abase + m * nsb + sb
                        nqs = min(128, nq - 128 * sb)
                        first_in_bank = (t == 0 and a % 3 == 0)
                        mm(psO[0:nqs, 512 * (a // 3) + 130 * (a % 3):512 * (a // 3) + 130 * (a % 3) + 130],
                           pbv[0:nk, m, 128 * sb:128 * sb + nqs], tl["v"], first_in_bank, t == T - 1,
                           [pk] + tl["keys"], [Okeys[a // 3]])

            qk(0)
            for t in range(T):
                if t + 1 < T:
                    qk(t + 1)
                softmax_exp(t)
                if t >= 1:
                    pv(t - 1)
            pv(T - 1)
            epilogue(nsb, Okeys, abase)

        def attention_unit_T(nq, qT_fn, tiles, Pbufs, tmpb, lacc, eA, eB, h, row0):
            T = len(tiles)
            Sps = [(psA, [("ps", "A", 0), ("ps", "A", 1)]), (psB, [("ps", "B", 0), ("ps", "B", 1)])]
            Okeys = [("ps", "O", 0), ("ps", "O", 1)]
            laccv = lacc.rearrange("p (m q) -> p m q", m=2)
            lk = "lacc"
            Lps = [psO[:, 1024:1536], psT.bitcast(F32)]
            Lkeys = [("ps", "O", 2), KT_T]
            dve_started = [False]

            def qk(t):
                tl = tiles[t]
                sp, skeys = Sps[t % 2]
                for m in range(2):
                    mm(sp[0:tl["nk"], 512 * m:512 * m + nq], tl["kT"](m), qT_fn(m), True, True,
                       tl["keys"], [skeys[m]])

            def softmax_exp(t):
                tl = tiles[t]
                nk = tl["nk"]
                sp, skeys = Sps[t % 2]
                pb = Pbufs[t % len(Pbufs)]
                pk = ("Pb", t % len(Pbufs))
                pbv = pb.rearrange("p (m q) -> p m q", m=2)
                spv = sp.rearrange("p (m q) -> p m q", m=2)
                if tl["bias"] is not None:
                    tb = tmpb[t % len(tmpb)]
                    tk = ("tmpb", t % len(tmpb))
                    tbv = tb.rearrange("p (m q) -> p m q", m=2)
                    for m in range(2):
                        tt("dve", tbv[0:nk, m, 0:nq], spv[0:nk, m, 0:nq], tl["bias"][0:nk, 0:nq], ALU.add,
                           [skeys[m], tl["biaskey"]], [tk])
                    act(pbv[0:nk, :, 0:nq], tbv[0:nk, :, 0:nq], AF.Exp, [tk, "visA", "visB"], [pk], bias=tl["vis"])
                else:
                    act(pbv[0:nk, :, 0:nq], spv[0:nk, :, 0:nq], AF.Exp, skeys + ["visA", "visB"], [pk], bias=tl["vis"])

            def pv(t):
                tl = tiles[t]
                nk = tl["nk"]
                pb = Pbufs[t % len(Pbufs)]
                pk = ("Pb", t % len(Pbufs))
                pbv = pb.rearrange("p (m q) -> p m q", m=2)
                on_pe = (t % 3 == 0)
                if on_pe:
                    for m in range(2):
                        mm(Lps[m][:, 0:nq], ones_b[0:nk, :], pbv[0:nk, m, 0:nq], t == 0, False,
                           [pk, "ones_b"], [Lkeys[m]])
                elif not dve_started[0]:
                    dve_started[0] = True
                    cp("dve", laccv[0:nk, :, 0:nq], pbv[0:nk, :, 0:nq], [pk], [lk])
                else:
                    tt("dve", laccv[0:nk, :, 0:nq], laccv[0:nk, :, 0:nq], pbv[0:nk, :, 0:nq], ALU.add, [pk, lk], [lk])
                for m in range(2):
                    mm(psO[:, 512 * m:512 * m + nq], tl["v"][0:nk, 0:128], pbv[0:nk, m, 0:nq], t == 0, t == T - 1,
                       [pk] + tl["keys"], [Okeys[m]])

            qk(0)
            for t in range(T):
                if t + 1 < T:
                    qk(t + 1)
                softmax_exp(t)
                if t >= 1:
                    pv(t - 1)
            pv(T - 1)
            sp, skeys = Sps[T % 2]
            spv = sp.rearrange("p (m q) -> p m q", m=2)
            eAv = eA.rearrange("p (m q) -> p m q", m=2)
            eBv = eB.rearrange("p (m q) -> p m q", m=2)
            for m in range(2):
                mm(Lps[m][:, 0:nq], ones_f, laccv[:, m, 0:nq], False, True, [lk, "ones_f"], [Lkeys[m]])
            for m in range(2):
                act(eAv[:, m, 0:nq], Lps[m][:, 0:nq], AF.Ln, [Lkeys[m]], ["eA"])
            act(eAv[:, :, 0:nq], eAv[:, :, 0:nq], AF.Exp, ["eA"], ["eA"], scale=-1.0)
            tt("dve", eBv[:, 1, 0:nq], psO[:, 512:512 + nq], eAv[:, 1, 0:nq], ALU.mult, [Okeys[1], "eA"], ["eB"])
            tt("dve", eBv[:, 0, 0:nq], psO[:, 0:nq], eAv[:, 0, 0:nq], ALU.mult, [Okeys[0], "eA"], ["eB"])
            stt(eBv[:, 0, 0:nq], eBv[:, 1, 0:nq], neg_lam, eBv[:, 0, 0:nq], ALU.mult, ALU.add, ["eB", "lamc"], ["eB"])
            tt("dve", eBv[:, 1, 0:nq], eBv[:, 0, 0:nq], eBv[:, 0, 0:nq], ALU.mult, ["eB"], ["eB"])
            mm(sp[:, 0:nq], onesm_f, eBv[:, 1, 0:nq], True, True, ["eB", "onesm_f"], [skeys[0]])
            act(eAv[:, 0, 0:nq], sp[:, 0:nq], AF.Ln, [skeys[0]], ["eA"], bias=EPS)
            act(eAv[:, 0, 0:nq], eAv[:, 0, 0:nq], AF.Exp, ["eA"], ["eA"], scale=-0.5)
            tt("dve", eBv[:, 0, 0:nq], eBv[:, 0, 0:nq], eAv[:, 0, 0:nq], ALU.mult, ["eB", "eA"], ["eB"])
            szk = [("szgaT", h, b0) for b0 in (0, 512, 1024) if b0 < row0 + nq and b0 + 512 > row0]
            tt("dve", ozTav[:, h, row0:row0 + nq], eBv[:, 0, 0:nq], szgaTv[:, h, row0:row0 + nq], ALU.mult,
               ["eB"] + szk, [("ozTa", h, row0)])

        def acc_ap(a, n):
            return psO[0:n, 512 * (a // 3) + 130 * (a % 3):512 * (a // 3) + 130 * (a % 3) + 130]

        def make_epilogue_A(h, row0, nq, osb, ep):
            def epilogue(nsb, Okeys):
                for sb in range(nsb):
                    n = min(128, nq - 128 * sb)
                    a0, a1 = sb, nsb + sb
                    k = ("ep", sb % 2)
                    e = ep[sb % 2]
                    o1 = osb[sb % 2]
                    cp("act", o1[0:n, 0:130], acc_ap(a0, n), [Okeys[a0 // 3]], [k])
                    cp("act", o1[0:n, 130:260], acc_ap(a1, n), [Okeys[a1 // 3]], [k])
                    P.op("dve", lambda E, e=e, o1=o1, n=n: E.reciprocal(e[0:n, 0:1], o1[0:n, 128:129]), [k], [k])
                    P.op("dve", lambda E, e=e, o1=o1, n=n: E.reciprocal(e[0:n, 1:2], o1[0:n, 258:259]), [k], [k])
                    tt("dve", e[0:n, 1:2], e[0:n, 1:2], neg_lam[0:n, :], ALU.mult, [k, "lamc"], [k])
                    ts("dve", o1[0:n, 130:258], o1[0:n, 130:258], e[0:n, 1:2], None, ALU.mult, ALU.bypass, [k], [k])
                    stt(o1[0:n, 0:128], o1[0:n, 0:128], e[0:n, 0:1], o1[0:n, 130:258], ALU.mult, ALU.add, [k], [k])
                    P.op("dve", lambda E, e=e, o1=o1, n=n: E.scalar_tensor_tensor(
                        o1[0:n, 130:258], o1[0:n, 0:128], 1.0, o1[0:n, 0:128], ALU.mult, ALU.mult,
                        accum_out=e[0:n, 2:3]), [k], [k])
                    act(e[0:n, 2:3], e[0:n, 2:3], AF.Ln, [k], [k], bias=EPS, scale=1.0 / 128)
                    act(e[0:n, 3:4], e[0:n, 2:3], AF.Exp, [k], [k], scale=-0.5)
                    r = row0 + 128 * sb
                    tix, rr = r // 128, r % 128
                    ob = o1[:, 260:324].bitcast(BF16)
                    stt(ob[0:n, :], o1[0:n, 0:128], e[0:n, 3:4], szgav[rr:rr + n, tix, 128 * h:128 * (h + 1)],
                        ALU.mult, ALU.mult, [k, ("szga", tix, (128 * h) // GC * GC)], [k])
                    tr(psT[:, 0:n], ob[0:n, :], ident_b[0:n, 0:n], [k, "ident_b"], [KT_T])
                    cp("act", ozTav[:, h, r:r + n], psT[:, 0:n], [KT_T], [("ozTa", h, r)])
            return epilogue

        def load_bias_tile(dstb, dkey, hk, hkkey, src, offset, nq, mask_ap, maskkey, nk=128):
            hsrc = bass.AP(tensor=src.tensor, offset=src.offset + offset, ap=[[1, 128], [1, nq]])
            dma("sp", hk[:, 0:nq], hsrc, ["fa_s", "gb_s"], [hkkey], "hk")
            bk, bkey = banks[6]
            mm(bk[:, 0:nq], antij, hk[:, 0:nq], True, True, [hkkey, "antij"], [bkey])
            if mask_ap is None:
                cp("dve", dstb[0:nk, 0:nq], bk[0:nk, 0:nq], [bkey], [dkey])
            else:
                tt("dve", dstb[0:nk, 0:nq], bk[0:nk, 0:nq], mask_ap[0:nk, 0:nq], ALU.add, [bkey, maskkey], [dkey])

        AR.push()
        Pbufs = [AR.bf16(1024) for _ in range(4)]
        tmpb = [AR.f32(1024) for _ in range(2)]
        lacc = AR.f32(1024)
        eA = AR.f32(1024)
        eB = AR.f32(1024)
        hkb = AR.f32(512)
        maskA = AR.f32(5 * 512)
        maskAv = maskA.rearrange("p (s q) -> p s q", s=5)
        dma("sp", maskAv, c_maskA.rearrange("s p q -> p s q"), (), ["maskA"], "c0")
        AR.push()
        biasA = [AR.f32(5 * 512) for _ in range(2)]
        KTw = [AR.bf16(64 * 128) for _ in range(2)]
        Vw = [AR.bf16(64 * 130) for _ in range(2)]
        for i in range(2):
            memset("pool", Vw[i].rearrange("p (t d) -> p t d", t=64)[:, :, 128:130], 1.0, [("Vw1", i)])

        def p3_dyn(E):
            if "pid" not in pid_holder:
                pid_holder["pid"] = E.partition_id()
            return pid_holder["pid"]

        u = 0
        for h in range(8):
            bA = biasA[h % 2]
            bAv = bA.rearrange("p (s q) -> p s q", s=5)
            for s in range(5):
                load_bias_tile(bAv[:, s, :], ("biasA", h % 2, s), hkb, "hkb", fa_s[h:h + 1, :], 512 - 128 * s, 512,
                               maskAv[:, s, :], "maskA")
            for qb in range(2):
                sl = u % 2
                ktw = KTw[sl]
                vw = Vw[sl].rearrange("p (t d) -> p t d", t=64)

                dma("pool", ktw, KTs[h, :, 512 * qb:512 * qb + 64 * 128], (), [("KTw", sl)], "ktw%d" % sl)
                for half in range(2):
                    dma("pool", vw[:, 32 * half:32 * half + 32, 0:128],
                        Vs[h, :, 4 * qb + 32 * half:4 * qb + 32 * half + 32, :], (), [("Vw", sl, half)],
                        "vw%d" % sl)
                tiles = []
                for s in range(64):
                    tiles.append(dict(
                        kT=(lambda m, s=s, ktw=ktw: ktw[64 * m:64 * m + 64, 128 * s:128 * (s + 1)]),
                        v=vw[:, s, :], nk=128,
                        bias=(bAv[:, s, :] if s < 5 else None), biaskey=("biasA", h % 2, s),
                        vis=visA[:, 64 * qb + s:64 * qb + s + 1],
                        keys=[("KTw", sl), ("Vw", sl, s // 32), ("Vw1", sl)]))
                r0 = 512 * qb
                ukey = ("qaT", h, r0)
                attention_unit_T(512, (lambda m, h=h, r0=r0: qaTv[64 * m:64 * m + 64, h, r0:r0 + 512]),
                                 tiles, Pbufs, tmpb, lacc, eA, eB, h, r0)
                u += 1
        P.barrier()
        AR.pop()

        AR.push()
        KTc = AR.bf16(8 * PAST)
        Vc = AR.bf16(16 * 8 * 130)
        cst = [AR.f32(1024) for _ in range(2)]
        bsA = [AR.f32(2 * 32) for _ in range(2)]
        KTcv = KTc.rearrange("p (h n) -> p h n", h=8)
        Vcv = Vc.rearrange("p (t h d) -> p t h d", t=16, h=8)
        memset("pool", Vcv[:, :, :, 128:130], 1.0, ["Vc1"])
        psTf = [b for b in banks]
        for t in range(16):
            sl = t % 2
            dma("sp", cst[sl], cka[128 * t:128 * (t + 1), :], (), [("cst", sl)], "cst%d" % sl)
            for hh in range(2):
                bk, bkey = next_bank()
                for j in range(4):
                    h = 4 * hh + j
                    tr(bk[:, 128 * j:128 * (j + 1)], cst[sl][:, 128 * h:128 * (h + 1)], ident_f, [("cst", sl), "ident_f"],
                       [bkey])
                evac(KTcv[:, 4 * hh:4 * hh + 4, 128 * t:128 * (t + 1)], bk.rearrange("p (h n) -> p h n", h=4),
                     [bkey], [("KTc", t)])
        for t in range(16):
            sl = t % 2
            dma("sp", cst[sl], cva[128 * t:128 * (t + 1), :], (), [("cst", sl)], "cst%d" % sl)
            cp("pool", Vcv[:, t, :, 0:128], cst[sl].rearrange("p (h d) -> p h d", h=8), [("cst", sl)], [("Vc", t)])
        for h in range(8):
            bs = bsA[h % 2].rearrange("p (s q) -> p s q", s=2)
            load_bias_tile(bs[:, 0, :], ("bsA", h % 2, 0), hkb, "hkb", fa_s[h:h + 1, :], 512, 32, None, None)
            load_bias_tile(bs[:, 1, :], ("bsA", h % 2, 1), hkb, "hkb", fa_s[h:h + 1, :], 384, 32, None, None, nk=32)
            tiles = []
            for t in range(16):
                tiles.append(dict(
                    kT=(lambda m, t=t, h=h: KTcv[64 * m:64 * m + 64, h, 128 * t:128 * (t + 1)]),
                    v=Vcv[:, t, h, :], nk=128, bias=(bs[:, 0, :] if t == 15 else None), biaskey=("bsA", h % 2, 0),
                    vis=visB[:, 15:16], keys=[("KTc", t), ("Vc", t), "Vc1"]))
            tiles.append(dict(
                kT=(lambda m, h=h: kaTsv[64 * m:64 * m + 64, h, 0:32]),
                v=vasv[0:32, h, :], nk=32, bias=bs[:, 1, :], biaskey=("bsA", h % 2, 1),
                vis=visB[0:32, 15:16], keys=[("kaTs", h), ("vas", (128 * h) // GC * GC), "vas1"]))
            attention_unit_T(32, (lambda m, h=h: qaTv[64 * m:64 * m + 64, h, 1024:1056]),
                             tiles, Pbufs, tmpb, lacc, eA, eB, h, 1024)
        P.barrier()
        AR.pop()
        AR.pop()
        AR.pop()

        AR.push()
        qbT = AR.bf16(8 * OWN)
        szb = AR.bf16(NT_OWN * 1024)
        kbT = AR.bf16(8 * 1664)
        vbb = AR.bf16(13 * 8 * 130)
        qbTv = qbT.rearrange("p (h t) -> p h t", h=8)
        szbv = szb.rearrange("p (t n) -> p t n", t=NT_OWN)
        kbTv = kbT.rearrange("p (h t) -> p h t", h=8)
        vbv = vbb.rearrange("p (t h d) -> p t h d", t=13, h=8)
        memset("pool", vbv[:, :, :, 128:130], 1.0, ["vb1"])

        AR.push()
        hTo, hTh = own_prep(AR, True)
        hTov = hTo.rearrange("p (c t) -> p c t", c=16)
        hThv = hTh.rearrange("p (c t) -> p c t", c=16)
        hokeys = [("hTo", t) for t in range(NT_OWN)]
        hhkeys = [("hTh", t) for t in range(4)]
        GC = 128
        wstf = [AR.f32(16 * GC) for _ in range(2)]
        wbfs = [AR.bf16(16 * GC) for _ in range(2)]
        ost = [AR.f32(GC) for _ in range(3)]
        sgt = [AR.f32(GC) for _ in range(2)]
        qscale = float(128 ** -0.5)
        for g in range(4096 // GC):
            col0 = 4096 + g * GC
            c1 = (g * GC) % 1024
            wbf, wkeys = wgroup(col0)
            if g * GC < 1024:
                feat_major(wbf, wkeys, GC, c1,
                           lambda c, t0, n: (qbTv[:, c // 128, t0:t0 + n], ("qbT", c // 128, t0)),
                           qscale, hTov, hokeys, tok_blocks)
            elif g * GC < 2048:
                feat_major(wbf, wkeys, GC, c1,
                           lambda c, t0, n: (kbTv[:, c // 128, t0:t0 + n], ("kbT", c // 128, t0)),
                           None, hThv, hhkeys, [(0, 512)])
                feat_major(wbf, wkeys, GC, c1,
                           lambda c, t0, n: (kbTv[:, c // 128, 512 + t0:512 + t0 + n], ("kbT", c // 128, 512 + t0)),
                           None, hTov, hokeys, tok_blocks)

                def sink(t, ps, bkey, c1=c1):
                    if t < 4:
                        return
                    o = ost[ost_i[0] % 3]
                    ok = ("ost", ost_i[0] % 3)
                    ost_i[0] += 1
                    evac(o[:, 0:GC], ps, [bkey], [ok])
                    if t < 8:
                        dma("sp", o_kbp[128 * (t - 4):128 * (t - 3), c1:c1 + GC], o[:, 0:GC], [ok],
                            [("okbp", t, c1)], "outst%d" % ok[1])
                    else:
                        dma("sp", o_kbs[480:512, c1:c1 + GC], o[0:32, 0:GC], [ok], [("okbs", c1)], "outst%d" % ok[1])
                tok_major(wbf, wkeys, GC, hTov, hokeys, NT_OWN, sink)
            elif g * GC < 3072:
                def sinkh(t, ps, bkey, c1=c1):
                    evac(vbv[:, t, c1 // 128:c1 // 128 + GC // 128, 0:128], ps.rearrange("p (h d) -> p h d", h=GC // 128),
                         [bkey], [("vb", t, c1)])
                tok_major(wbf, wkeys, GC, hThv, hhkeys, 4, sinkh)

                def sink(t, ps, bkey, c1=c1):
                    o = ost[ost_i[0] % 3]
                    ok = ("ost", ost_i[0] % 3)
                    ost_i[0] += 1
                    evac(o[:, 0:GC], ps, [bkey], [ok])
                    cp("pool", vbv[:, 4 + t, c1 // 128:c1 // 128 + GC // 128, 0:128],
                       o[:, 0:GC].rearrange("p (h d) -> p h d", h=GC // 128), [ok], [("vb", 4 + t, c1)])
                    if 4 <= t < 8:
                        dma("sp", o_vbp[128 * (t - 4):128 * (t - 3), c1:c1 + GC], o[:, 0:GC], [ok],
                            [("ovbp", t, c1)], "outst%d" % ok[1])
                    elif t == 8:
                        dma("sp", o_vbs[480:512, c1:c1 + GC], o[0:32, 0:GC], [ok], [("ovbs", c1)], "outst%d" % ok[1])
                tok_major(wbf, wkeys, GC, hTov, hokeys, NT_OWN, sink)
            else:
                def sink(t, ps, bkey, c1=c1):
                    silu_to(szbv[:, t, c1:c1 + GC], ps, bkey, ("szb", t, c1))
                tok_major(wbf, wkeys, GC, hTov, hokeys, NT_OWN, sink)
        dma("pool", o_kbs[0:480, :], ckb[32:512, :], (), [("okbs_c",)], "outc")
        dma("pool", o_vbs[0:480, :], cvb[32:512, :], (), [("ovbs_c",)], "outc")
        P.barrier()
        AR.pop()

        ozTb = AR.top_bf16(8 * OWN)
        ozTbv = ozTb.rearrange("p (h t) -> p h t", h=8)
        AR.push()
        Pbufs = [AR.bf16(1024) for _ in range(4)]
        tmpb = [AR.f32(1024) for _ in range(2)]
        osb = [AR.f32(324) for _ in range(2)]
        ep = [AR.f32(4) for _ in range(2)]
        hkb = AR.f32(512)
        maskB = AR.f32(5 * 128)
        maskBv = maskB.rearrange("p (s q) -> p s q", s=5)
        dma("sp", maskBv, c_maskB.rearrange("s p q -> p s q"), (), ["maskB"], "c0")
        bsB = [AR.f32(5 * 32) for _ in range(2)]
        KbTc = AR.bf16(8 * 512)
        Vbc = AR.bf16(4 * 8 * 130)
        AR.push()
        cst = [AR.f32(1024) for _ in range(2)]
        KbTcv = KbTc.rearrange("p (h n) -> p h n", h=8)
        Vbcv = Vbc.rearrange("p (t h d) -> p t h d", t=4, h=8)
        memset("pool", Vbcv[:, :, :, 128:130], 1.0, ["Vbc1"])
        for t in range(4):
            sl = t % 2
            dma("sp", cst[sl], ckb[128 * t:128 * (t + 1), :], (), [("cst", sl)], "cst%d" % sl)
            for hh in range(2):
                bk, bkey = next_bank()
                for j in range(4):
                    h = 4 * hh + j
                    tr(bk[:, 128 * j:128 * (j + 1)], cst[sl][:, 128 * h:128 * (h + 1)], ident_f, [("cst", sl), "ident_f"],
                       [bkey])
                evac(KbTcv[:, 4 * hh:4 * hh + 4, 128 * t:128 * (t + 1)], bk.rearrange("p (h n) -> p h n", h=4),
                     [bkey], [("KbTc", t)])
        for t in range(4):
            sl = t % 2
            dma("sp", cst[sl], cvb[128 * t:128 * (t + 1), :], (), [("cst", sl)], "cst%d" % sl)
            cp("pool", Vbcv[:, t, :, 0:128], cst[sl].rearrange("p (h d) -> p h d", h=8), [("cst", sl)], [("Vbc", t)])
        P.barrier()
        AR.pop()

        def make_epilogue_B(h, row0, nq):
            def epilogue(nsb, Okeys, abase):
                for sb in range(nsb):
                    n = min(128, nq - 128 * sb)
                    slot = (abase // 3 + sb) % 2
                    k = ("ep", slot)
                    e = ep[slot]
                    o1 = osb[slot]
                    cp("act", o1[0:n, 0:130], acc_ap(abase + sb, n), [Okeys[(abase + sb) // 3]], [k])
                    P.op("dve", lambda E, e=e, o1=o1, n=n: E.reciprocal(e[0:n, 0:1], o1[0:n, 128:129]), [k], [k])
                    r = row0 + 128 * sb
                    tix, rr = r // 128, r % 128
                    ob = o1[:, 260:324].bitcast(BF16)
                    stt(ob[0:n, :], o1[0:n, 0:128], e[0:n, 0:1], szbv[rr:rr + n, tix, 128 * h:128 * (h + 1)],
                        ALU.mult, ALU.mult, [k, ("szb", tix, (128 * h) // GC * GC)], [k])
                    tr(psT[:, 0:n], ob[0:n, :], ident_b[0:n, 0:n], [k, "ident_b"], [KT_T])
                    cp("act", ozTbv[:, h, r:r + n], psT[:, 0:n], [KT_T], [("ozTb", h, r)])
            return epilogue

        def kb_keys(h, c0, n):
            ks = []
            for (a, b_) in ((0, 512), (512, 1024), (1024, 1536), (1536, 1664)):
                if c0 < b_ and c0 + n > a:
                    ks.append(("kbT", h, a))
            return ks

        biasBall = AR.f32(5 * 8 * 128)
        bBv = biasBall.rearrange("p (s h q) -> p s h q", s=5, h=8)
        o1all = AR.f32(8 * 130)
        oball = AR.bf16(8 * 128)
        rall = AR.f32(8)
        o1v = o1all.rearrange("p (h d) -> p h d", h=8)
        obv = oball.rearrange("p (h d) -> p h d", h=8)
        for h in range(8):
            for s_ in range(5):
                load_bias_tile(bBv[:, s_, h, :], ("biasBall", s_), hkb, "hkb", gb_s[h:h + 1, :], 512 - 128 * s_, 128,
                               maskBv[:, s_, :], "maskB")
        SpsB = [(psA, [("ps", "A", 0), ("ps", "A", 1)]), (psB, [("ps", "B", 0), ("ps", "B", 1)])]
        OkB = [("ps", "O", 0), ("ps", "O", 1), ("ps", "O", 2)]
        steps = [(p, s_) for p in range(8) for s_ in range(5)]

        def b_qk(i):
            p, s_ = steps[i]
            w = p + s_
            sp, skeys = SpsB[i % 2]
            for h in range(8):
                mm(sp[:, 128 * h:128 * (h + 1)], kbTv[:, h, 128 * w:128 * (w + 1)], qbTv[:, h, 128 * p:128 * (p + 1)],
                   True, True, [], [skeys[h // 4]])

        def b_exp(i):
            p, s_ = steps[i]
            w = p + s_
            sp, skeys = SpsB[i % 2]
            tb = tmpb[i % 2]
            tk = ("tmpb", i % 2)
            pb = Pbufs[i % 4]
            pk = ("Pb", i % 4)
            tt("dve", tb, sp[:, :], bBv[:, s_, :, :].rearrange("p h q -> p (h q)"), ALU.add,
               skeys + [("biasBall", s_)], [tk])
            act(pb, tb, AF.Exp, [tk, "visB"], [pk], bias=visB[:, w:w + 1])

        def b_pv(i):
            p, s_ = steps[i]
            w = p + s_
            pb = Pbufs[i % 4]
            pk = ("Pb", i % 4)
            for h in range(8):
                mm(acc_ap(h, 128), pb[:, 128 * h:128 * (h + 1)], vbv[:, w, h, :], s_ == 0 and h % 3 == 0, s_ == 4,
                   [pk], [OkB[h // 3]])
            if s_ == 4:
                b_epilogue(p)

        def b_epilogue(p):
            r0 = 128 * p
            k = "o1all"
            cp("act", o1all[:, 0:390], psO[:, 0:390], [OkB[0]], [k])
            cp("act", o1all[:, 390:780], psO[:, 512:902], [OkB[1]], [k])
            cp("act", o1all[:, 780:1040], psO[:, 1024:1284], [OkB[2]], [k])
            P.op("dve", lambda E: E.reciprocal(rall[:, 0:8], o1v[:, :, 128]), [k], ["rall"])
            for h in range(8):
                stt(obv[:, h, :], o1v[:, h, 0:128], rall[:, h:h + 1], szbv[:, p, 128 * h:128 * (h + 1)],
                    ALU.mult, ALU.mult, [k, "rall"], ["oball"])
            for h in range(8):
                tr(psT[:, 128 * h:128 * (h + 1)], obv[:, h, :], ident_b, ["oball", "ident_b"], [KT_T])
            cp("act", ozTbv[:, :, r0:r0 + 128], psT.rearrange("p (h t) -> p h t", h=8), [KT_T], [("ozTb", "p", p)])

        b_qk(0)
        for i in range(len(steps)):
            if i + 1 < len(steps):
                b_qk(i + 1)
            b_exp(i)
            if i >= 1:
                b_pv(i - 1)
        b_pv(len(steps) - 1)

        for h in range(8):
            bS = bsB[h % 2].rearrange("p (s q) -> p s q", s=5)
            for s in range(4):
                load_bias_tile(bS[:, s, :], ("bsB", h % 2, s), hkb, "hkb", gb_s[h:h + 1, :], 512 - 128 * s, 32,
                               None, None)
            load_bias_tile(bS[:, 4, :], ("bsB", h % 2, 4), hkb, "hkb", gb_s[h:h + 1, :], 0, 32, None, None, nk=32)
            tiles = []
            for t in range(4):
                tiles.append(dict(
                    kT=(lambda m, t=t, h=h: KbTcv[:, h, 128 * t:128 * (t + 1)]),
                    v=Vbcv[:, t, h, :], nk=128, bias=bS[:, t, :], biaskey=("bsB", h % 2, t),
                    vis=visB[:, 15:16], keys=[("KbTc", t), ("Vbc", t), "Vbc1"]))
            tiles.append(dict(
                kT=(lambda m, h=h: kbTv[:, h, 512 + 1024:512 + 1056]),
                v=vbv[0:32, 12, h, :], nk=32, bias=bS[:, 4, :], biaskey=("bsB", h % 2, 4),
                vis=visB[0:32, 15:16], keys=[("kbT", h, 1536), ("vb", 12, (128 * h) // GC * GC), "vb1"]))
            attention_unit(1, 32, (lambda m, h=h: qbTv[:, h, 1024:1056]),
                           tiles, Pbufs, tmpb, make_epilogue_B(h, 1024, 32), ("qbT", h, 1024))
        P.barrier()
        AR.pop()
        AR.pop()

        AR.push()
        hTo, _ = own_prep(AR, False)
        hTov = hTo.rearrange("p (c t) -> p c t", c=16)
        hokeys = [("hTo", t) for t in range(NT_OWN)]
        dma("sp", gvec, post.partition_broadcast(128), (), ["gvec"], "c0")
        mT = AR.bf16(16 * OWN)
        mTv = mT.rearrange("p (c t) -> p c t", c=16)
        AR.push()
        wgst = [AR.f32(16 * 128) for _ in range(2)]
        wgbf = [AR.bf16(16 * 128) for _ in range(4)]
        wost = [AR.f32(8 * 128) for _ in range(2)]
        wobf = [AR.bf16(8 * 128) for _ in range(4)]
        sga = [AR.f32(512) for _ in range(2)]
        sgb = [AR.f32(512) for _ in range(2)]
        ya = [AR.f32(512) for _ in range(2)]
        k5 = [0]
        for cc in range(16):
            wk = {}
            for gi_, (c0, nm) in enumerate(((8192, "ga"), (10240, "gb"))):
                sl = (2 * cc + gi_) % 2
                sl4 = (2 * cc + gi_) % 4
                wk[nm] = (wgbf[sl4], load_wgroup(w_in[:, c0 + 128 * cc:c0 + 128 * (cc + 1)], 128, wgst[sl], wgbf[sl4],
                                                 ("wg5", sl4), "wg5%d" % sl4))
            for gi_, (wsrc, nm) in enumerate(((w_oa, "oa"), (w_ob, "ob"))):
                sl = (2 * cc + gi_) % 2
                sl4 = (2 * cc + gi_) % 4
                wk[nm] = (wobf[sl4], load_wgroup(wsrc[:, 128 * cc:128 * (cc + 1)], 128, wost[sl], wobf[sl4],
                                                 ("wo5", sl4), "wo5%d" % sl4, nk=8))
            for (t0, n) in tok_blocks:
                i2 = k5[0] % 2
                k5[0] += 1
                sig = {}
                for nm, sbuf_ in (("ga", sga[i2]), ("gb", sgb[i2])):
                    wbf, wkeys = wk[nm]
                    wv = wbf.rearrange("p (c n) -> p c n", c=16)
                    bk, bkey = next_bank()
                    for dc in range(16):
                        mm(bk[:, 0:n], wv[:, dc, :], hTov[:, dc, t0:t0 + n], dc == 0, dc == 15, wkeys + hokeys, [bkey])
                    sk = ("sg", nm, i2)
                    act(sbuf_[:, 0:n], bk[:, 0:n], AF.Sigmoid, [bkey], [sk])
                    sig[nm] = (sbuf_, sk)
                yk = ("ya", i2)
                for nm, ozv, oznm, gnm in (("oa", ozTav, "ozTa", "ga"), ("ob", ozTbv, "ozTb", "gb")):
                    wbf, wkeys = wk[nm]
                    wv = wbf.rearrange("p (c n) -> p c n", c=8)
                    bk, bkey = next_bank()
                    for h in range(8):
                        mm(bk[:, 0:n], wv[:, h, :], ozv[:, h, t0:t0 + n], h == 0, h == 7, wkeys, [bkey])
                    sb_, sk = sig[gnm]
                    if nm == "oa":
                        tt("dve", ya[i2][:, 0:n], bk[:, 0:n], sb_[:, 0:n], ALU.mult, [bkey, sk], [yk])
                    else:
                        tt("dve", sb_[:, 0:n], bk[:, 0:n], sb_[:, 0:n], ALU.mult, [bkey, sk], [sk])
                        tt("dve", mTv[:, cc, t0:t0 + n], sb_[:, 0:n], ya[i2][:, 0:n], ALU.add, [sk, yk], [("mT", cc, t0)])
        P.barrier()
        AR.pop()
        AR.n = ARENA_WORDS
        wout_bf_full = AR.bf16(16 * 2048)
        wost2 = [AR.f32(2048) for _ in range(2)]
        woutv = wout_bf_full.rearrange("p (c n) -> p c n", c=16)
        for dc in range(16):
            sl = dc % 2
            dma("pool", woutv[:, dc, :], w_out[128 * dc:128 * (dc + 1), :], (), [("wout", dc)], "wout")
        wokeys = [("wout", dc) for dc in range(16)]
        stage = hTo.bitcast(F32)
        yrow = [stage[:, 0:2048], stage[:, 2048:4096]]
        xrow = [stage[:, 4096:6144], stage[:, 6144:8192]]
        sq5 = stage[:, 8192:8200]
        junk5 = AR.f32(1024)
        for t in range(NT_OWN):
            i2 = t % 2
            nrow = min(128, 1056 - 128 * t)
            yk = ("yrow", i2)
            xk = ("xrow", i2)
            dma("sp", xrow[i2], xo[128 * t:128 * (t + 1), :], (), [xk], "xr%d" % i2)
            for cg in range(4):
                bk, bkey = next_bank()
                for kc in range(16):
                    mm(bk, mTv[:, kc, 128 * t:128 * (t + 1)], woutv[:, kc, 512 * cg:512 * (cg + 1)], kc == 0, kc == 15,
                       wokeys, [bkey])
                evac(yrow[i2][:, 512 * cg:512 * (cg + 1)], bk, [bkey], [yk])
            sq = sq5[:, 4 * i2:4 * i2 + 2]
            sk = ("sq5", i2)
            for hh in range(2):
                P.op("dve", lambda E, i2=i2, hh=hh: E.scalar_tensor_tensor(
                    junk5, yrow[i2][:, 1024 * hh:1024 * (hh + 1)], 1.0, yrow[i2][:, 1024 * hh:1024 * (hh + 1)],
                    ALU.mult, ALU.mult, accum_out=sq5[:, 4 * i2 + 2 + hh:4 * i2 + 3 + hh]), [yk], ["junk5", sk])
            tt("dve", sq[:, 0:1], sq5[:, 4 * i2 + 2:4 * i2 + 3], sq5[:, 4 * i2 + 3:4 * i2 + 4], ALU.add, [sk], [sk])
            act(sq[:, 0:1], sq[:, 0:1], AF.Ln, [sk], [sk], bias=EPS, scale=1.0 / D)
            act(sq[:, 1:2], sq[:, 0:1], AF.Exp, [sk], [sk], scale=-0.5)
            stt(yrow[i2], yrow[i2], sq[:, 1:2], gvec, ALU.mult, ALU.mult, [yk, sk, "gvec"], [yk])
            tt("pool", yrow[i2], yrow[i2], xrow[i2], ALU.add, [yk, xk], [yk])
            dma("sp", o_y[128 * t:128 * t + nrow, :], yrow[i2][0:nrow, :], [yk], [("oy", t)], "outy")
        P.barrier()
        AR.pop()

        P.barrier()
        sem_names = P.sem_names()
        sem_ctx = [nc.semaphore("s%d" % i) for i in range(len(sem_names))]
        sems = {}
        import contextlib
        with contextlib.ExitStack() as stack:
            for nm, c in zip(sem_names, sem_ctx):
                sems[nm] = stack.enter_context(c)
            block = stack.enter_context(nc.Block())

            def replay(E, eng):
                for waits, fn, tok in P.ops[eng]:
                    for s, v in waits:
                        E.wait_ge(sems[s], v)
                    if fn is None:
                        continue
                    ins = fn(E)
                    ins.then_inc(sems[tok[0]], 16 if tok[0].startswith("dma:") else 1)

            @block.tensor
            def _(E):
                replay(E, "pe")

            @block.scalar
            def _(E):
                replay(E, "act")

            @block.vector
            def _(E):
                replay(E, "dve")

            @block.gpsimd
            def _(E):
                replay(E, "pool")

            @block.sync
            def _(E):
                replay(E, "sp")
    return nc


def _constants():
    c = {}
    c["c_ident"] = np.eye(128, dtype=np.float32)
    c["c_antij"] = np.ascontiguousarray(np.eye(128, dtype=np.float32)[::-1])
    u = np.arange(FA_LEN)
    d = u - 511
    bk = _t5_bucket_np(-d)
    oh = np.zeros((32, FA_LEN), np.float32)
    oh[bk, u] = 1.0
    oh[15, :] -= 1.0
    c["c_oha"] = oh
    v = np.arange(GB_LEN)
    idx = np.clip(127 - v, -128, 128) + 128
    ohb = np.zeros((384, GB_LEN), np.float32)
    ohb[idx, v] = 1.0
    c["c_ohb"] = ohb
    i = np.arange(128)[:, None]
    j = np.arange(512)[None, :]
    mA = np.zeros((5, 128, 512), np.float32)
    for s in range(5):
        krel = (s - 1) * 128 + i
        mA[s] = np.where(np.floor_divide(krel, 64) > (j // 64), NEG, 0.0)
    c["c_maskA"] = mA
    j2 = np.arange(128)[None, :]
    mB = np.zeros((5, 128, 128), np.float32)
    for s in range(5):
        kc = (128 * s + i) // 64 - 8
        qc = j2 // 64
        ok = (kc <= qc) & (kc >= qc - 8)
        mB[s] = np.where(ok, 0.0, NEG)
    c["c_maskB"] = mB
    return c


def _vis_tables(core):
    visA = np.zeros((128, 128), np.float32)
    for qb in range(2):
        T0 = 8 * core + 4 * qb
        for s in range(64):
            tile = T0 - 1 + s
            if tile >= 64:
                visible = True
            else:
                visible = (0 <= tile <= T0 + 3)
            visA[:, 64 * qb + s] = 0.0 if visible else NEG
    visB = np.zeros((128, 16), np.float32)
    if core == 0:
        visB[:, 0:4] = NEG
    return visA, visB


_NC_CACHE = {}


def kernel(x_prompt, x_sample, cache_k_a, cache_v_a, cache_k_b, cache_v_b, t5_bias, pre_norm, post_norm,
           w_in, lambda_q1, lambda_k1, lambda_q2, lambda_k2, subln_a, rel_bias_b, w_o_a, w_o_b, w_out):
    f = lambda a: np.ascontiguousarray(np.asarray(a, dtype=np.float32))
    xp = f(x_prompt)[0]
    xs = f(x_sample)
    consts = _constants()
    relbT = np.zeros((384, 8), np.float32)
    relbT[:257] = f(rel_bias_b)[0].T
    lam4 = np.concatenate([f(lambda_q1)[0], f(lambda_k1)[0], f(lambda_q2)[0], f(lambda_k2)[0]])[None, :]
    shared = {
        "w_in": f(w_in)[0], "w_oa": f(w_o_a)[0], "w_ob": f(w_o_b)[0], "w_out": f(w_out)[0],
        "pre": f(pre_norm), "post": f(post_norm), "subln": f(subln_a), "lam4": np.ascontiguousarray(lam4),
        "t5": f(t5_bias), "relbT": relbT,
    }
    shared.update(consts)
    in_maps = []
    for c in range(NCORES):
        xo = np.zeros((OWN, D), np.float32)
        xo[:ROWS] = xp[ROWS * c:ROWS * (c + 1)]
        xo[ROWS:ROWS + NS] = xs[c]
        xh = np.zeros((512, D), np.float32)
        if c > 0:
            xh[:] = xp[ROWS * c - 512:ROWS * c]
        visA, visB = _vis_tables(c)
        m = dict(shared)
        m.update({
            "xf": np.ascontiguousarray(np.roll(xp, -128 * (8 * c - 1), axis=0)),
            "xo": xo, "xh": xh,
            "cka": f(cache_k_a)[0, c].reshape(PAST, 1024), "cva": f(cache_v_a)[0, c].reshape(PAST, 1024),
            "ckb": f(cache_k_b)[0, c].reshape(512, 1024), "cvb": f(cache_v_b)[0, c].reshape(512, 1024),
            "c_visA": visA, "c_visB": visB,
        })
        in_maps.append(m)
    if "nc" not in _NC_CACHE:
        _NC_CACHE["nc"] = build()
    nc = _NC_CACHE["nc"]
    res = run_bass_kernel_spmd(nc, in_maps, core_ids=list(range(NCORES)))
    R = res.results
    y_prompt = np.concatenate([R[c]["y"][:ROWS] for c in range(NCORES)], 0)[None]
    y_sample = np.stack([R[c]["y"][ROWS:ROWS + NS] for c in range(NCORES)], 0)
    kap = np.concatenate([R[c]["ka"][:ROWS] for c in range(NCORES)], 0).reshape(1, 1, SEQ, 16, 64)
    vap = np.concatenate([R[c]["va"][:ROWS] for c in range(NCORES)], 0).reshape(1, 1, SEQ, 8, 128)
    kbp = R[NCORES - 1]["kbp"].reshape(1, 1, 512, 8, 128)
    vbp = R[NCORES - 1]["vbp"].reshape(1, 1, 512, 8, 128)
    kas = np.stack([R[c]["ka"][ROWS:ROWS + NS] for c in range(NCORES)], 0).reshape(1, NCORES, NS, 16, 64)
    vas = np.stack([R[c]["va"][ROWS:ROWS + NS] for c in range(NCORES)], 0).reshape(1, NCORES, NS, 8, 128)
    kbs = np.stack([R[c]["kbs"] for c in range(NCORES)], 0).reshape(1, NCORES, 512, 8, 128)
    vbs = np.stack([R[c]["vbs"] for c in range(NCORES)], 0).reshape(1, NCORES, 512, 8, 128)
    out = (y_prompt, y_sample, kap, vap, kbp, vbp, kas, vas, kbs, vbs)
    return tuple(np.ascontiguousarray(o.astype(np.float32)) for o in out)
```

```python
import numpy as np
import ml_dtypes
import concourse.bass as bass
import concourse.mybir as mybir
from concourse.bass_utils import run_bass_kernel_spmd

F32 = mybir.dt.float32
BF16 = mybir.dt.bfloat16
AF = mybir.ActivationFunctionType
ALU = mybir.AluOpType

NCORES = 8
D = 2048
SEQ = 8192
ROWS = 1024
NS = 32
OWN = 1152
NT_OWN = 9
PAST = 2048
WIN = 12288
NEG = -30000.0
EPS = 1e-6
LAM_INIT = 0.2
FA_LEN = 1152
GB_LEN = 768
ENGS = ("pe", "act", "dve", "pool", "sp")


class Prog:
    def __init__(self):
        self.ops = {e: [] for e in ENGS}
        self.cnt = {e: 0 for e in ENGS}
        self.last_w = {}
        self.readers = {}
        self.waited = {e: {} for e in ENGS}
        self.dma_cnt = {}

    def _deps(self, eng, reads, writes):
        deps = {}
        def add(tok):
            if tok is None:
                return
            s, v = tok
            if s == eng and eng == "pe":
                return
            if s.startswith("dma:"):
                v = 16 * self.dma_cnt[s[4:]]
            if deps.get(s, 0) < v:
                deps[s] = v
        for k in reads:
            add(self.last_w.get(k))
        for k in writes:
            add(self.last_w.get(k))
            for t in self.readers.get(k, ()):
                add(t)
        waits = []
        for s, v in deps.items():
            if self.waited[eng].get(s, 0) < v:
                self.waited[eng][s] = v
                waits.append((s, v))
        return waits

    def op(self, eng, fn, reads=(), writes=(), tag=None):
        waits = self._deps(eng, reads, writes)
        if tag is not None:
            self.dma_cnt[tag] = self.dma_cnt.get(tag, 0) + 1
            tok = ("dma:" + tag, 16 * self.dma_cnt[tag])
        else:
            self.cnt[eng] += 1
            tok = (eng, self.cnt[eng])
        self.ops[eng].append((waits, fn, tok))
        for k in writes:
            self.last_w[k] = tok
            self.readers[k] = []
        for k in reads:
            self.readers.setdefault(k, []).append(tok)
        return tok

    def barrier(self):
        allt = [(e, self.cnt[e]) for e in ENGS if self.cnt[e] > 0]
        allt += [("dma:" + t, 16 * c) for t, c in self.dma_cnt.items()]
        for e in ENGS:
            waits = []
            for s, v in allt:
                if self.waited[e].get(s, 0) < v:
                    self.waited[e][s] = v
                    waits.append((s, v))
            if waits:
                self.ops[e].append((waits, None, None))
        self.last_w = {}
        self.readers = {}

    def sem_names(self):
        names = [e for e in ENGS if e != "sp"]
        names += ["dma:" + t for t in self.dma_cnt]
        return names


class Arena:
    def __init__(self, ap_f32, nwords):
        self.ap = ap_f32
        self.n = nwords
        self.off = 0
        self.marks = []

    def push(self):
        self.marks.append(self.off)

    def pop(self):
        self.off = self.marks.pop()

    def f32(self, n):
        assert self.off + n <= self.n, ("arena overflow", self.off, n, self.n)
        a = self.ap[:, self.off:self.off + n]
        self.off += n
        return a

    def bf16(self, n):
        w = (n + 1) // 2
        return self.f32(w).bitcast(BF16)

    def top_bf16(self, n):
        w = (n + 1) // 2
        assert self.n - w >= self.off
        self.n -= w
        return self.ap[:, self.n:self.n + w].bitcast(BF16)


def _t5_bucket_np(rel):
    rel = np.asarray(rel, np.int64)
    half = 16
    max_exact = 8
    ret = np.where(rel > 0, half, 0)
    n = np.abs(rel)
    nf = np.maximum(n, 1).astype(np.float32)
    large = max_exact + (np.log(nf / np.float32(max_exact)) / np.float32(np.log(128 / max_exact))
                         * np.float32(half - max_exact)).astype(np.int32)
    large = np.minimum(large, half - 1)
    return ret + np.where(n < max_exact, n, large)


def build():
    nc = bass.Bass("TRN2", target_bir_lowering=False)

    def din(name, shape, dt=F32):
        return nc.dram_tensor(name, list(shape), dt, kind="ExternalInput").ap()

    def dout(name, shape):
        return nc.dram_tensor(name, list(shape), F32, kind="ExternalOutput").ap()

    xf = din("xf", [SEQ, D])
    xo = din("xo", [OWN, D])
    xh = din("xh", [512, D])
    w_in = din("w_in", [D, WIN])
    w_oa = din("w_oa", [1024, D])
    w_ob = din("w_ob", [1024, D])
    w_out = din("w_out", [D, D])
    cka = din("cka", [PAST, 1024])
    cva = din("cva", [PAST, 1024])
    ckb = din("ckb", [512, 1024])
    cvb = din("cvb", [512, 1024])
    pre = din("pre", [1, D])
    post = din("post", [1, D])
    subln = din("subln", [1, 128])
    lam4 = din("lam4", [1, 256])
    t5 = din("t5", [32, 8])
    relbT = din("relbT", [384, 8])
    c_ident = din("c_ident", [128, 128])
    c_antij = din("c_antij", [128, 128])
    c_oha = din("c_oha", [32, FA_LEN])
    c_ohb = din("c_ohb", [384, GB_LEN])
    c_maskA = din("c_maskA", [5, 128, 512])
    c_maskB = din("c_maskB", [5, 128, 128])
    c_visA = din("c_visA", [128, 128])
    c_visB = din("c_visB", [128, 16])

    o_y = dout("y", [1056, D])
    o_ka = dout("ka", [1056, 1024])
    o_va = dout("va", [1056, 1024])
    o_kbp = dout("kbp", [512, 1024])
    o_vbp = dout("vbp", [512, 1024])
    o_kbs = dout("kbs", [512, 1024])
    o_vbs = dout("vbs", [512, 1024])

    KTs = nc.dram_tensor("KTs", [8, 128, 2 * SEQ], BF16).ap()
    Vs = nc.dram_tensor("Vs", [8, 128, 128, 128], BF16).ap()
    fa_s = nc.dram_tensor("fa_s", [8, FA_LEN], F32).ap()
    gb_s = nc.dram_tensor("gb_s", [8, GB_LEN], F32).ap()

    P = Prog()
    pid_holder = {}

    ARENA_WORDS = 53000
    with (
        nc.sbuf_tensor("arena", [128, ARENA_WORDS], F32) as arena_t,
        nc.psum_tensor("psA", [128, 1024], F32) as psA,
        nc.psum_tensor("psB", [128, 1024], F32) as psB,
        nc.psum_tensor("psO", [128, 1536], F32) as psO,
        nc.psum_tensor("psT", [128, 1024], BF16) as psT,
    ):
        AR = Arena(arena_t[:, :], ARENA_WORDS)
        banks = []
        for nm, t, nb in (("A", psA, 2), ("B", psB, 2), ("O", psO, 3)):
            for i in range(nb):
                banks.append((t[:, 512 * i:512 * (i + 1)], ("ps", nm, i)))
        bank_rr = [0]

        def next_bank():
            b = banks[bank_rr[0] % len(banks)]
            bank_rr[0] += 1
            return b
        KT_T = ("ps", "T", 0)

        def dma(q, out, in_, reads, writes, tag):
            P.op(q, lambda E: E.dma_start(out=out, in_=in_), reads, writes, tag=tag)

        def mm(out, lhsT, rhs, start, stop, reads, writes):
            P.op("pe", lambda E: E.matmul(out, lhsT, rhs, start=start, stop=stop,
                                          skip_group_check=True), reads, writes)

        def tr(out, in_, ident, reads, writes):
            P.op("pe", lambda E: E.transpose(out, in_, ident), reads, writes)

        def act(out, in_, func, reads, writes, bias=None, scale=None, accum=None):
            kw = {}
            if bias is not None:
                kw["bias"] = bias
            if scale is not None:
                kw["scale"] = scale
            if accum is not None:
                kw["accum_out"] = accum
            P.op("act", lambda E: E.activation(out, in_, func, **kw), reads, writes)

        def ts(eng, out, in0, s1, s2, op0, op1, reads, writes, accum=None):
            if accum is None:
                P.op(eng, lambda E: E.tensor_scalar(out, in0, s1, s2, op0, op1), reads, writes)
            else:
                P.op(eng, lambda E: E.tensor_scalar(out, in0, s1, s2, op0, op1, accum_out=accum),
                     reads, writes)

        def tt(eng, out, in0, in1, op, reads, writes):
            P.op(eng, lambda E: E.tensor_tensor(out, in0, in1, op), reads, writes)

        def stt(out, in0, scalar, in1, op0, op1, reads, writes):
            P.op("dve", lambda E: E.scalar_tensor_tensor(out, in0, scalar, in1, op0, op1), reads, writes)

        def cp(eng, out, in_, reads, writes):
            if eng == "act":
                P.op("act", lambda E: E.copy(out, in_), reads, writes)
            else:
                P.op(eng, lambda E: E.tensor_copy(out, in_), reads, writes)

        def memset(eng, ap, val, writes):
            P.op(eng, lambda E: E.memset(ap, val), (), writes)

        evac_rr = [0]

        def evac(out, in_, reads, writes):
            e = ("act", "dve")[evac_rr[0] % 2]
            evac_rr[0] += 1
            cp(e, out, in_, reads, writes)

        ident_f = AR.f32(128)
        antij = AR.f32(128)
        ident_b = AR.bf16(128)
        gvec = AR.f32(2048)
        sublnb = AR.f32(128)
        lamt = AR.f32(256)
        lamtmp = AR.f32(64)
        lamc = AR.f32(8)
        visA = AR.f32(128)
        visB = AR.f32(16)
        sublnc = AR.f32(2)
        ones_f = AR.f32(128)
        onesm_f = AR.f32(128)
        ones_b = AR.bf16(128)

        dma("sp", ident_f, c_ident, (), ["ident_f"], "c0")
        dma("sp", antij, c_antij, (), ["antij"], "c0")
        dma("sp", visA, c_visA, (), ["visA"], "c0")
        dma("sp", visB, c_visB, (), ["visB"], "c0")
        dma("sp", gvec, pre.partition_broadcast(128), (), ["gvec"], "c0")
        dma("sp", sublnb, subln.partition_broadcast(128), (), ["sublnb"], "c0")
        dma("sp", lamt, lam4.partition_broadcast(128), (), ["lamt"], "c0")
        dma("sp", sublnc[:, 0:1], subln.rearrange("o d -> d o"), (), ["sublnc"], "c0")
        AR.push()
        t5_sb = AR.f32(8)
        oha_sb = AR.f32(FA_LEN)
        rb_sb = AR.f32(24)
        ohb_sb = AR.f32(3 * GB_LEN)
        vec_sb = AR.f32(FA_LEN)
        dma("sp", t5_sb[0:32, :], t5, (), ["t5_sb"], "c0")
        dma("sp", oha_sb[0:32, :], c_oha, (), ["oha_sb"], "c0")
        dma("sp", rb_sb.rearrange("p (c h) -> p c h", c=3), relbT.rearrange("(c p) h -> p c h", p=128),
            (), ["rb_sb"], "c0")
        dma("sp", ohb_sb.rearrange("p (c n) -> p c n", c=3), c_ohb.rearrange("(c p) n -> p c n", p=128),
            (), ["ohb_sb"], "c0")
        P.barrier()
        cp("dve", ident_b, ident_f, ["ident_f"], ["ident_b"])
        ts("dve", sublnb, sublnb, 1.0 - LAM_INIT, None, ALU.mult, ALU.bypass, ["sublnb"], ["sublnb"])
        ts("dve", sublnc[:, 0:1], sublnc[:, 0:1], 1.0 - LAM_INIT, None, ALU.mult, ALU.bypass, ["sublnc"], ["sublnc"])
        memset("dve", ones_f, 1.0, ["ones_f"])
        memset("dve", onesm_f, 1.0 / 128, ["onesm_f"])
        memset("dve", ones_b, 1.0, ["ones_b"])
        for i in range(2):
            P.op("dve", (lambda i: lambda E: E.scalar_tensor_tensor(
                lamtmp, lamt[:, 128 * i:128 * i + 64], 1.0, lamt[:, 128 * i + 64:128 * i + 128],
                ALU.mult, ALU.mult, accum_out=lamc[:, i:i + 1]))(i),
                ["lamt"], ["lamtmp", "lamc"])
        act(lamc[:, 2:4], lamc[:, 0:2], AF.Exp, ["lamc"], ["lamc"])
        tt("dve", lamc[:, 4:5], lamc[:, 2:3], lamc[:, 3:4], ALU.subtract, ["lamc"], ["lamc"])
        ts("dve", lamc[:, 5:6], lamc[:, 4:5], LAM_INIT, -1.0, ALU.add, ALU.mult, ["lamc"], ["lamc"])
        neg_lam = lamc[:, 5:6]

        for j in range(3):
            bk, bkey = next_bank()
            mm(bk[0:8, 0:384], t5_sb[0:32, 0:8], oha_sb[0:32, 384 * j:384 * (j + 1)], True, True,
               ["t5_sb", "oha_sb"], [bkey])
            cp("dve", vec_sb[0:8, 384 * j:384 * (j + 1)], bk[0:8, 0:384], [bkey], ["vec_sb"])
        dma("sp", fa_s, vec_sb[0:8, 0:FA_LEN], ["vec_sb"], ["fa_s"], "c1")
        for j in range(2):
            bk, bkey = next_bank()
            for c in range(3):
                mm(bk[0:8, 0:384], rb_sb[:, 8 * c:8 * c + 8], ohb_sb[:, GB_LEN * c + 384 * j:GB_LEN * c + 384 * (j + 1)],
                   c == 0, c == 2, ["rb_sb", "ohb_sb"], [bkey])
            cp("dve", vec_sb[0:8, 384 * j:384 * (j + 1)], bk[0:8, 0:384], [bkey, "fa_s"], ["vec_sb"])
        dma("sp", gb_s, vec_sb[0:8, 0:GB_LEN], ["vec_sb"], ["gb_s"], "c1")
        P.barrier()
        AR.pop()

        def norm_jobs(src, ntiles, hT, hT_key, tagbase, xs, hb, ssq, junk):
            hTv = hT.rearrange("p (c t) -> p c t", c=16)
            jobs = []

            def mk(t, part):
                def job():
                    sl = t % len(xs)
                    xk = ("xs", tagbase, sl)
                    hk = ("hb", tagbase, t % 2)
                    sk = ("ssq", tagbase, t % 2)
                    sq = ssq[:, 2 * (t % 2):2 * (t % 2) + 2]
                    if part == "norm":
                        dma("sp", xs[sl], src[128 * t:128 * (t + 1), :], (), [xk], "x%s%d" % (tagbase, sl))
                        P.op("dve", lambda E: E.scalar_tensor_tensor(
                            junk, xs[sl], 1.0, xs[sl], ALU.mult, ALU.mult, accum_out=sq[:, 0:1]),
                            [xk], [("junk", tagbase), sk])
                        act(sq[:, 0:1], sq[:, 0:1], AF.Ln, [sk], [sk], bias=EPS, scale=1.0 / D)
                        act(sq[:, 1:2], sq[:, 0:1], AF.Exp, [sk], [sk], scale=-0.5)
                        stt(hb[t % 2], xs[sl], sq[:, 1:2], gvec, ALU.mult, ALU.mult, [xk, sk, "gvec"], [hk])
                        return
                    half = part
                    for j in range(8):
                        dc = 8 * half + j
                        tr(psT[:, 128 * j:128 * (j + 1)], hb[t % 2][:, 128 * dc:128 * (dc + 1)], ident_b,
                           [hk, "ident_b"], [KT_T])
                    evac(hTv[:, 8 * half:8 * half + 8, 128 * t:128 * (t + 1)],
                         psT.rearrange("p (c t) -> p c t", c=8), [KT_T], [(hT_key, t)])
                return job
            for t in range(ntiles):
                if t == 0:
                    jobs.append(mk(0, "norm"))
                if t + 1 < ntiles:
                    jobs.append(mk(t + 1, "norm"))
                jobs.append(mk(t, 0))
                jobs.append(mk(t, 1))
            return jobs

        def norm_tiles(src, ntiles, hT, hT_key, tagbase, xs, hb, ssq, junk):
            for j in norm_jobs(src, ntiles, hT, hT_key, tagbase, xs, hb, ssq, junk):
                j()

        AR.push()
        Wkv = AR.bf16(16 * 2048)
        wst = [AR.f32(2048) for _ in range(2)]
        xs1 = [AR.f32(2048) for _ in range(3)]
        hb1 = [AR.bf16(2048) for _ in range(2)]
        ssq1 = AR.f32(4)
        junk1 = AR.bf16(2048)
        hTb = [AR.bf16(16 * 512) for _ in range(2)]
        ktb = [AR.bf16(8 * 512) for _ in range(2)]
        vbk = [AR.bf16(4 * 1024) for _ in range(2)]
        Wkvv = Wkv.rearrange("p (c n) -> p c n", c=16)
        for dc in range(16):
            sl = dc % 2
            dma("pool", Wkvv[:, dc, :], w_in[128 * dc:128 * (dc + 1), 1024:3072], (), [("Wkv", dc)], "wkv")
        Wkeys = [("Wkv", dc) for dc in range(16)]
        KTsv = KTs.rearrange("h p n -> p h n")
        def p1_jobs(b):
            return norm_jobs(xf[512 * b:512 * (b + 1), :], 4, hTb[b % 2], ("hTb", b % 2), "p1", xs1, hb1, ssq1, junk1)
        for j in p1_jobs(0):
            j()
        for b in range(16):
            hT = hTb[b % 2]
            hTkey = ("hTb", b % 2)
            pending = p1_jobs(b + 1) if b + 1 < 16 else []
            hTv = hT.rearrange("p (c t) -> p c t", c=16)
            hkeys = [(hTkey, t) for t in range(4)]
            kt = ktb[b % 2].rearrange("p (h t) -> p h t", h=8)
            ktk = ("ktb", b % 2)
            vb_ = vbk[b % 2].rearrange("p (t n) -> p t n", t=4)
            vk = ("vbk", b % 2)
            gcount = [0]

            def after_group():
                gcount[0] += 1
                if pending:
                    pending.pop(0)()
            for h in range(8):
                bk, bkey = next_bank()
                for dc in range(16):
                    mm(bk, Wkvv[:, dc, 128 * h:128 * (h + 1)], hTv[:, dc, :], dc == 0, dc == 15,
                       hkeys + [Wkeys[dc]], [bkey])
                evac(kt[:, h, :], bk, [bkey], [(ktk, h)])
                after_group()
            for rep in range(2):
                dma("pool", KTsv[:, :, rep * SEQ + 512 * b:rep * SEQ + 512 * (b + 1)], kt,
                    [(ktk, h) for h in range(8)], [("KTs", b, rep)], "kts%d" % (b % 2))
            for t in range(4):
                for half in range(2):
                    bk, bkey = next_bank()
                    for dc in range(16):
                        mm(bk, hTv[:, dc, 128 * t:128 * (t + 1)], Wkvv[:, dc, 1024 + 512 * half:1024 + 512 * (half + 1)],
                           dc == 0, dc == 15, hkeys + [Wkeys[dc]], [bkey])
                    evac(vb_[:, t, 512 * half:512 * (half + 1)], bk, [bkey], [(vk, t, half)])
                    after_group()
                for rep in range(2):
                    dma("pool", Vs[:, :, 64 * rep + 4 * b + t, :].rearrange("h p d -> p h d"),
                        vb_[:, t, :].rearrange("p (h d) -> p h d", h=8),
                        [(vk, t, 0), (vk, t, 1)], [("Vs", b, t, rep)], "vs%d" % (b % 2))
            while pending:
                pending.pop(0)()
        P.barrier()
        AR.pop()

        ozTa = AR.bf16(8 * OWN)
        ozTav = ozTa.rearrange("p (h t) -> p h t", h=8)

        def load_wgroup(wsrc_cols, ncol, wst_f, wbf, key, tag, nk=16, stkey=None):
            half = nk // 2
            for hh in range(2):
                dma("pool", wbf[:, hh * half * ncol:(hh + 1) * half * ncol].rearrange("p (c n) -> p c n", c=half),
                    wsrc_cols[128 * half * hh:128 * half * (hh + 1), :].rearrange("(c p) n -> p c n", p=128),
                    (), [(key, "bf", hh)], tag)
            return [(key, "bf", 0), (key, "bf", 1)]

        def own_prep(AR, with_halo):
            hTo = AR.bf16(16 * OWN)
            hTh = AR.bf16(16 * 512) if with_halo else None
            AR.push()
            xs = [AR.f32(2048) for _ in range(2)]
            hb = [AR.bf16(2048) for _ in range(2)]
            ssq = AR.f32(4)
            junk = AR.bf16(2048)
            norm_tiles(xo, NT_OWN, hTo, "hTo", "own", xs, hb, ssq, junk)
            if with_halo:
                norm_tiles(xh, 4, hTh, "hTh", "own", xs, hb, ssq, junk)
            P.barrier()
            AR.pop()
            return hTo, hTh

        tok_blocks = [(0, 512), (512, 512), (1024, 128)]

        AR.push()
        qaT = AR.bf16(8 * OWN)
        szga = AR.bf16(8 * OWN)
        kaTs = AR.bf16(8 * 32)
        vas = AR.bf16(8 * 130)
        qaTv = qaT.rearrange("p (h t) -> p h t", h=8)
        szgaTv = szga.rearrange("p (h t) -> p h t", h=8)
        kaTsv = kaTs.rearrange("p (h t) -> p h t", h=8)
        vasv = vas.rearrange("p (h d) -> p h d", h=8)
        memset("pool", vasv[:, :, 128:130], 1.0, ["vas1"])

        AR.push()
        hTo, _ = own_prep(AR, False)
        hTov = hTo.rearrange("p (c t) -> p c t", c=16)
        hokeys = [("hTo", t) for t in range(NT_OWN)]
        GC = 256
        wstf = [None, None]
        wbfs = [AR.bf16(16 * GC) for _ in range(2)]
        ost = [AR.f32(GC) for _ in range(3)]
        sgt = [AR.f32(GC) for _ in range(2)]
        ost_i = [0]

        def feat_major(wbf, wkeys, ncol, col0, dst_fn, scale, tokv, tokkeys, blocks, evac_fn=None):
            wv = wbf.rearrange("p (c n) -> p c n", c=16)
            for cc in range(ncol // 128):
                for (t0, n) in blocks:
                    bk, bkey = next_bank()
                    for dc in range(16):
                        mm(bk[:, 0:n], wv[:, dc, 128 * cc:128 * (cc + 1)], tokv[:, dc, t0:t0 + n], dc == 0, dc == 15,
                           wkeys + tokkeys, [bkey])
                    dst, dkey = dst_fn(col0 + 128 * cc, t0, n)
                    if evac_fn is not None:
                        evac_fn(dst, dkey, bk[:, 0:n], bkey, n)
                    elif scale is None:
                        evac(dst, bk[:, 0:n], [bkey], [dkey])
                    else:
                        ts("dve", dst, bk[:, 0:n], scale, None, ALU.mult, ALU.bypass, [bkey], [dkey])

        def tok_major(wbf, wkeys, ncol, tokv, tokkeys, ntiles, sink):
            wv = wbf.rearrange("p (c n) -> p c n", c=16)
            for t in range(ntiles):
                bk, bkey = next_bank()
                for dc in range(16):
                    mm(bk[:, 0:ncol], tokv[:, dc, 128 * t:128 * (t + 1)], wv[:, dc, :], dc == 0, dc == 15,
                       wkeys + tokkeys, [bkey])
                sink(t, bk[:, 0:ncol], bkey)

        def out_rows(dst, col0, ncol, src_ap, skey, t, nrows_total):
            r0 = 128 * t
            n = min(128, nrows_total - r0)
            if n <= 0:
                return
            dma("sp", dst[r0:r0 + n, col0:col0 + ncol], src_ap[0:n, :], [skey], [("out", id(dst), t, col0)], "outst%d" % skey[1])

        def silu_to(dst, psum, bkey, dkey, mulb=None, mkey=None):
            if mulb is None:
                act(dst, psum, AF.Silu, [bkey], [dkey])
                return
            s_ = sgt[ost_i[0] % 2]
            sk = ("sgt", ost_i[0] % 2)
            ost_i[0] += 1
            n = psum.shape[-1]
            act(s_[:, 0:n], psum, AF.Silu, [bkey], [sk])
            tt("dve", dst, s_[:, 0:n], mulb, ALU.mult, [sk, mkey], [dkey])

        gi = [0]

        def wgroup(col0):
            sl = gi[0] % 2
            gi[0] += 1
            keys = load_wgroup(w_in[:, col0:col0 + GC], GC, wstf[sl], wbfs[sl], ("wg", sl), "wg%d" % sl)
            return wbfs[sl], keys

        sgt2 = [AR.f32(512) for _ in range(2)]

        for g in range(4096 // GC):
            col0 = g * GC
            wbf, wkeys = wgroup(col0)
            if col0 < 1024:
                feat_major(wbf, wkeys, GC, col0,
                           lambda c, t0, n: (qaTv[:, c // 128, t0:t0 + n], ("qaT", c // 128, t0)),
                           0.125, hTov, hokeys, tok_blocks)
            elif col0 < 2048:
                c1 = col0 - 1024

                def sink(t, ps, bkey, c1=c1):
                    o = ost[ost_i[0] % 3]
                    ok = ("ost", ost_i[0] % 3)
                    ost_i[0] += 1
                    evac(o[:, 0:GC], ps, [bkey], [ok])
                    out_rows(o_ka, c1, GC, o, ok, t, 1056)
                tok_major(wbf, wkeys, GC, hTov, hokeys, NT_OWN, sink)
                feat_major(wbf, wkeys, GC, c1,
                           lambda c, t0, n: (kaTsv[:, c // 128, 0:32], ("kaTs", c // 128)),
                           None, hTov, hokeys, [(1024, 32)])
            elif col0 < 3072:
                c1 = col0 - 2048

                def sink(t, ps, bkey, c1=c1):
                    o = ost[ost_i[0] % 3]
                    ok = ("ost", ost_i[0] % 3)
                    ost_i[0] += 1
                    evac(o[:, 0:GC], ps, [bkey], [ok])
                    out_rows(o_va, c1, GC, o, ok, t, 1056)
                    if t == 8:
                        cp("pool", vasv[0:32, c1 // 128:c1 // 128 + 2, 0:128],
                           o[0:32, 0:GC].rearrange("p (h d) -> p h d", h=2), [ok], [("vas", c1)])
                tok_major(wbf, wkeys, GC, hTov, hokeys, NT_OWN, sink)
            else:
                c1 = col0 - 3072

                def zevac(dst, dkey, ps, bkey, n):
                    s_ = sgt2[ost_i[0] % 2]
                    sk = ("sgt2", ost_i[0] % 2)
                    ost_i[0] += 1
                    act(s_[:, 0:n], ps, AF.Silu, [bkey], [sk])
                    ts("dve", dst, s_[:, 0:n], sublnc[:, 0:1], None, ALU.mult, ALU.bypass, [sk, "sublnc"], [dkey])
                feat_major(wbf, wkeys, GC, c1,
                           lambda c, t0, n: (szgaTv[:, c // 128, t0:t0 + n], ("szgaT", c // 128, t0)),
                           None, hTov, hokeys, tok_blocks, evac_fn=zevac)
        P.barrier()
        AR.pop()

        def attention_unit(nmaps, nq, qT_fn, tiles, Pbufs, tmpb, epilogue, ukey, abase=0):
            nsb = (nq + 127) // 128
            T = len(tiles)
            Sps = [(psA, [("ps", "A", 0), ("ps", "A", 1)]), (psB, [("ps", "B", 0), ("ps", "B", 1)])]
            Okeys = [("ps", "O", 0), ("ps", "O", 1), ("ps", "O", 2)]

            def qk(t):
                tl = tiles[t]
                sp, skeys = Sps[t % 2]
                for m in range(nmaps):
                    mm(sp[0:tl["nk"], 512 * m:512 * m + nq], tl["kT"](m), qT_fn(m), True, True,
                       tl["keys"], [skeys[m]])

            def softmax_exp(t):
                tl = tiles[t]
                nk = tl["nk"]
                sp, skeys = Sps[t % 2]
                pb = Pbufs[t % len(Pbufs)]
                pk = ("Pb", t % len(Pbufs))
                pbv = pb.rearrange("p (m q) -> p m q", m=2)
                spv = sp.rearrange("p (m q) -> p m q", m=2)
                if tl["bias"] is not None:
                    tb = tmpb[t % len(tmpb)]
                    tk = ("tmpb", t % len(tmpb))
                    tbv = tb.rearrange("p (m q) -> p m q", m=2)
                    for m in range(nmaps):
                        tt("dve", tbv[0:nk, m, 0:nq], spv[0:nk, m, 0:nq], tl["bias"][0:nk, 0:nq], ALU.add,
                           [skeys[m], tl["biaskey"]], [tk])
                    act(pbv[0:nk, 0:nmaps, 0:nq], tbv[0:nk, 0:nmaps, 0:nq], AF.Exp, [tk, "visA", "visB"], [pk],
                        bias=tl["vis"])
                else:
                    act(pbv[0:nk, 0:nmaps, 0:nq], spv[0:nk, 0:nmaps, 0:nq], AF.Exp,
                        skeys[0:nmaps] + ["visA", "visB"], [pk], bias=tl["vis"])

            def pv(t):
                tl = tiles[t]
                nk = tl["nk"]
                pb = Pbufs[t % len(Pbufs)]
                pk = ("Pb", t % len(Pbufs))
                pbv = pb.rearrange("p (m q) -> p m q", m=2)
                for m in range(nmaps):
                    for sb in range(nsb):
                        a = abase + m * nsb + sb
                        nqs = min(128, nq - 128 * sb)
                        first_in_bank = (t == 0 and a % 3 == 0)
                        mm(psO[0:nqs, 512 * (a // 3) + 130 * (a % 3):512 * (a // 3) + 130 * (a % 3) + 130],
                           pbv[0:nk, m, 128 * sb:128 * sb + nqs], tl["v"], first_in_bank, t == T - 1,
                           [pk] + tl["keys"], [Okeys[a // 3]])

            qk(0)
            for t in range(T):
                if t + 1 < T:
                    qk(t + 1)
                softmax_exp(t)
                if t >= 1:
                    pv(t - 1)
            pv(T - 1)
            epilogue(nsb, Okeys, abase)

        def attention_unit_T(nq, qT_fn, tiles, Pbufs, tmpb, lacc, eA, eB, h, row0):
            T = len(tiles)
            Sps = [(psA, [("ps", "A", 0), ("ps", "A", 1)]), (psB, [("ps", "B", 0), ("ps", "B", 1)])]
            Okeys = [("ps", "O", 0), ("ps", "O", 1)]
            laccv = lacc.rearrange("p (m q) -> p m q", m=2)
            lk = "lacc"
            Lps = [psO[:, 1024:1536], psT.bitcast(F32)]
            Lkeys = [("ps", "O", 2), KT_T]
            dve_started = [False]

            def qk(t):
                tl = tiles[t]
                sp, skeys = Sps[t % 2]
                for m in range(2):
                    mm(sp[0:tl["nk"], 512 * m:512 * m + nq], tl["kT"](m), qT_fn(m), True, True,
                       tl["keys"], [skeys[m]])

            def softmax_exp(t):
                tl = tiles[t]
                nk = tl["nk"]
                sp, skeys = Sps[t % 2]
                pb = Pbufs[t % len(Pbufs)]
                pk = ("Pb", t % len(Pbufs))
                pbv = pb.rearrange("p (m q) -> p m q", m=2)
                spv = sp.rearrange("p (m q) -> p m q", m=2)
                if tl["bias"] is not None:
                    tb = tmpb[t % len(tmpb)]
                    tk = ("tmpb", t % len(tmpb))
                    tbv = tb.rearrange("p (m q) -> p m q", m=2)
                    for m in range(2):
                        tt("dve", tbv[0:nk, m, 0:nq], spv[0:nk, m, 0:nq], tl["bias"][0:nk, 0:nq], ALU.add,
                           [skeys[m], tl["biaskey"]], [tk])
                    act(pbv[0:nk, :, 0:nq], tbv[0:nk, :, 0:nq], AF.Exp, [tk, "visA", "visB"], [pk], bias=tl["vis"])
                else:
                    act(pbv[0:nk, :, 0:nq], spv[0:nk, :, 0:nq], AF.Exp, skeys + ["visA", "visB"], [pk], bias=tl["vis"])

            def pv(t):
                tl = tiles[t]
                nk = tl["nk"]
                pb = Pbufs[t % len(Pbufs)]
                pk = ("Pb", t % len(Pbufs))
                pbv = pb.rearrange("p (m q) -> p m q", m=2)
                on_pe = (t % 3 == 0)
                if on_pe:
                    for m in range(2):
                        mm(Lps[m][:, 0:nq], ones_b[0:nk, :], pbv[0:nk, m, 0:nq], t == 0, False,
                           [pk, "ones_b"], [Lkeys[m]])
                elif not dve_started[0]:
                    dve_started[0] = True
                    cp("dve", laccv[0:nk, :, 0:nq], pbv[0:nk, :, 0:nq], [pk], [lk])
                else:
                    tt("dve", laccv[0:nk, :, 0:nq], laccv[0:nk, :, 0:nq], pbv[0:nk, :, 0:nq], ALU.add, [pk, lk], [lk])
                for m in range(2):
                    mm(psO[:, 512 * m:512 * m + nq], tl["v"][0:nk, 0:128], pbv[0:nk, m, 0:nq], t == 0, t == T - 1,
                       [pk] + tl["keys"], [Okeys[m]])

            qk(0)
            for t in range(T):
                if t + 1 < T:
                    qk(t + 1)
                softmax_exp(t)
                if t >= 1:
                    pv(t - 1)
            pv(T - 1)
            sp, skeys = Sps[T % 2]
            spv = sp.rearrange("p (m q) -> p m q", m=2)
            eAv = eA.rearrange("p (m q) -> p m q", m=2)
            eBv = eB.rearrange("p (m q) -> p m q", m=2)
            for m in range(2):
                mm(Lps[m][:, 0:nq], ones_f, laccv[:, m, 0:nq], False, True, [lk, "ones_f"], [Lkeys[m]])
            for m in range(2):
                act(eAv[:, m, 0:nq], Lps[m][:, 0:nq], AF.Ln, [Lkeys[m]], ["eA"])
            act(eAv[:, :, 0:nq], eAv[:, :, 0:nq], AF.Exp, ["eA"], ["eA"], scale=-1.0)
            tt("dve", eBv[:, 1, 0:nq], psO[:, 512:512 + nq], eAv[:, 1, 0:nq], ALU.mult, [Okeys[1], "eA"], ["eB"])
            tt("dve", eBv[:, 0, 0:nq], psO[:, 0:nq], eAv[:, 0, 0:nq], ALU.mult, [Okeys[0], "eA"], ["eB"])
            stt(eBv[:, 0, 0:nq], eBv[:, 1, 0:nq], neg_lam, eBv[:, 0, 0:nq], ALU.mult, ALU.add, ["eB", "lamc"], ["eB"])
            tt("dve", eBv[:, 1, 0:nq], eBv[:, 0, 0:nq], eBv[:, 0, 0:nq], ALU.mult, ["eB"], ["eB"])
            mm(sp[:, 0:nq], onesm_f, eBv[:, 1, 0:nq], True, True, ["eB", "onesm_f"], [skeys[0]])
            act(eAv[:, 0, 0:nq], sp[:, 0:nq], AF.Ln, [skeys[0]], ["eA"], bias=EPS)
            act(eAv[:, 0, 0:nq], eAv[:, 0, 0:nq], AF.Exp, ["eA"], ["eA"], scale=-0.5)
            tt("dve", eBv[:, 0, 0:nq], eBv[:, 0, 0:nq], eAv[:, 0, 0:nq], ALU.mult, ["eB", "eA"], ["eB"])
            szk = [("szgaT", h, b0) for b0 in (0, 512, 1024) if b0 < row0 + nq and b0 + 512 > row0]
            tt("dve", ozTav[:, h, row0:row0 + nq], eBv[:, 0, 0:nq], szgaTv[:, h, row0:row0 + nq], ALU.mult,
               ["eB"] + szk, [("ozTa", h, row0)])

        def acc_ap(a, n):
            return psO[0:n, 512 * (a // 3) + 130 * (a % 3):512 * (a // 3) + 130 * (a % 3) + 130]

        def make_epilogue_A(h, row0, nq, osb, ep):
            def epilogue(nsb, Okeys):
                for sb in range(nsb):
                    n = min(128, nq - 128 * sb)
                    a0, a1 = sb, nsb + sb
                    k = ("ep", sb % 2)
                    e = ep[sb % 2]
                    o1 = osb[sb % 2]
                    cp("act", o1[0:n, 0:130], acc_ap(a0, n), [Okeys[a0 // 3]], [k])
                    cp("act", o1[0:n, 130:260], acc_ap(a1, n), [Okeys[a1 // 3]], [k])
                    P.op("dve", lambda E, e=e, o1=o1, n=n: E.reciprocal(e[0:n, 0:1], o1[0:n, 128:129]), [k], [k])
                    P.op("dve", lambda E, e=e, o1=o1, n=n: E.reciprocal(e[0:n, 1:2], o1[0:n, 258:259]), [k], [k])
                    tt("dve", e[0:n, 1:2], e[0:n, 1:2], neg_lam[0:n, :], ALU.mult, [k, "lamc"], [k])
                    ts("dve", o1[0:n, 130:258], o1[0:n, 130:258], e[0:n, 1:2], None, ALU.mult, ALU.bypass, [k], [k])
                    stt(o1[0:n, 0:128], o1[0:n, 0:128], e[0:n, 0:1], o1[0:n, 130:258], ALU.mult, ALU.add, [k], [k])
                    P.op("dve", lambda E, e=e, o1=o1, n=n: E.scalar_tensor_tensor(
                        o1[0:n, 130:258], o1[0:n, 0:128], 1.0, o1[0:n, 0:128], ALU.mult, ALU.mult,
                        accum_out=e[0:n, 2:3]), [k], [k])
                    act(e[0:n, 2:3], e[0:n, 2:3], AF.Ln, [k], [k], bias=EPS, scale=1.0 / 128)
                    act(e[0:n, 3:4], e[0:n, 2:3], AF.Exp, [k], [k], scale=-0.5)
                    r = row0 + 128 * sb
                    tix, rr = r // 128, r % 128
                    ob = o1[:, 260:324].bitcast(BF16)
                    stt(ob[0:n, :], o1[0:n, 0:128], e[0:n, 3:4], szgav[rr:rr + n, tix, 128 * h:128 * (h + 1)],
                        ALU.mult, ALU.mult, [k, ("szga", tix, (128 * h) // GC * GC)], [k])
                    tr(psT[:, 0:n], ob[0:n, :], ident_b[0:n, 0:n], [k, "ident_b"], [KT_T])
                    cp("act", ozTav[:, h, r:r + n], psT[:, 0:n], [KT_T], [("ozTa", h, r)])
            return epilogue

        def load_bias_tile(dstb, dkey, hk, hkkey, src, offset, nq, mask_ap, maskkey, nk=128):
            hsrc = bass.AP(tensor=src.tensor, offset=src.offset + offset, ap=[[1, 128], [1, nq]])
            dma("sp", hk[:, 0:nq], hsrc, ["fa_s", "gb_s"], [hkkey], "hk")
            bk, bkey = banks[6]
            mm(bk[:, 0:nq], antij, hk[:, 0:nq], True, True, [hkkey, "antij"], [bkey])
            if mask_ap is None:
                cp("dve", dstb[0:nk, 0:nq], bk[0:nk, 0:nq], [bkey], [dkey])
            else:
                tt("dve", dstb[0:nk, 0:nq], bk[0:nk, 0:nq], mask_ap[0:nk, 0:nq], ALU.add, [bkey, maskkey], [dkey])

        AR.push()
        Pbufs = [AR.bf16(1024) for _ in range(4)]
        tmpb = [AR.f32(1024) for _ in range(2)]
        lacc = AR.f32(1024)
        eA = AR.f32(1024)
        eB = AR.f32(1024)
        hkb = AR.f32(512)
        maskA = AR.f32(5 * 512)
        maskAv = maskA.rearrange("p (s q) -> p s q", s=5)
        dma("sp", maskAv, c_maskA.rearrange("s p q -> p s q"), (), ["maskA"], "c0")
        AR.push()
        biasA = [AR.f32(5 * 512) for _ in range(2)]
        KTw = [AR.bf16(64 * 128) for _ in range(2)]
        Vw = [AR.bf16(64 * 130) for _ in range(2)]
        for i in range(2):
            memset("pool", Vw[i].rearrange("p (t d) -> p t d", t=64)[:, :, 128:130], 1.0, [("Vw1", i)])

        def p3_dyn(E):
            if "pid" not in pid_holder:
                pid_holder["pid"] = E.partition_id()
            return pid_holder["pid"]

        u = 0
        for h in range(8):
            bA = biasA[h % 2]
            bAv = bA.rearrange("p (s q) -> p s q", s=5)
            for s in range(5):
                load_bias_tile(bAv[:, s, :], ("biasA", h % 2, s), hkb, "hkb", fa_s[h:h + 1, :], 512 - 128 * s, 512,
                               maskAv[:, s, :], "maskA")
            for qb in range(2):
                sl = u % 2
                ktw = KTw[sl]
                vw = Vw[sl].rearrange("p (t d) -> p t d", t=64)

                dma("pool", ktw, KTs[h, :, 512 * qb:512 * qb + 64 * 128], (), [("KTw", sl)], "ktw%d" % sl)
                for half in range(2):
                    dma("pool", vw[:, 32 * half:32 * half + 32, 0:128],
                        Vs[h, :, 4 * qb + 32 * half:4 * qb + 32 * half + 32, :], (), [("Vw", sl, half)],
                        "vw%d" % sl)
                tiles = []
                for s in range(64):
                    tiles.append(dict(
                        kT=(lambda m, s=s, ktw=ktw: ktw[64 * m:64 * m + 64, 128 * s:128 * (s + 1)]),
                        v=vw[:, s, :], nk=128,
                        bias=(bAv[:, s, :] if s < 5 else None), biaskey=("biasA", h % 2, s),
                        vis=visA[:, 64 * qb + s:64 * qb + s + 1],
                        keys=[("KTw", sl), ("Vw", sl, s // 32), ("Vw1", sl)]))
                r0 = 512 * qb
                ukey = ("qaT", h, r0)
                attention_unit_T(512, (lambda m, h=h, r0=r0: qaTv[64 * m:64 * m + 64, h, r0:r0 + 512]),
                                 tiles, Pbufs, tmpb, lacc, eA, eB, h, r0)
                u += 1
        P.barrier()
        AR.pop()

        AR.push()
        KTc = AR.bf16(8 * PAST)
        Vc = AR.bf16(16 * 8 * 130)
        cst = [AR.f32(1024) for _ in range(2)]
        KTcv = KTc.rearrange("p (h n) -> p h n", h=8)
        Vcv = Vc.rearrange("p (t h d) -> p t h d", t=16, h=8)
        memset("pool", Vcv[:, :, :, 128:130], 1.0, ["Vc1"])
        psTf = [b for b in banks]
        for t in range(16):
            sl = t % 2
            dma("sp", cst[sl], cka[128 * t:128 * (t + 1), :], (), [("cst", sl)], "cst%d" % sl)
            for hh in range(2):
                bk, bkey = next_bank()
                for j in range(4):
                    h = 4 * hh + j
                    tr(bk[:, 128 * j:128 * (j + 1)], cst[sl][:, 128 * h:128 * (h + 1)], ident_f, [("cst", sl), "ident_f"],
                       [bkey])
                evac(KTcv[:, 4 * hh:4 * hh + 4, 128 * t:128 * (t + 1)], bk.rearrange("p (h n) -> p h n", h=4),
                     [bkey], [("KTc", t)])
        for t in range(16):
            sl = t % 2
            dma("sp", cst[sl], cva[128 * t:128 * (t + 1), :], (), [("cst", sl)], "cst%d" % sl)
            cp("pool", Vcv[:, t, :, 0:128], cst[sl].rearrange("p (h d) -> p h d", h=8), [("cst", sl)], [("Vc", t)])
        bsAll = AR.f32(2 * 8 * 32)
        bsAllv = bsAll.rearrange("p (j h q) -> p j h q", j=2, h=8)
        for h in range(8):
            load_bias_tile(bsAllv[:, 0, h, :], ("bsAll", 0), hkb, "hkb", fa_s[h:h + 1, :], 512, 32, None, None)
            load_bias_tile(bsAllv[:, 1, h, :], ("bsAll", 1), hkb, "hkb", fa_s[h:h + 1, :], 384, 32, None, None, nk=32)
        SpsS = [(psA, [("ps", "A", 0), ("ps", "A", 1)]), (psB, [("ps", "B", 0), ("ps", "B", 1)])]
        oTk, Lk, Mk = ("ps", "O", 0), ("ps", "O", 1), ("ps", "O", 2)
        TS = 17

        def sa_nk(t):
            return 128 if t < 16 else 32

        def sa_qk(t):
            sp, skeys = SpsS[t % 2]
            nk = sa_nk(t)
            for m in range(2):
                for h in range(8):
                    kT = KTcv[64 * m:64 * m + 64, h, 128 * t:128 * (t + 1)] if t < 16 else kaTsv[64 * m:64 * m + 64, h, 0:32]
                    mm(sp[0:nk, 512 * m + 32 * h:512 * m + 32 * h + 32], kT, qaTv[64 * m:64 * m + 64, h, 1024:1056],
                       True, True, [("KTc", t), ("kaTs", h)], [skeys[m]])

        def sa_exp(t):
            sp, skeys = SpsS[t % 2]
            nk = sa_nk(t)
            spv = sp.rearrange("p (m c) -> p m c", m=2)[:, :, 0:256]
            pb = Pbufs[t % len(Pbufs)][:, 0:512]
            pbv = pb.rearrange("p (m c) -> p m c", m=2)
            pk = ("Pb", t % len(Pbufs))
            if t >= 15:
                tb = tmpb[t % 2][:, 0:512]
                tk = ("tmpb", t % 2)
                tbv = tb.rearrange("p (m c) -> p m c", m=2)
                for m in range(2):
                    tt("dve", tbv[0:nk, m, :], spv[0:nk, m, :],
                       bsAllv[0:nk, t - 15, :, :].rearrange("p h q -> p (h q)"), ALU.add,
                       [skeys[m], ("bsAll", t - 15)], [tk])
                act(pbv[0:nk, :, :], tbv[0:nk, :, :], AF.Exp, [tk, "visB"], [pk], bias=visB[0:nk, 15:16])
            else:
                act(pbv[0:nk, :, :], spv[0:nk, :, :], AF.Exp, skeys + ["visB"], [pk], bias=visB[0:nk, 15:16])

        def sa_pv(t):
            nk = sa_nk(t)
            pb = Pbufs[t % len(Pbufs)][:, 0:512]
            pk = ("Pb", t % len(Pbufs))
            mm(psO[:, 512:1024], ones_b[0:nk, :], pb[0:nk, :], t == 0, t == TS - 1, [pk, "ones_b"], [Lk])
            first = True
            for h in range(8):
                v = Vcv[:, t, h, 0:128] if t < 16 else vasv[0:32, h, 0:128]
                for m in range(2):
                    c0 = 256 * m + 32 * h
                    mm(psO[:, c0:c0 + 32], v, pb[0:nk, c0:c0 + 32], t == 0 and first, t == TS - 1,
                       [pk, ("Vc", t), "Vc1", ("vas", (128 * h) // GC * GC), "vas1"], [oTk])
                    first = False

        sa_qk(0)
        for t in range(TS):
            if t + 1 < TS:
                sa_qk(t + 1)
            sa_exp(t)
            if t >= 1:
                sa_pv(t - 1)
        sa_pv(TS - 1)
        r_ = eA[:, 0:512]
        t_ = eB[:, 0:512]
        o_ = eA[:, 512:768]
        sq_ = eA[:, 768:1024]
        rn_ = eB[:, 512:768]
        act(r_, psO[:, 512:1024], AF.Ln, [Lk], ["eA"])
        act(r_, r_, AF.Exp, ["eA"], ["eA"], scale=-1.0)
        tt("dve", t_, psO[:, 0:512], r_, ALU.mult, [oTk, "eA"], ["eB"])
        stt(o_, t_[:, 256:512], neg_lam, t_[:, 0:256], ALU.mult, ALU.add, ["eB", "lamc"], ["eA"])
        tt("dve", sq_, o_, o_, ALU.mult, ["eA"], ["eA"])
        mm(psO[:, 1024:1280], onesm_f, sq_, True, True, ["eA", "onesm_f"], [Mk])
        act(rn_, psO[:, 1024:1280], AF.Ln, [Mk], ["eB"], bias=EPS)
        act(rn_, rn_, AF.Exp, ["eB"], ["eB"], scale=-0.5)
        tt("dve", o_, o_, rn_, ALU.mult, ["eA", "eB"], ["eA"])
        tt("dve", ozTav[:, :, 1024:1056], o_.rearrange("p (h q) -> p h q", h=8), szgaTv[:, :, 1024:1056], ALU.mult,
           ["eA"] + [("szgaT", h, 1024) for h in range(8)], [("ozTa", "s")])
        P.barrier()
        AR.pop()
        AR.pop()
        AR.pop()

        AR.push()
        qbT = AR.bf16(8 * OWN)
        szb = AR.bf16(NT_OWN * 1024)
        kbT = AR.bf16(8 * 1664)
        vbb = AR.bf16(13 * 8 * 130)
        qbTv = qbT.rearrange("p (h t) -> p h t", h=8)
        szbv = szb.rearrange("p (t n) -> p t n", t=NT_OWN)
        kbTv = kbT.rearrange("p (h t) -> p h t", h=8)
        vbv = vbb.rearrange("p (t h d) -> p t h d", t=13, h=8)
        memset("pool", vbv[:, :, :, 128:130], 1.0, ["vb1"])

        AR.push()
        hTo, hTh = own_prep(AR, True)
        hTov = hTo.rearrange("p (c t) -> p c t", c=16)
        hThv = hTh.rearrange("p (c t) -> p c t", c=16)
        hokeys = [("hTo", t) for t in range(NT_OWN)]
        hhkeys = [("hTh", t) for t in range(4)]
        GC = 256
        wstf = [None, None]
        wbfs = [AR.bf16(16 * GC) for _ in range(2)]
        ost = [AR.f32(GC) for _ in range(3)]
        sgt = [AR.f32(GC) for _ in range(2)]
        qscale = float(128 ** -0.5)
        for g in range(4096 // GC):
            col0 = 4096 + g * GC
            c1 = (g * GC) % 1024
            wbf, wkeys = wgroup(col0)
            if g * GC < 1024:
                feat_major(wbf, wkeys, GC, c1,
                           lambda c, t0, n: (qbTv[:, c // 128, t0:t0 + n], ("qbT", c // 128, t0)),
                           qscale, hTov, hokeys, tok_blocks)
            elif g * GC < 2048:
                feat_major(wbf, wkeys, GC, c1,
                           lambda c, t0, n: (kbTv[:, c // 128, t0:t0 + n], ("kbT", c // 128, t0)),
                           None, hThv, hhkeys, [(0, 512)])
                feat_major(wbf, wkeys, GC, c1,
                           lambda c, t0, n: (kbTv[:, c // 128, 512 + t0:512 + t0 + n], ("kbT", c // 128, 512 + t0)),
                           None, hTov, hokeys, tok_blocks)

                def sink(t, ps, bkey, c1=c1):
                    if t < 4:
                        return
                    o = ost[ost_i[0] % 3]
                    ok = ("ost", ost_i[0] % 3)
                    ost_i[0] += 1
                    evac(o[:, 0:GC], ps, [bkey], [ok])
                    if t < 8:
                        dma("sp", o_kbp[128 * (t - 4):128 * (t - 3), c1:c1 + GC], o[:, 0:GC], [ok],
                            [("okbp", t, c1)], "outst%d" % ok[1])
                    else:
                        dma("sp", o_kbs[480:512, c1:c1 + GC], o[0:32, 0:GC], [ok], [("okbs", c1)], "outst%d" % ok[1])
                tok_major(wbf, wkeys, GC, hTov, hokeys, NT_OWN, sink)
            elif g * GC < 3072:
                def sinkh(t, ps, bkey, c1=c1):
                    evac(vbv[:, t, c1 // 128:c1 // 128 + GC // 128, 0:128], ps.rearrange("p (h d) -> p h d", h=GC // 128),
                         [bkey], [("vb", t, c1)])
                tok_major(wbf, wkeys, GC, hThv, hhkeys, 4, sinkh)

                def sink(t, ps, bkey, c1=c1):
                    o = ost[ost_i[0] % 3]
                    ok = ("ost", ost_i[0] % 3)
                    ost_i[0] += 1
                    evac(o[:, 0:GC], ps, [bkey], [ok])
                    cp("pool", vbv[:, 4 + t, c1 // 128:c1 // 128 + GC // 128, 0:128],
                       o[:, 0:GC].rearrange("p (h d) -> p h d", h=GC // 128), [ok], [("vb", 4 + t, c1)])
                    if 4 <= t < 8:
                        dma("sp", o_vbp[128 * (t - 4):128 * (t - 3), c1:c1 + GC], o[:, 0:GC], [ok],
                            [("ovbp", t, c1)], "outst%d" % ok[1])
                    elif t == 8:
                        dma("sp", o_vbs[480:512, c1:c1 + GC], o[0:32, 0:GC], [ok], [("ovbs", c1)], "outst%d" % ok[1])
                tok_major(wbf, wkeys, GC, hTov, hokeys, NT_OWN, sink)
            else:
                def sink(t, ps, bkey, c1=c1):
                    silu_to(szbv[:, t, c1:c1 + GC], ps, bkey, ("szb", t, c1))
                tok_major(wbf, wkeys, GC, hTov, hokeys, NT_OWN, sink)
        dma("pool", o_kbs[0:480, :], ckb[32:512, :], (), [("okbs_c",)], "outc")
        dma("pool", o_vbs[0:480, :], cvb[32:512, :], (), [("ovbs_c",)], "outc")
        P.barrier()
        AR.pop()

        ozTb = AR.top_bf16(8 * OWN)
        ozTbv = ozTb.rearrange("p (h t) -> p h t", h=8)
        AR.push()
        Pbufs = [AR.bf16(1024) for _ in range(4)]
        tmpb = [AR.f32(1024) for _ in range(2)]
        osb = [AR.f32(324) for _ in range(2)]
        ep = [AR.f32(4) for _ in range(2)]
        hkb = AR.f32(512)
        maskB = AR.f32(5 * 128)
        maskBv = maskB.rearrange("p (s q) -> p s q", s=5)
        dma("sp", maskBv, c_maskB.rearrange("s p q -> p s q"), (), ["maskB"], "c0")
        bsB = [AR.f32(5 * 32) for _ in range(2)]
        KbTc = AR.bf16(8 * 512)
        Vbc = AR.bf16(4 * 8 * 130)
        AR.push()
        cst = [AR.f32(1024) for _ in range(2)]
        KbTcv = KbTc.rearrange("p (h n) -> p h n", h=8)
        Vbcv = Vbc.rearrange("p (t h d) -> p t h d", t=4, h=8)
        memset("pool", Vbcv[:, :, :, 128:130], 1.0, ["Vbc1"])
        for t in range(4):
            sl = t % 2
            dma("sp", cst[sl], ckb[128 * t:128 * (t + 1), :], (), [("cst", sl)], "cst%d" % sl)
            for hh in range(2):
                bk, bkey = next_bank()
                for j in range(4):
                    h = 4 * hh + j
                    tr(bk[:, 128 * j:128 * (j + 1)], cst[sl][:, 128 * h:128 * (h + 1)], ident_f, [("cst", sl), "ident_f"],
                       [bkey])
                evac(KbTcv[:, 4 * hh:4 * hh + 4, 128 * t:128 * (t + 1)], bk.rearrange("p (h n) -> p h n", h=4),
                     [bkey], [("KbTc", t)])
        for t in range(4):
            sl = t % 2
            dma("sp", cst[sl], cvb[128 * t:128 * (t + 1), :], (), [("cst", sl)], "cst%d" % sl)
            cp("pool", Vbcv[:, t, :, 0:128], cst[sl].rearrange("p (h d) -> p h d", h=8), [("cst", sl)], [("Vbc", t)])
        P.barrier()
        AR.pop()

        def make_epilogue_B(h, row0, nq):
            def epilogue(nsb, Okeys, abase):
                for sb in range(nsb):
                    n = min(128, nq - 128 * sb)
                    slot = (abase // 3 + sb) % 2
                    k = ("ep", slot)
                    e = ep[slot]
                    o1 = osb[slot]
                    cp("act", o1[0:n, 0:130], acc_ap(abase + sb, n), [Okeys[(abase + sb) // 3]], [k])
                    P.op("dve", lambda E, e=e, o1=o1, n=n: E.reciprocal(e[0:n, 0:1], o1[0:n, 128:129]), [k], [k])
                    r = row0 + 128 * sb
                    tix, rr = r // 128, r % 128
                    ob = o1[:, 260:324].bitcast(BF16)
                    stt(ob[0:n, :], o1[0:n, 0:128], e[0:n, 0:1], szbv[rr:rr + n, tix, 128 * h:128 * (h + 1)],
                        ALU.mult, ALU.mult, [k, ("szb", tix, (128 * h) // GC * GC)], [k])
                    tr(psT[:, 0:n], ob[0:n, :], ident_b[0:n, 0:n], [k, "ident_b"], [KT_T])
                    cp("act", ozTbv[:, h, r:r + n], psT[:, 0:n], [KT_T], [("ozTb", h, r)])
            return epilogue

        def kb_keys(h, c0, n):
            ks = []
            for (a, b_) in ((0, 512), (512, 1024), (1024, 1536), (1536, 1664)):
                if c0 < b_ and c0 + n > a:
                    ks.append(("kbT", h, a))
            return ks

        biasBall = AR.f32(5 * 8 * 128)
        bBv = biasBall.rearrange("p (s h q) -> p s h q", s=5, h=8)
        o1all = AR.f32(8 * 130)
        oball = AR.bf16(8 * 128)
        rall = AR.f32(8)
        o1v = o1all.rearrange("p (h d) -> p h d", h=8)
        obv = oball.rearrange("p (h d) -> p h d", h=8)
        for h in range(8):
            for s_ in range(5):
                load_bias_tile(bBv[:, s_, h, :], ("biasBall", s_), hkb, "hkb", gb_s[h:h + 1, :], 512 - 128 * s_, 128,
                               maskBv[:, s_, :], "maskB")
        SpsB = [(psA, [("ps", "A", 0), ("ps", "A", 1)]), (psB, [("ps", "B", 0), ("ps", "B", 1)])]
        OkB = [("ps", "O", 0), ("ps", "O", 1), ("ps", "O", 2)]
        steps = [(p, s_) for p in range(8) for s_ in range(5)]

        def b_qk(i):
            p, s_ = steps[i]
            w = p + s_
            sp, skeys = SpsB[i % 2]
            for h in range(8):
                mm(sp[:, 128 * h:128 * (h + 1)], kbTv[:, h, 128 * w:128 * (w + 1)], qbTv[:, h, 128 * p:128 * (p + 1)],
                   True, True, [], [skeys[h // 4]])

        def b_exp(i):
            p, s_ = steps[i]
            w = p + s_
            sp, skeys = SpsB[i % 2]
            tb = tmpb[i % 2]
            tk = ("tmpb", i % 2)
            pb = Pbufs[i % 4]
            pk = ("Pb", i % 4)
            tt("dve", tb, sp[:, :], bBv[:, s_, :, :].rearrange("p h q -> p (h q)"), ALU.add,
               skeys + [("biasBall", s_)], [tk])
            act(pb, tb, AF.Exp, [tk, "visB"], [pk], bias=visB[:, w:w + 1])

        def b_pv(i):
            p, s_ = steps[i]
            w = p + s_
            pb = Pbufs[i % 4]
            pk = ("Pb", i % 4)
            for h in range(8):
                mm(acc_ap(h, 128), pb[:, 128 * h:128 * (h + 1)], vbv[:, w, h, :], s_ == 0 and h % 3 == 0, s_ == 4,
                   [pk], [OkB[h // 3]])
            if s_ == 4:
                b_epilogue(p)

        def b_epilogue(p):
            r0 = 128 * p
            k = "o1all"
            cp("act", o1all[:, 0:390], psO[:, 0:390], [OkB[0]], [k])
            cp("act", o1all[:, 390:780], psO[:, 512:902], [OkB[1]], [k])
            cp("act", o1all[:, 780:1040], psO[:, 1024:1284], [OkB[2]], [k])
            P.op("dve", lambda E: E.reciprocal(rall[:, 0:8], o1v[:, :, 128]), [k], ["rall"])
            for h in range(8):
                stt(obv[:, h, :], o1v[:, h, 0:128], rall[:, h:h + 1], szbv[:, p, 128 * h:128 * (h + 1)],
                    ALU.mult, ALU.mult, [k, "rall"], ["oball"])
            for h in range(8):
                tr(psT[:, 128 * h:128 * (h + 1)], obv[:, h, :], ident_b, ["oball", "ident_b"], [KT_T])
            cp("act", ozTbv[:, :, r0:r0 + 128], psT.rearrange("p (h t) -> p h t", h=8), [KT_T], [("ozTb", "p", p)])

        b_qk(0)
        for i in range(len(steps)):
            if i + 1 < len(steps):
                b_qk(i + 1)
            b_exp(i)
            if i >= 1:
                b_pv(i - 1)
        b_pv(len(steps) - 1)

        for h in range(8):
            bS = bsB[h % 2].rearrange("p (s q) -> p s q", s=5)
            for s in range(4):
                load_bias_tile(bS[:, s, :], ("bsB", h % 2, s), hkb, "hkb", gb_s[h:h + 1, :], 512 - 128 * s, 32,
                               None, None)
            load_bias_tile(bS[:, 4, :], ("bsB", h % 2, 4), hkb, "hkb", gb_s[h:h + 1, :], 0, 32, None, None, nk=32)
            tiles = []
            for t in range(4):
                tiles.append(dict(
                    kT=(lambda m, t=t, h=h: KbTcv[:, h, 128 * t:128 * (t + 1)]),
                    v=Vbcv[:, t, h, :], nk=128, bias=bS[:, t, :], biaskey=("bsB", h % 2, t),
                    vis=visB[:, 15:16], keys=[("KbTc", t), ("Vbc", t), "Vbc1"]))
            tiles.append(dict(
                kT=(lambda m, h=h: kbTv[:, h, 512 + 1024:512 + 1056]),
                v=vbv[0:32, 12, h, :], nk=32, bias=bS[:, 4, :], biaskey=("bsB", h % 2, 4),
                vis=visB[0:32, 15:16], keys=[("kbT", h, 1536), ("vb", 12, (128 * h) // GC * GC), "vb1"]))
            attention_unit(1, 32, (lambda m, h=h: qbTv[:, h, 1024:1056]),
                           tiles, Pbufs, tmpb, make_epilogue_B(h, 1024, 32), ("qbT", h, 1024))
        P.barrier()
        AR.pop()
        AR.pop()

        AR.push()
        hTo, _ = own_prep(AR, False)
        hTov = hTo.rearrange("p (c t) -> p c t", c=16)
        hokeys = [("hTo", t) for t in range(NT_OWN)]
        dma("sp", gvec, post.partition_broadcast(128), (), ["gvec"], "c0")
        mT = AR.bf16(16 * OWN)
        mTv = mT.rearrange("p (c t) -> p c t", c=16)
        AR.push()
        wgst = [AR.f32(16 * 128) for _ in range(2)]
        wgbf = [AR.bf16(16 * 128) for _ in range(4)]
        wost = [AR.f32(8 * 128) for _ in range(2)]
        wobf = [AR.bf16(8 * 128) for _ in range(4)]
        sga = [AR.f32(512) for _ in range(2)]
        sgb = [AR.f32(512) for _ in range(2)]
        ya = [AR.f32(512) for _ in range(2)]
        k5 = [0]
        for cc in range(16):
            wk = {}
            for gi_, (c0, nm) in enumerate(((8192, "ga"), (10240, "gb"))):
                sl = (2 * cc + gi_) % 2
                sl4 = (2 * cc + gi_) % 4
                wk[nm] = (wgbf[sl4], load_wgroup(w_in[:, c0 + 128 * cc:c0 + 128 * (cc + 1)], 128, wgst[sl], wgbf[sl4],
                                                 ("wg5", sl4), "wg5%d" % sl4))
            for gi_, (wsrc, nm) in enumerate(((w_oa, "oa"), (w_ob, "ob"))):
                sl = (2 * cc + gi_) % 2
                sl4 = (2 * cc + gi_) % 4
                wk[nm] = (wobf[sl4], load_wgroup(wsrc[:, 128 * cc:128 * (cc + 1)], 128, wost[sl], wobf[sl4],
                                                 ("wo5", sl4), "wo5%d" % sl4, nk=8))
            for (t0, n) in tok_blocks:
                i2 = k5[0] % 2
                k5[0] += 1
                sig = {}
                for nm, sbuf_ in (("ga", sga[i2]), ("gb", sgb[i2])):
                    wbf, wkeys = wk[nm]
                    wv = wbf.rearrange("p (c n) -> p c n", c=16)
                    bk, bkey = next_bank()
                    for dc in range(16):
                        mm(bk[:, 0:n], wv[:, dc, :], hTov[:, dc, t0:t0 + n], dc == 0, dc == 15, wkeys + hokeys, [bkey])
                    sk = ("sg", nm, i2)
                    act(sbuf_[:, 0:n], bk[:, 0:n], AF.Sigmoid, [bkey], [sk])
                    sig[nm] = (sbuf_, sk)
                yk = ("ya", i2)
                for nm, ozv, oznm, gnm in (("oa", ozTav, "ozTa", "ga"), ("ob", ozTbv, "ozTb", "gb")):
                    wbf, wkeys = wk[nm]
                    wv = wbf.rearrange("p (c n) -> p c n", c=8)
                    bk, bkey = next_bank()
                    for h in range(8):
                        mm(bk[:, 0:n], wv[:, h, :], ozv[:, h, t0:t0 + n], h == 0, h == 7, wkeys, [bkey])
                    sb_, sk = sig[gnm]
                    if nm == "oa":
                        tt("dve", ya[i2][:, 0:n], bk[:, 0:n], sb_[:, 0:n], ALU.mult, [bkey, sk], [yk])
                    else:
                        tt("dve", sb_[:, 0:n], bk[:, 0:n], sb_[:, 0:n], ALU.mult, [bkey, sk], [sk])
                        tt("dve", mTv[:, cc, t0:t0 + n], sb_[:, 0:n], ya[i2][:, 0:n], ALU.add, [sk, yk], [("mT", cc, t0)])
        P.barrier()
        AR.pop()
        AR.n = ARENA_WORDS
        wout_bf_full = AR.bf16(16 * 2048)
        wost2 = [AR.f32(2048) for _ in range(2)]
        woutv = wout_bf_full.rearrange("p (c n) -> p c n", c=16)
        for dc in range(16):
            sl = dc % 2
            dma("pool", woutv[:, dc, :], w_out[128 * dc:128 * (dc + 1), :], (), [("wout", dc)], "wout")
        wokeys = [("wout", dc) for dc in range(16)]
        stage = hTo.bitcast(F32)
        yrow = [stage[:, 0:2048], stage[:, 2048:4096]]
        xrow = [stage[:, 4096:6144], stage[:, 6144:8192]]
        sq5 = stage[:, 8192:8200]
        junk5 = AR.f32(1024)
        for t in range(NT_OWN):
            i2 = t % 2
            nrow = min(128, 1056 - 128 * t)
            yk = ("yrow", i2)
            xk = ("xrow", i2)
            dma("sp", xrow[i2], xo[128 * t:128 * (t + 1), :], (), [xk], "xr%d" % i2)
            for cg in range(4):
                bk, bkey = next_bank()
                for kc in range(16):
                    mm(bk, mTv[:, kc, 128 * t:128 * (t + 1)], woutv[:, kc, 512 * cg:512 * (cg + 1)], kc == 0, kc == 15,
                       wokeys, [bkey])
                evac(yrow[i2][:, 512 * cg:512 * (cg + 1)], bk, [bkey], [yk])
            sq = sq5[:, 4 * i2:4 * i2 + 2]
            sk = ("sq5", i2)
            for hh in range(2):
                P.op("dve", lambda E, i2=i2, hh=hh: E.scalar_tensor_tensor(
                    junk5, yrow[i2][:, 1024 * hh:1024 * (hh + 1)], 1.0, yrow[i2][:, 1024 * hh:1024 * (hh + 1)],
                    ALU.mult, ALU.mult, accum_out=sq5[:, 4 * i2 + 2 + hh:4 * i2 + 3 + hh]), [yk], ["junk5", sk])
            tt("dve", sq[:, 0:1], sq5[:, 4 * i2 + 2:4 * i2 + 3], sq5[:, 4 * i2 + 3:4 * i2 + 4], ALU.add, [sk], [sk])
            act(sq[:, 0:1], sq[:, 0:1], AF.Ln, [sk], [sk], bias=EPS, scale=1.0 / D)
            act(sq[:, 1:2], sq[:, 0:1], AF.Exp, [sk], [sk], scale=-0.5)
            stt(yrow[i2], yrow[i2], sq[:, 1:2], gvec, ALU.mult, ALU.mult, [yk, sk, "gvec"], [yk])
            tt("pool", yrow[i2], yrow[i2], xrow[i2], ALU.add, [yk, xk], [yk])
            dma("sp", o_y[128 * t:128 * t + nrow, :], yrow[i2][0:nrow, :], [yk], [("oy", t)], "outy")
        P.barrier()
        AR.pop()

        P.barrier()
        sem_names = P.sem_names()
        sem_ctx = [nc.semaphore("s%d" % i) for i in range(len(sem_names))]
        sems = {}
        import contextlib
        with contextlib.ExitStack() as stack:
            for nm, c in zip(sem_names, sem_ctx):
                sems[nm] = stack.enter_context(c)
            block = stack.enter_context(nc.Block())

            def replay(E, eng):
                for waits, fn, tok in P.ops[eng]:
                    for s, v in waits:
                        E.wait_ge(sems[s], v)
                    if fn is None:
                        continue
                    ins = fn(E)
                    ins.then_inc(sems[tok[0]], 16 if tok[0].startswith("dma:") else 1)

            @block.tensor
            def _(E):
                replay(E, "pe")

            @block.scalar
            def _(E):
                replay(E, "act")

            @block.vector
            def _(E):
                replay(E, "dve")

            @block.gpsimd
            def _(E):
                replay(E, "pool")

            @block.sync
            def _(E):
                replay(E, "sp")
    return nc


def _constants():
    c = {}
    c["c_ident"] = np.eye(128, dtype=np.float32)
    c["c_antij"] = np.ascontiguousarray(np.eye(128, dtype=np.float32)[::-1])
    u = np.arange(FA_LEN)
    d = u - 511
    bk = _t5_bucket_np(-d)
    oh = np.zeros((32, FA_LEN), np.float32)
    oh[bk, u] = 1.0
    oh[15, :] -= 1.0
    c["c_oha"] = oh
    v = np.arange(GB_LEN)
    idx = np.clip(127 - v, -128, 128) + 128
    ohb = np.zeros((384, GB_LEN), np.float32)
    ohb[idx, v] = 1.0
    c["c_ohb"] = ohb
    i = np.arange(128)[:, None]
    j = np.arange(512)[None, :]
    mA = np.zeros((5, 128, 512), np.float32)
    for s in range(5):
        krel = (s - 1) * 128 + i
        mA[s] = np.where(np.floor_divide(krel, 64) > (j // 64), NEG, 0.0)
    c["c_maskA"] = mA
    j2 = np.arange(128)[None, :]
    mB = np.zeros((5, 128, 128), np.float32)
    for s in range(5):
        kc = (128 * s + i) // 64 - 8
        qc = j2 // 64
        ok = (kc <= qc) & (kc >= qc - 8)
        mB[s] = np.where(ok, 0.0, NEG)
    c["c_maskB"] = mB
    return c


def _vis_tables(core):
    visA = np.zeros((128, 128), np.float32)
    for qb in range(2):
        T0 = 8 * core + 4 * qb
        for s in range(64):
            tile = T0 - 1 + s
            if tile >= 64:
                visible = True
            else:
                visible = (0 <= tile <= T0 + 3)
            visA[:, 64 * qb + s] = 0.0 if visible else NEG
    visB = np.zeros((128, 16), np.float32)
    if core == 0:
        visB[:, 0:4] = NEG
    return visA, visB


_NC_CACHE = {}


def kernel(x_prompt, x_sample, cache_k_a, cache_v_a, cache_k_b, cache_v_b, t5_bias, pre_norm, post_norm,
           w_in, lambda_q1, lambda_k1, lambda_q2, lambda_k2, subln_a, rel_bias_b, w_o_a, w_o_b, w_out):
    f = lambda a: np.ascontiguousarray(np.asarray(a, dtype=np.float32))
    xp = f(x_prompt)[0]
    xs = f(x_sample)
    consts = _constants()
    relbT = np.zeros((384, 8), np.float32)
    relbT[:257] = f(rel_bias_b)[0].T
    lam4 = np.concatenate([f(lambda_q1)[0], f(lambda_k1)[0], f(lambda_q2)[0], f(lambda_k2)[0]])[None, :]
    shared = {
        "w_in": f(w_in)[0], "w_oa": f(w_o_a)[0], "w_ob": f(w_o_b)[0], "w_out": f(w_out)[0],
        "pre": f(pre_norm), "post": f(post_norm), "subln": f(subln_a), "lam4": np.ascontiguousarray(lam4),
        "t5": f(t5_bias), "relbT": relbT,
    }
    shared.update(consts)
    in_maps = []
    for c in range(NCORES):
        xo = np.zeros((OWN, D), np.float32)
        xo[:ROWS] = xp[ROWS * c:ROWS * (c + 1)]
        xo[ROWS:ROWS + NS] = xs[c]
        xh = np.zeros((512, D), np.float32)
        if c > 0:
            xh[:] = xp[ROWS * c - 512:ROWS * c]
        visA, visB = _vis_tables(c)
        m = dict(shared)
        m.update({
            "xf": np.ascontiguousarray(np.roll(xp, -128 * (8 * c - 1), axis=0)),
            "xo": xo, "xh": xh,
            "cka": f(cache_k_a)[0, c].reshape(PAST, 1024), "cva": f(cache_v_a)[0, c].reshape(PAST, 1024),
            "ckb": f(cache_k_b)[0, c].reshape(512, 1024), "cvb": f(cache_v_b)[0, c].reshape(512, 1024),
            "c_visA": visA, "c_visB": visB,
        })
        in_maps.append(m)
    if "nc" not in _NC_CACHE:
        _NC_CACHE["nc"] = build()
    nc = _NC_CACHE["nc"]
    res = run_bass_kernel_spmd(nc, in_maps, core_ids=list(range(NCORES)))
    R = res.results
    y_prompt = np.concatenate([R[c]["y"][:ROWS] for c in range(NCORES)], 0)[None]
    y_sample = np.stack([R[c]["y"][ROWS:ROWS + NS] for c in range(NCORES)], 0)
    kap = np.concatenate([R[c]["ka"][:ROWS] for c in range(NCORES)], 0).reshape(1, 1, SEQ, 16, 64)
    vap = np.concatenate([R[c]["va"][:ROWS] for c in range(NCORES)], 0).reshape(1, 1, SEQ, 8, 128)
    kbp = R[NCORES - 1]["kbp"].reshape(1, 1, 512, 8, 128)
    vbp = R[NCORES - 1]["vbp"].reshape(1, 1, 512, 8, 128)
    kas = np.stack([R[c]["ka"][ROWS:ROWS + NS] for c in range(NCORES)], 0).reshape(1, NCORES, NS, 16, 64)
    vas = np.stack([R[c]["va"][ROWS:ROWS + NS] for c in range(NCORES)], 0).reshape(1, NCORES, NS, 8, 128)
    kbs = np.stack([R[c]["kbs"] for c in range(NCORES)], 0).reshape(1, NCORES, 512, 8, 128)
    vbs = np.stack([R[c]["vbs"] for c in range(NCORES)], 0).reshape(1, NCORES, 512, 8, 128)
    out = (y_prompt, y_sample, kap, vap, kbp, vbp, kas, vas, kbs, vbs)
    return tuple(np.ascontiguousarray(o.astype(np.float32)) for o in out)
```

```python
import numpy as np
import ml_dtypes
import concourse.bass as bass
import concourse.mybir as mybir
from concourse.bass_utils import run_bass_kernel_spmd

F32 = mybir.dt.float32
BF16 = mybir.dt.bfloat16
AF = mybir.ActivationFunctionType
ALU = mybir.AluOpType

NCORES = 8
D = 2048
SEQ = 8192
ROWS = 1024
NS = 32
OWN = 1152
NT_OWN = 9
PAST = 2048
WIN = 12288
NEG = -30000.0
EPS = 1e-6
LAM_INIT = 0.2
FA_LEN = 1152
GB_LEN = 768
ENGS = ("pe", "act", "dve", "pool", "sp")


class Prog:
    def __init__(self):
        self.ops = {e: [] for e in ENGS}
        self.cnt = {e: 0 for e in ENGS}
        self.last_w = {}
        self.readers = {}
        self.waited = {e: {} for e in ENGS}
        self.dma_cnt = {}

    def _deps(self, eng, reads, writes):
        deps = {}
        def add(tok):
            if tok is None:
                return
            s, v = tok
            if s == eng and eng == "pe":
                return
            if s.startswith("dma:"):
                v = 16 * self.dma_cnt[s[4:]]
            if deps.get(s, 0) < v:
                deps[s] = v
        for k in reads:
            add(self.last_w.get(k))
        for k in writes:
            add(self.last_w.get(k))
            for t in self.readers.get(k, ()):
                add(t)
        waits = []
        for s, v in deps.items():
            if self.waited[eng].get(s, 0) < v:
                self.waited[eng][s] = v
                waits.append((s, v))
        return waits

    def op(self, eng, fn, reads=(), writes=(), tag=None):
        waits = self._deps(eng, reads, writes)
        if tag is not None:
            self.dma_cnt[tag] = self.dma_cnt.get(tag, 0) + 1
            tok = ("dma:" + tag, 16 * self.dma_cnt[tag])
        else:
            self.cnt[eng] += 1
            tok = (eng, self.cnt[eng])
        self.ops[eng].append((waits, fn, tok))
        for k in writes:
            self.last_w[k] = tok
            self.readers[k] = []
        for k in reads:
            self.readers.setdefault(k, []).append(tok)
        return tok

    def barrier(self):
        allt = [(e, self.cnt[e]) for e in ENGS if self.cnt[e] > 0]
        allt += [("dma:" + t, 16 * c) for t, c in self.dma_cnt.items()]
        for e in ENGS:
            waits = []
            for s, v in allt:
                if self.waited[e].get(s, 0) < v:
                    self.waited[e][s] = v
                    waits.append((s, v))
            if waits:
                self.ops[e].append((waits, None, None))
        self.last_w = {}
        self.readers = {}

    def sem_names(self):
        names = [e for e in ENGS if e != "sp"]
        names += ["dma:" + t for t in self.dma_cnt]
        return names


class Arena:
    def __init__(self, ap_f32, nwords):
        self.ap = ap_f32
        self.n = nwords
        self.off = 0
        self.marks = []

    def push(self):
        self.marks.append(self.off)

    def pop(self):
        self.off = self.marks.pop()

    def f32(self, n):
        assert self.off + n <= self.n, ("arena overflow", self.off, n, self.n)
        a = self.ap[:, self.off:self.off + n]
        self.off += n
        return a

    def bf16(self, n):
        w = (n + 1) // 2
        return self.f32(w).bitcast(BF16)

    def top_bf16(self, n):
        w = (n + 1) // 2
        assert self.n - w >= self.off
        self.n -= w
        return self.ap[:, self.n:self.n + w].bitcast(BF16)


def _t5_bucket_np(rel):
    rel = np.asarray(rel, np.int64)
    half = 16
    max_exact = 8
    ret = np.where(rel > 0, half, 0)
    n = np.abs(rel)
    nf = np.maximum(n, 1).astype(np.float32)
    large = max_exact + (np.log(nf / np.float32(max_exact)) / np.float32(np.log(128 / max_exact))
                         * np.float32(half - max_exact)).astype(np.int32)
    large = np.minimum(large, half - 1)
    return ret + np.where(n < max_exact, n, large)


def build():
    nc = bass.Bass("TRN2", target_bir_lowering=False)

    def din(name, shape, dt=F32):
        return nc.dram_tensor(name, list(shape), dt, kind="ExternalInput").ap()

    def dout(name, shape):
        return nc.dram_tensor(name, list(shape), F32, kind="ExternalOutput").ap()

    xf = din("xf", [SEQ, D])
    xo = din("xo", [OWN, D])
    xh = din("xh", [512, D])
    w_in = din("w_in", [D, WIN])
    w_oa = din("w_oa", [1024, D])
    w_ob = din("w_ob", [1024, D])
    w_out = din("w_out", [D, D])
    cka = din("cka", [PAST, 1024])
    cva = din("cva", [PAST, 1024])
    ckb = din("ckb", [512, 1024])
    cvb = din("cvb", [512, 1024])
    pre = din("pre", [1, D])
    post = din("post", [1, D])
    subln = din("subln", [1, 128])
    lam4 = din("lam4", [1, 256])
    t5 = din("t5", [32, 8])
    relbT = din("relbT", [384, 8])
    c_ident = din("c_ident", [128, 128])
    c_antij = din("c_antij", [128, 128])
    c_oha = din("c_oha", [32, FA_LEN])
    c_ohb = din("c_ohb", [384, GB_LEN])
    c_maskA = din("c_maskA", [5, 128, 512])
    c_maskB = din("c_maskB", [5, 128, 128])
    c_visA = din("c_visA", [128, 128])
    c_visB = din("c_visB", [128, 16])

    o_y = dout("y", [1056, D])
    o_ka = dout("ka", [1056, 1024])
    o_va = dout("va", [1056, 1024])
    o_kbp = dout("kbp", [512, 1024])
    o_vbp = dout("vbp", [512, 1024])
    o_kbs = dout("kbs", [512, 1024])
    o_vbs = dout("vbs", [512, 1024])

    KTs = nc.dram_tensor("KTs", [8, 128, 2 * SEQ], BF16).ap()
    Vs = nc.dram_tensor("Vs", [8, 128, 128, 128], BF16).ap()
    fa_s = nc.dram_tensor("fa_s", [8, FA_LEN], F32).ap()
    gb_s = nc.dram_tensor("gb_s", [8, GB_LEN], F32).ap()

    P = Prog()
    pid_holder = {}

    ARENA_WORDS = 53000
    with (
        nc.sbuf_tensor("arena", [128, ARENA_WORDS], F32) as arena_t,
        nc.psum_tensor("psA", [128, 1024], F32) as psA,
        nc.psum_tensor("psB", [128, 1024], F32) as psB,
        nc.psum_tensor("psO", [128, 1536], F32) as psO,
        nc.psum_tensor("psT", [128, 1024], BF16) as psT,
    ):
        AR = Arena(arena_t[:, :], ARENA_WORDS)
        banks = []
        for nm, t, nb in (("A", psA, 2), ("B", psB, 2), ("O", psO, 3)):
            for i in range(nb):
                banks.append((t[:, 512 * i:512 * (i + 1)], ("ps", nm, i)))
        bank_rr = [0]

        def next_bank():
            b = banks[bank_rr[0] % len(banks)]
            bank_rr[0] += 1
            return b
        KT_T = ("ps", "T", 0)

        def dma(q, out, in_, reads, writes, tag):
            P.op(q, lambda E: E.dma_start(out=out, in_=in_), reads, writes, tag=tag)

        def mm(out, lhsT, rhs, start, stop, reads, writes):
            P.op("pe", lambda E: E.matmul(out, lhsT, rhs, start=start, stop=stop,
                                          skip_group_check=True), reads, writes)

        def tr(out, in_, ident, reads, writes):
            P.op("pe", lambda E: E.transpose(out, in_, ident), reads, writes)

        def act(out, in_, func, reads, writes, bias=None, scale=None, accum=None):
            kw = {}
            if bias is not None:
                kw["bias"] = bias
            if scale is not None:
                kw["scale"] = scale
            if accum is not None:
                kw["accum_out"] = accum
            P.op("act", lambda E: E.activation(out, in_, func, **kw), reads, writes)

        def ts(eng, out, in0, s1, s2, op0, op1, reads, writes, accum=None):
            if accum is None:
                P.op(eng, lambda E: E.tensor_scalar(out, in0, s1, s2, op0, op1), reads, writes)
            else:
                P.op(eng, lambda E: E.tensor_scalar(out, in0, s1, s2, op0, op1, accum_out=accum),
                     reads, writes)

        def tt(eng, out, in0, in1, op, reads, writes):
            P.op(eng, lambda E: E.tensor_tensor(out, in0, in1, op), reads, writes)

        def stt(out, in0, scalar, in1, op0, op1, reads, writes):
            P.op("dve", lambda E: E.scalar_tensor_tensor(out, in0, scalar, in1, op0, op1), reads, writes)

        def cp(eng, out, in_, reads, writes):
            if eng == "act":
                P.op("act", lambda E: E.copy(out, in_), reads, writes)
            else:
                P.op(eng, lambda E: E.tensor_copy(out, in_), reads, writes)

        def memset(eng, ap, val, writes):
            P.op(eng, lambda E: E.memset(ap, val), (), writes)

        evac_rr = [0]

        def evac(out, in_, reads, writes):
            e = ("act", "dve")[evac_rr[0] % 2]
            evac_rr[0] += 1
            cp(e, out, in_, reads, writes)

        ident_f = AR.f32(128)
        antij = AR.f32(128)
        ident_b = AR.bf16(128)
        gvec = AR.f32(2048)
        sublnb = AR.f32(128)
        lamt = AR.f32(256)
        lamtmp = AR.f32(64)
        lamc = AR.f32(8)
        visA = AR.f32(128)
        visB = AR.f32(16)
        sublnc = AR.f32(2)
        ones_f = AR.f32(128)
        onesm_f = AR.f32(128)
        ones_b = AR.bf16(128)

        dma("sp", ident_f, c_ident, (), ["ident_f"], "c0")
        dma("sp", antij, c_antij, (), ["antij"], "c0")
        dma("sp", visA, c_visA, (), ["visA"], "c0")
        dma("sp", visB, c_visB, (), ["visB"], "c0")
        dma("sp", gvec, pre.partition_broadcast(128), (), ["gvec"], "c0")
        dma("sp", sublnb, subln.partition_broadcast(128), (), ["sublnb"], "c0")
        dma("sp", lamt, lam4.partition_broadcast(128), (), ["lamt"], "c0")
        dma("sp", sublnc[:, 0:1], subln.rearrange("o d -> d o"), (), ["sublnc"], "c0")
        AR.push()
        t5_sb = AR.f32(8)
        oha_sb = AR.f32(FA_LEN)
        rb_sb = AR.f32(24)
        ohb_sb = AR.f32(3 * GB_LEN)
        vec_sb = AR.f32(FA_LEN)
        dma("sp", t5_sb[0:32, :], t5, (), ["t5_sb"], "c0")
        dma("sp", oha_sb[0:32, :], c_oha, (), ["oha_sb"], "c0")
        dma("sp", rb_sb.rearrange("p (c h) -> p c h", c=3), relbT.rearrange("(c p) h -> p c h", p=128),
            (), ["rb_sb"], "c0")
        dma("sp", ohb_sb.rearrange("p (c n) -> p c n", c=3), c_ohb.rearrange("(c p) n -> p c n", p=128),
            (), ["ohb_sb"], "c0")
        P.barrier()
        cp("dve", ident_b, ident_f, ["ident_f"], ["ident_b"])
        ts("dve", sublnb, sublnb, 1.0 - LAM_INIT, None, ALU.mult, ALU.bypass, ["sublnb"], ["sublnb"])
        ts("dve", sublnc[:, 0:1], sublnc[:, 0:1], 1.0 - LAM_INIT, None, ALU.mult, ALU.bypass, ["sublnc"], ["sublnc"])
        memset("dve", ones_f, 1.0, ["ones_f"])
        memset("dve", onesm_f, 1.0 / 128, ["onesm_f"])
        memset("dve", ones_b, 1.0, ["ones_b"])
        for i in range(2):
            P.op("dve", (lambda i: lambda E: E.scalar_tensor_tensor(
                lamtmp, lamt[:, 128 * i:128 * i + 64], 1.0, lamt[:, 128 * i + 64:128 * i + 128],
                ALU.mult, ALU.mult, accum_out=lamc[:, i:i + 1]))(i),
                ["lamt"], ["lamtmp", "lamc"])
        act(lamc[:, 2:4], lamc[:, 0:2], AF.Exp, ["lamc"], ["lamc"])
        tt("dve", lamc[:, 4:5], lamc[:, 2:3], lamc[:, 3:4], ALU.subtract, ["lamc"], ["lamc"])
        ts("dve", lamc[:, 5:6], lamc[:, 4:5], LAM_INIT, -1.0, ALU.add, ALU.mult, ["lamc"], ["lamc"])
        neg_lam = lamc[:, 5:6]

        for j in range(3):
            bk, bkey = next_bank()
            mm(bk[0:8, 0:384], t5_sb[0:32, 0:8], oha_sb[0:32, 384 * j:384 * (j + 1)], True, True,
               ["t5_sb", "oha_sb"], [bkey])
            cp("dve", vec_sb[0:8, 384 * j:384 * (j + 1)], bk[0:8, 0:384], [bkey], ["vec_sb"])
        dma("sp", fa_s, vec_sb[0:8, 0:FA_LEN], ["vec_sb"], ["fa_s"], "c1")
        for j in range(2):
            bk, bkey = next_bank()
            for c in range(3):
                mm(bk[0:8, 0:384], rb_sb[:, 8 * c:8 * c + 8], ohb_sb[:, GB_LEN * c + 384 * j:GB_LEN * c + 384 * (j + 1)],
                   c == 0, c == 2, ["rb_sb", "ohb_sb"], [bkey])
            cp("dve", vec_sb[0:8, 384 * j:384 * (j + 1)], bk[0:8, 0:384], [bkey, "fa_s"], ["vec_sb"])
        dma("sp", gb_s, vec_sb[0:8, 0:GB_LEN], ["vec_sb"], ["gb_s"], "c1")
        P.barrier()
        AR.pop()

        def norm_jobs(src, ntiles, hT, hT_key, tagbase, xs, hb, ssq, junk):
            hTv = hT.rearrange("p (c t) -> p c t", c=16)
            jobs = []

            def mk(t, part):
                def job():
                    sl = t % len(xs)
                    xk = ("xs", tagbase, sl)
                    hk = ("hb", tagbase, t % 2)
                    sk = ("ssq", tagbase, t % 2)
                    sq = ssq[:, 2 * (t % 2):2 * (t % 2) + 2]
                    if part == "norm":
                        dma("sp", xs[sl], src[128 * t:128 * (t + 1), :], (), [xk], "x%s%d" % (tagbase, sl))
                        P.op("dve", lambda E: E.scalar_tensor_tensor(
                            junk, xs[sl], 1.0, xs[sl], ALU.mult, ALU.mult, accum_out=sq[:, 0:1]),
                            [xk], [("junk", tagbase), sk])
                        act(sq[:, 0:1], sq[:, 0:1], AF.Ln, [sk], [sk], bias=EPS, scale=1.0 / D)
                        act(sq[:, 1:2], sq[:, 0:1], AF.Exp, [sk], [sk], scale=-0.5)
                        stt(hb[t % 2], xs[sl], sq[:, 1:2], gvec, ALU.mult, ALU.mult, [xk, sk, "gvec"], [hk])
                        return
                    half = part
                    for j in range(8):
                        dc = 8 * half + j
                        tr(psT[:, 128 * j:128 * (j + 1)], hb[t % 2][:, 128 * dc:128 * (dc + 1)], ident_b,
                           [hk, "ident_b"], [KT_T])
                    evac(hTv[:, 8 * half:8 * half + 8, 128 * t:128 * (t + 1)],
                         psT.rearrange("p (c t) -> p c t", c=8), [KT_T], [(hT_key, t)])
                return job
            for t in range(ntiles):
                if t == 0:
                    jobs.append(mk(0, "norm"))
                if t + 1 < ntiles:
                    jobs.append(mk(t + 1, "norm"))
                jobs.append(mk(t, 0))
                jobs.append(mk(t, 1))
            return jobs

        def norm_tiles(src, ntiles, hT, hT_key, tagbase, xs, hb, ssq, junk):
            for j in norm_jobs(src, ntiles, hT, hT_key, tagbase, xs, hb, ssq, junk):
                j()

        AR.push()
        Wkv = AR.bf16(16 * 2048)
        wst = [AR.f32(2048) for _ in range(2)]
        xs1 = [AR.f32(2048) for _ in range(3)]
        hb1 = [AR.bf16(2048) for _ in range(2)]
        ssq1 = AR.f32(4)
        junk1 = AR.bf16(2048)
        hTb = [AR.bf16(16 * 512) for _ in range(2)]
        ktb = [AR.bf16(8 * 512) for _ in range(2)]
        vbk = [AR.bf16(4 * 1024) for _ in range(2)]
        Wkvv = Wkv.rearrange("p (c n) -> p c n", c=16)
        for dc in range(16):
            sl = dc % 2
            dma("pool", Wkvv[:, dc, :], w_in[128 * dc:128 * (dc + 1), 1024:3072], (), [("Wkv", dc)], "wkv")
        Wkeys = [("Wkv", dc) for dc in range(16)]
        KTsv = KTs.rearrange("h p n -> p h n")
        def p1_jobs(b):
            return norm_jobs(xf[512 * b:512 * (b + 1), :], 4, hTb[b % 2], ("hTb", b % 2), "p1", xs1, hb1, ssq1, junk1)
        for j in p1_jobs(0):
            j()
        for b in range(16):
            hT = hTb[b % 2]
            hTkey = ("hTb", b % 2)
            pending = p1_jobs(b + 1) if b + 1 < 16 else []
            hTv = hT.rearrange("p (c t) -> p c t", c=16)
            hkeys = [(hTkey, t) for t in range(4)]
            kt = ktb[b % 2].rearrange("p (h t) -> p h t", h=8)
            ktk = ("ktb", b % 2)
            vb_ = vbk[b % 2].rearrange("p (t n) -> p t n", t=4)
            vk = ("vbk", b % 2)
            gcount = [0]

            def after_group():
                gcount[0] += 1
                if pending:
                    pending.pop(0)()
            for h in range(8):
                bk, bkey = next_bank()
                for dc in range(16):
                    mm(bk, Wkvv[:, dc, 128 * h:128 * (h + 1)], hTv[:, dc, :], dc == 0, dc == 15,
                       hkeys + [Wkeys[dc]], [bkey])
                evac(kt[:, h, :], bk, [bkey], [(ktk, h)])
                after_group()
            for rep in range(2):
                dma("pool", KTsv[:, :, rep * SEQ + 512 * b:rep * SEQ + 512 * (b + 1)], kt,
                    [(ktk, h) for h in range(8)], [("KTs", b, rep)], "kts%d" % (b % 2))
            for t in range(4):
                for half in range(2):
                    bk, bkey = next_bank()
                    for dc in range(16):
                        mm(bk, hTv[:, dc, 128 * t:128 * (t + 1)], Wkvv[:, dc, 1024 + 512 * half:1024 + 512 * (half + 1)],
                           dc == 0, dc == 15, hkeys + [Wkeys[dc]], [bkey])
                    evac(vb_[:, t, 512 * half:512 * (half + 1)], bk, [bkey], [(vk, t, half)])
                    after_group()
                for rep in range(2):
                    dma("pool", Vs[:, :, 64 * rep + 4 * b + t, :].rearrange("h p d -> p h d"),
                        vb_[:, t, :].rearrange("p (h d) -> p h d", h=8),
                        [(vk, t, 0), (vk, t, 1)], [("Vs", b, t, rep)], "vs%d" % (b % 2))
            while pending:
                pending.pop(0)()
        P.barrier()
        AR.pop()

        ozTa = AR.bf16(8 * OWN)
        ozTav = ozTa.rearrange("p (h t) -> p h t", h=8)

        def load_wgroup(wsrc_cols, ncol, wst_f, wbf, key, tag, nk=16, stkey=None):
            half = nk // 2
            for hh in range(2):
                dma("pool", wbf[:, hh * half * ncol:(hh + 1) * half * ncol].rearrange("p (c n) -> p c n", c=half),
                    wsrc_cols[128 * half * hh:128 * half * (hh + 1), :].rearrange("(c p) n -> p c n", p=128),
                    (), [(key, "bf", hh)], tag)
            return [(key, "bf", 0), (key, "bf", 1)]

        def own_prep(AR, with_halo):
            hTo = AR.bf16(16 * OWN)
            hTh = AR.bf16(16 * 512) if with_halo else None
            AR.push()
            xs = [AR.f32(2048) for _ in range(2)]
            hb = [AR.bf16(2048) for _ in range(2)]
            ssq = AR.f32(4)
            junk = AR.bf16(2048)
            norm_tiles(xo, NT_OWN, hTo, "hTo", "own", xs, hb, ssq, junk)
            if with_halo:
                norm_tiles(xh, 4, hTh, "hTh", "own", xs, hb, ssq, junk)
            P.barrier()
            AR.pop()
            return hTo, hTh

        tok_blocks = [(0, 512), (512, 512), (1024, 128)]

        AR.push()
        qaT = AR.bf16(8 * OWN)
        szga = AR.bf16(8 * OWN)
        kaTs = AR.bf16(8 * 32)
        vas = AR.bf16(8 * 130)
        qaTv = qaT.rearrange("p (h t) -> p h t", h=8)
        szgaTv = szga.rearrange("p (h t) -> p h t", h=8)
        kaTsv = kaTs.rearrange("p (h t) -> p h t", h=8)
        vasv = vas.rearrange("p (h d) -> p h d", h=8)
        memset("pool", vasv[:, :, 128:130], 1.0, ["vas1"])

        AR.push()
        hTo, _ = own_prep(AR, False)
        hTov = hTo.rearrange("p (c t) -> p c t", c=16)
        hokeys = [("hTo", t) for t in range(NT_OWN)]
        GC = 256
        wstf = [None, None]
        wbfs = [AR.bf16(16 * GC) for _ in range(2)]
        ost = [AR.f32(GC) for _ in range(3)]
        sgt = [AR.f32(GC) for _ in range(2)]
        ost_i = [0]

        def feat_major(wbf, wkeys, ncol, col0, dst_fn, scale, tokv, tokkeys, blocks, evac_fn=None):
            wv = wbf.rearrange("p (c n) -> p c n", c=16)
            for cc in range(ncol // 128):
                for (t0, n) in blocks:
                    bk, bkey = next_bank()
                    for dc in range(16):
                        mm(bk[:, 0:n], wv[:, dc, 128 * cc:128 * (cc + 1)], tokv[:, dc, t0:t0 + n], dc == 0, dc == 15,
                           wkeys + tokkeys, [bkey])
                    dst, dkey = dst_fn(col0 + 128 * cc, t0, n)
                    if evac_fn is not None:
                        evac_fn(dst, dkey, bk[:, 0:n], bkey, n)
                    elif scale is None:
                        evac(dst, bk[:, 0:n], [bkey], [dkey])
                    else:
                        ts("dve", dst, bk[:, 0:n], scale, None, ALU.mult, ALU.bypass, [bkey], [dkey])

        def tok_major(wbf, wkeys, ncol, tokv, tokkeys, ntiles, sink):
            wv = wbf.rearrange("p (c n) -> p c n", c=16)
            for t in range(ntiles):
                bk, bkey = next_bank()
                for dc in range(16):
                    mm(bk[:, 0:ncol], tokv[:, dc, 128 * t:128 * (t + 1)], wv[:, dc, :], dc == 0, dc == 15,
                       wkeys + tokkeys, [bkey])
                sink(t, bk[:, 0:ncol], bkey)

        def out_rows(dst, col0, ncol, src_ap, skey, t, nrows_total):
            r0 = 128 * t
            n = min(128, nrows_total - r0)
            if n <= 0:
                return
            dma("sp", dst[r0:r0 + n, col0:col0 + ncol], src_ap[0:n, :], [skey], [("out", id(dst), t, col0)], "outst%d" % skey[1])

        def silu_to(dst, psum, bkey, dkey, mulb=None, mkey=None):
            if mulb is None:
                act(dst, psum, AF.Silu, [bkey], [dkey])
                return
            s_ = sgt[ost_i[0] % 2]
            sk = ("sgt", ost_i[0] % 2)
            ost_i[0] += 1
            n = psum.shape[-1]
            act(s_[:, 0:n], psum, AF.Silu, [bkey], [sk])
            tt("dve", dst, s_[:, 0:n], mulb, ALU.mult, [sk, mkey], [dkey])

        gi = [0]

        def wgroup(col0):
            sl = gi[0] % 2
            gi[0] += 1
            keys = load_wgroup(w_in[:, col0:col0 + GC], GC, wstf[sl], wbfs[sl], ("wg", sl), "wg%d" % sl)
            return wbfs[sl], keys

        sgt2 = [AR.f32(512) for _ in range(2)]

        for g in range(4096 // GC):
            col0 = g * GC
            wbf, wkeys = wgroup(col0)
            if col0 < 1024:
                feat_major(wbf, wkeys, GC, col0,
                           lambda c, t0, n: (qaTv[:, c // 128, t0:t0 + n], ("qaT", c // 128, t0)),
                           0.125, hTov, hokeys, tok_blocks)
            elif col0 < 2048:
                c1 = col0 - 1024

                def sink(t, ps, bkey, c1=c1):
                    o = ost[ost_i[0] % 3]
                    ok = ("ost", ost_i[0] % 3)
                    ost_i[0] += 1
                    evac(o[:, 0:GC], ps, [bkey], [ok])
                    out_rows(o_ka, c1, GC, o, ok, t, 1056)
                tok_major(wbf, wkeys, GC, hTov, hokeys, NT_OWN, sink)
                feat_major(wbf, wkeys, GC, c1,
                           lambda c, t0, n: (kaTsv[:, c // 128, 0:32], ("kaTs", c // 128)),
                           None, hTov, hokeys, [(1024, 32)])
            elif col0 < 3072:
                c1 = col0 - 2048

                def sink(t, ps, bkey, c1=c1):
                    o = ost[ost_i[0] % 3]
                    ok = ("ost", ost_i[0] % 3)
                    ost_i[0] += 1
                    evac(o[:, 0:GC], ps, [bkey], [ok])
                    out_rows(o_va, c1, GC, o, ok, t, 1056)
                    if t == 8:
                        cp("pool", vasv[0:32, c1 // 128:c1 // 128 + 2, 0:128],
                           o[0:32, 0:GC].rearrange("p (h d) -> p h d", h=2), [ok], [("vas", c1)])
                tok_major(wbf, wkeys, GC, hTov, hokeys, NT_OWN, sink)
            else:
                c1 = col0 - 3072

                def zevac(dst, dkey, ps, bkey, n):
                    s_ = sgt2[ost_i[0] % 2]
                    sk = ("sgt2", ost_i[0] % 2)
                    ost_i[0] += 1
                    act(s_[:, 0:n], ps, AF.Silu, [bkey], [sk])
                    ts("dve", dst, s_[:, 0:n], sublnc[:, 0:1], None, ALU.mult, ALU.bypass, [sk, "sublnc"], [dkey])
                feat_major(wbf, wkeys, GC, c1,
                           lambda c, t0, n: (szgaTv[:, c // 128, t0:t0 + n], ("szgaT", c // 128, t0)),
                           None, hTov, hokeys, tok_blocks, evac_fn=zevac)
        P.barrier()
        AR.pop()

        def attention_unit(nmaps, nq, qT_fn, tiles, Pbufs, tmpb, epilogue, ukey, abase=0):
            nsb = (nq + 127) // 128
            T = len(tiles)
            Sps = [(psA, [("ps", "A", 0), ("ps", "A", 1)]), (psB, [("ps", "B", 0), ("ps", "B", 1)])]
            Okeys = [("ps", "O", 0), ("ps", "O", 1), ("ps", "O", 2)]

            def qk(t):
                tl = tiles[t]
                sp, skeys = Sps[t % 2]
                for m in range(nmaps):
                    mm(sp[0:tl["nk"], 512 * m:512 * m + nq], tl["kT"](m), qT_fn(m), True, True,
                       tl["keys"], [skeys[m]])

            def softmax_exp(t):
                tl = tiles[t]
                nk = tl["nk"]
                sp, skeys = Sps[t % 2]
                pb = Pbufs[t % len(Pbufs)]
                pk = ("Pb", t % len(Pbufs))
                pbv = pb.rearrange("p (m q) -> p m q", m=2)
                spv = sp.rearrange("p (m q) -> p m q", m=2)
                if tl["bias"] is not None:
                    tb = tmpb[t % len(tmpb)]
                    tk = ("tmpb", t % len(tmpb))
                    tbv = tb.rearrange("p (m q) -> p m q", m=2)
                    for m in range(nmaps):
                        tt("dve", tbv[0:nk, m, 0:nq], spv[0:nk, m, 0:nq], tl["bias"][0:nk, 0:nq], ALU.add,
                           [skeys[m], tl["biaskey"]], [tk])
                    act(pbv[0:nk, 0:nmaps, 0:nq], tbv[0:nk, 0:nmaps, 0:nq], AF.Exp, [tk, "visA", "visB"], [pk],
                        bias=tl["vis"])
                else:
                    act(pbv[0:nk, 0:nmaps, 0:nq], spv[0:nk, 0:nmaps, 0:nq], AF.Exp,
                        skeys[0:nmaps] + ["visA", "visB"], [pk], bias=tl["vis"])

            def pv(t):
                tl = tiles[t]
                nk = tl["nk"]
                pb = Pbufs[t % len(Pbufs)]
                pk = ("Pb", t % len(Pbufs))
                pbv = pb.rearrange("p (m q) -> p m q", m=2)
                for m in range(nmaps):
                    for sb in range(nsb):
                        a = abase + m * nsb + sb
                        nqs = min(128, nq - 128 * sb)
                        first_in_bank = (t == 0 and a % 3 == 0)
                        mm(psO[0:nqs, 512 * (a // 3) + 130 * (a % 3):512 * (a // 3) + 130 * (a % 3) + 130],
                           pbv[0:nk, m, 128 * sb:128 * sb + nqs], tl["v"], first_in_bank, t == T - 1,
                           [pk] + tl["keys"], [Okeys[a // 3]])

            qk(0)
            for t in range(T):
                if t + 1 < T:
                    qk(t + 1)
                softmax_exp(t)
                if t >= 1:
                    pv(t - 1)
            pv(T - 1)
            epilogue(nsb, Okeys, abase)

        def attention_unit_T(nq, qT_fn, tiles, Pbufs, tmpb, lacc, eA, eB, h, row0):
            T = len(tiles)
            Sps = [(psA, [("ps", "A", 0), ("ps", "A", 1)]), (psB, [("ps", "B", 0), ("ps", "B", 1)])]
            Okeys = [("ps", "O", 0), ("ps", "O", 1)]
            laccv = lacc.rearrange("p (m q) -> p m q", m=2)
            lk = "lacc"
            Lps = [psO[:, 1024:1536], psT.bitcast(F32)]
            Lkeys = [("ps", "O", 2), KT_T]
            dve_started = [False]

            def qk(t):
                tl = tiles[t]
                sp, skeys = Sps[t % 2]
                for m in range(2):
                    mm(sp[0:tl["nk"], 512 * m:512 * m + nq], tl["kT"](m), qT_fn(m), True, True,
                       tl["keys"], [skeys[m]])

            def softmax_exp(t):
                tl = tiles[t]
                nk = tl["nk"]
                sp, skeys = Sps[t % 2]
                pb = Pbufs[t % len(Pbufs)]
                pk = ("Pb", t % len(Pbufs))
                pbv = pb.rearrange("p (m q) -> p m q", m=2)
                spv = sp.rearrange("p (m q) -> p m q", m=2)
                if tl["bias"] is not None:
                    tb = tmpb[t % len(tmpb)]
                    tk = ("tmpb", t % len(tmpb))
                    tbv = tb.rearrange("p (m q) -> p m q", m=2)
                    for m in range(2):
                        tt("dve", tbv[0:nk, m, 0:nq], spv[0:nk, m, 0:nq], tl["bias"][0:nk, 0:nq], ALU.add,
                           [skeys[m], tl["biaskey"]], [tk])
                    act(pbv[0:nk, :, 0:nq], tbv[0:nk, :, 0:nq], AF.Exp, [tk, "visA", "visB"], [pk], bias=tl["vis"])
                else:
                    act(pbv[0:nk, :, 0:nq], spv[0:nk, :, 0:nq], AF.Exp, skeys + ["visA", "visB"], [pk], bias=tl["vis"])

            def pv(t):
                tl = tiles[t]
                nk = tl["nk"]
                pb = Pbufs[t % len(Pbufs)]
                pk = ("Pb", t % len(Pbufs))
                pbv = pb.rearrange("p (m q) -> p m q", m=2)
                on_pe = (t % 3 == 0)
                if on_pe:
                    for m in range(2):
                        mm(Lps[m][:, 0:nq], ones_b[0:nk, :], pbv[0:nk, m, 0:nq], t == 0, False,
                           [pk, "ones_b"], [Lkeys[m]])
                elif not dve_started[0]:
                    dve_started[0] = True
                    cp("dve", laccv[0:nk, :, 0:nq], pbv[0:nk, :, 0:nq], [pk], [lk])
                else:
                    tt("dve", laccv[0:nk, :, 0:nq], laccv[0:nk, :, 0:nq], pbv[0:nk, :, 0:nq], ALU.add, [pk, lk], [lk])
                for m in range(2):
                    mm(psO[:, 512 * m:512 * m + nq], tl["v"][0:nk, 0:128], pbv[0:nk, m, 0:nq], t == 0, t == T - 1,
                       [pk] + tl["keys"], [Okeys[m]])

            qk(0)
            for t in range(T):
                if t + 1 < T:
                    qk(t + 1)
                softmax_exp(t)
                if t >= 1:
                    pv(t - 1)
            pv(T - 1)
            sp, skeys = Sps[T % 2]
            spv = sp.rearrange("p (m q) -> p m q", m=2)
            eAv = eA.rearrange("p (m q) -> p m q", m=2)
            eBv = eB.rearrange("p (m q) -> p m q", m=2)
            for m in range(2):
                mm(Lps[m][:, 0:nq], ones_f, laccv[:, m, 0:nq], False, True, [lk, "ones_f"], [Lkeys[m]])
            for m in range(2):
                act(eAv[:, m, 0:nq], Lps[m][:, 0:nq], AF.Ln, [Lkeys[m]], ["eA"])
            act(eAv[:, :, 0:nq], eAv[:, :, 0:nq], AF.Exp, ["eA"], ["eA"], scale=-1.0)
            tt("dve", eBv[:, 1, 0:nq], psO[:, 512:512 + nq], eAv[:, 1, 0:nq], ALU.mult, [Okeys[1], "eA"], ["eB"])
            tt("dve", eBv[:, 0, 0:nq], psO[:, 0:nq], eAv[:, 0, 0:nq], ALU.mult, [Okeys[0], "eA"], ["eB"])
            stt(eBv[:, 0, 0:nq], eBv[:, 1, 0:nq], neg_lam, eBv[:, 0, 0:nq], ALU.mult, ALU.add, ["eB", "lamc"], ["eB"])
            tt("dve", eBv[:, 1, 0:nq], eBv[:, 0, 0:nq], eBv[:, 0, 0:nq], ALU.mult, ["eB"], ["eB"])
            mm(sp[:, 0:nq], onesm_f, eBv[:, 1, 0:nq], True, True, ["eB", "onesm_f"], [skeys[0]])
            act(eAv[:, 0, 0:nq], sp[:, 0:nq], AF.Ln, [skeys[0]], ["eA"], bias=EPS)
            act(eAv[:, 0, 0:nq], eAv[:, 0, 0:nq], AF.Exp, ["eA"], ["eA"], scale=-0.5)
            tt("dve", eBv[:, 0, 0:nq], eBv[:, 0, 0:nq], eAv[:, 0, 0:nq], ALU.mult, ["eB", "eA"], ["eB"])
            szk = [("szgaT", h, b0) for b0 in (0, 512, 1024) if b0 < row0 + nq and b0 + 512 > row0]
            tt("dve", ozTav[:, h, row0:row0 + nq], eBv[:, 0, 0:nq], szgaTv[:, h, row0:row0 + nq], ALU.mult,
               ["eB"] + szk, [("ozTa", h, row0)])

        def acc_ap(a, n):
            return psO[0:n, 512 * (a // 3) + 130 * (a % 3):512 * (a // 3) + 130 * (a % 3) + 130]

        def make_epilogue_A(h, row0, nq, osb, ep):
            def epilogue(nsb, Okeys):
                for sb in range(nsb):
                    n = min(128, nq - 128 * sb)
                    a0, a1 = sb, nsb + sb
                    k = ("ep", sb % 2)
                    e = ep[sb % 2]
                    o1 = osb[sb % 2]
                    cp("act", o1[0:n, 0:130], acc_ap(a0, n), [Okeys[a0 // 3]], [k])
                    cp("act", o1[0:n, 130:260], acc_ap(a1, n), [Okeys[a1 // 3]], [k])
                    P.op("dve", lambda E, e=e, o1=o1, n=n: E.reciprocal(e[0:n, 0:1], o1[0:n, 128:129]), [k], [k])
                    P.op("dve", lambda E, e=e, o1=o1, n=n: E.reciprocal(e[0:n, 1:2], o1[0:n, 258:259]), [k], [k])
                    tt("dve", e[0:n, 1:2], e[0:n, 1:2], neg_lam[0:n, :], ALU.mult, [k, "lamc"], [k])
                    ts("dve", o1[0:n, 130:258], o1[0:n, 130:258], e[0:n, 1:2], None, ALU.mult, ALU.bypass, [k], [k])
                    stt(o1[0:n, 0:128], o1[0:n, 0:128], e[0:n, 0:1], o1[0:n, 130:258], ALU.mult, ALU.add, [k], [k])
                    P.op("dve", lambda E, e=e, o1=o1, n=n: E.scalar_tensor_tensor(
                        o1[0:n, 130:258], o1[0:n, 0:128], 1.0, o1[0:n, 0:128], ALU.mult, ALU.mult,
                        accum_out=e[0:n, 2:3]), [k], [k])
                    act(e[0:n, 2:3], e[0:n, 2:3], AF.Ln, [k], [k], bias=EPS, scale=1.0 / 128)
                    act(e[0:n, 3:4], e[0:n, 2:3], AF.Exp, [k], [k], scale=-0.5)
                    r = row0 + 128 * sb
                    tix, rr = r // 128, r % 128
                    ob = o1[:, 260:324].bitcast(BF16)
                    stt(ob[0:n, :], o1[0:n, 0:128], e[0:n, 3:4], szgav[rr:rr + n, tix, 128 * h:128 * (h + 1)],
                        ALU.mult, ALU.mult, [k, ("szga", tix, (128 * h) // GC * GC)], [k])
                    tr(psT[:, 0:n], ob[0:n, :], ident_b[0:n, 0:n], [k, "ident_b"], [KT_T])
                    cp("act", ozTav[:, h, r:r + n], psT[:, 0:n], [KT_T], [("ozTa", h, r)])
            return epilogue

        def load_bias_tile(dstb, dkey, hk, hkkey, src, offset, nq, mask_ap, maskkey, nk=128, bank_idx=6):
            hsrc = bass.AP(tensor=src.tensor, offset=src.offset + offset, ap=[[1, 128], [1, nq]])
            dma("sp", hk[:, 0:nq], hsrc, ["fa_s", "gb_s"], [hkkey], "hk")
            bk, bkey = banks[bank_idx]
            mm(bk[:, 0:nq], antij, hk[:, 0:nq], True, True, [hkkey, "antij"], [bkey])
            if mask_ap is None:
                cp("dve", dstb[0:nk, 0:nq], bk[0:nk, 0:nq], [bkey], [dkey])
            else:
                tt("dve", dstb[0:nk, 0:nq], bk[0:nk, 0:nq], mask_ap[0:nk, 0:nq], ALU.add, [bkey, maskkey], [dkey])

        AR.push()
        Pbufs = [AR.bf16(1024) for _ in range(4)]
        tmpb = [AR.f32(1024) for _ in range(2)]
        lacc = AR.f32(1024)
        eA = AR.f32(1024)
        eB = AR.f32(1024)
        hkb = AR.f32(512)
        maskA = AR.f32(5 * 512)
        maskAv = maskA.rearrange("p (s q) -> p s q", s=5)
        dma("sp", maskAv, c_maskA.rearrange("s p q -> p s q"), (), ["maskA"], "c0")
        AR.push()
        biasA = [AR.f32(5 * 512) for _ in range(2)]
        KTw = [AR.bf16(64 * 128) for _ in range(2)]
        Vw = [AR.bf16(64 * 130) for _ in range(2)]
        for i in range(2):
            memset("pool", Vw[i].rearrange("p (t d) -> p t d", t=64)[:, :, 128:130], 1.0, [("Vw1", i)])

        def p3_dyn(E):
            if "pid" not in pid_holder:
                pid_holder["pid"] = E.partition_id()
            return pid_holder["pid"]

        u = 0
        for h in range(8):
            bA = biasA[h % 2]
            bAv = bA.rearrange("p (s q) -> p s q", s=5)
            for s in range(5):
                load_bias_tile(bAv[:, s, :], ("biasA", h % 2, s), hkb, "hkb", fa_s[h:h + 1, :], 512 - 128 * s, 512,
                               maskAv[:, s, :], "maskA")
            for qb in range(2):
                sl = u % 2
                ktw = KTw[sl]
                vw = Vw[sl].rearrange("p (t d) -> p t d", t=64)

                dma("pool", ktw, KTs[h, :, 512 * qb:512 * qb + 64 * 128], (), [("KTw", sl)], "ktw%d" % sl)
                for half in range(2):
                    dma("pool", vw[:, 32 * half:32 * half + 32, 0:128],
                        Vs[h, :, 4 * qb + 32 * half:4 * qb + 32 * half + 32, :], (), [("Vw", sl, half)],
                        "vw%d" % sl)
                tiles = []
                for s in range(64):
                    tiles.append(dict(
                        kT=(lambda m, s=s, ktw=ktw: ktw[64 * m:64 * m + 64, 128 * s:128 * (s + 1)]),
                        v=vw[:, s, :], nk=128,
                        bias=(bAv[:, s, :] if s < 5 else None), biaskey=("biasA", h % 2, s),
                        vis=visA[:, 64 * qb + s:64 * qb + s + 1],
                        keys=[("KTw", sl), ("Vw", sl, s // 32), ("Vw1", sl)]))
                r0 = 512 * qb
                ukey = ("qaT", h, r0)
                attention_unit_T(512, (lambda m, h=h, r0=r0: qaTv[64 * m:64 * m + 64, h, r0:r0 + 512]),
                                 tiles, Pbufs, tmpb, lacc, eA, eB, h, r0)
                u += 1
        P.barrier()
        AR.pop()

        AR.push()
        KTc = AR.bf16(8 * PAST)
        Vc = AR.bf16(16 * 8 * 130)
        cst = [AR.f32(1024) for _ in range(2)]
        KTcv = KTc.rearrange("p (h n) -> p h n", h=8)
        Vcv = Vc.rearrange("p (t h d) -> p t h d", t=16, h=8)
        memset("pool", Vcv[:, :, :, 128:130], 1.0, ["Vc1"])
        psTf = [b for b in banks]
        for t in range(16):
            sl = t % 2
            dma("sp", cst[sl], cka[128 * t:128 * (t + 1), :], (), [("cst", sl)], "cst%d" % sl)
            for hh in range(2):
                bk, bkey = next_bank()
                for j in range(4):
                    h = 4 * hh + j
                    tr(bk[:, 128 * j:128 * (j + 1)], cst[sl][:, 128 * h:128 * (h + 1)], ident_f, [("cst", sl), "ident_f"],
                       [bkey])
                evac(KTcv[:, 4 * hh:4 * hh + 4, 128 * t:128 * (t + 1)], bk.rearrange("p (h n) -> p h n", h=4),
                     [bkey], [("KTc", t)])
        for t in range(16):
            sl = t % 2
            dma("sp", cst[sl], cva[128 * t:128 * (t + 1), :], (), [("cst", sl)], "cst%d" % sl)
            cp("pool", Vcv[:, t, :, 0:128], cst[sl].rearrange("p (h d) -> p h d", h=8), [("cst", sl)], [("Vc", t)])
        bsAll = AR.f32(2 * 8 * 32)
        bsAllv = bsAll.rearrange("p (j h q) -> p j h q", j=2, h=8)
        for h in range(8):
            load_bias_tile(bsAllv[:, 0, h, :], ("bsAll", 0), hkb, "hkb", fa_s[h:h + 1, :], 512, 32, None, None)
            load_bias_tile(bsAllv[:, 1, h, :], ("bsAll", 1), hkb, "hkb", fa_s[h:h + 1, :], 384, 32, None, None, nk=32)
        SpsS = [(psA, [("ps", "A", 0), ("ps", "A", 1)]), (psB, [("ps", "B", 0), ("ps", "B", 1)])]
        oTk, Lk, Mk = ("ps", "O", 0), ("ps", "O", 1), ("ps", "O", 2)
        TS = 17

        def sa_nk(t):
            return 128 if t < 16 else 32

        def sa_qk(t):
            sp, skeys = SpsS[t % 2]
            nk = sa_nk(t)
            for m in range(2):
                for h in range(8):
                    kT = KTcv[64 * m:64 * m + 64, h, 128 * t:128 * (t + 1)] if t < 16 else kaTsv[64 * m:64 * m + 64, h, 0:32]
                    mm(sp[0:nk, 512 * m + 32 * h:512 * m + 32 * h + 32], kT, qaTv[64 * m:64 * m + 64, h, 1024:1056],
                       True, True, [("KTc", t), ("kaTs", h)], [skeys[m]])

        def sa_exp(t):
            sp, skeys = SpsS[t % 2]
            nk = sa_nk(t)
            spv = sp.rearrange("p (m c) -> p m c", m=2)[:, :, 0:256]
            pb = Pbufs[t % len(Pbufs)][:, 0:512]
            pbv = pb.rearrange("p (m c) -> p m c", m=2)
            pk = ("Pb", t % len(Pbufs))
            if t >= 15:
                tb = tmpb[t % 2][:, 0:512]
                tk = ("tmpb", t % 2)
                tbv = tb.rearrange("p (m c) -> p m c", m=2)
                for m in range(2):
                    tt("dve", tbv[0:nk, m, :], spv[0:nk, m, :],
                       bsAllv[0:nk, t - 15, :, :].rearrange("p h q -> p (h q)"), ALU.add,
                       [skeys[m], ("bsAll", t - 15)], [tk])
                act(pbv[0:nk, :, :], tbv[0:nk, :, :], AF.Exp, [tk, "visB"], [pk], bias=visB[0:nk, 15:16])
            else:
                act(pbv[0:nk, :, :], spv[0:nk, :, :], AF.Exp, skeys + ["visB"], [pk], bias=visB[0:nk, 15:16])

        def sa_pv(t):
            nk = sa_nk(t)
            pb = Pbufs[t % len(Pbufs)][:, 0:512]
            pk = ("Pb", t % len(Pbufs))
            mm(psO[:, 512:1024], ones_b[0:nk, :], pb[0:nk, :], t == 0, t == TS - 1, [pk, "ones_b"], [Lk])
            first = True
            for h in range(8):
                v = Vcv[:, t, h, 0:128] if t < 16 else vasv[0:32, h, 0:128]
                for m in range(2):
                    c0 = 256 * m + 32 * h
                    mm(psO[:, c0:c0 + 32], v, pb[0:nk, c0:c0 + 32], t == 0 and first, t == TS - 1,
                       [pk, ("Vc", t), "Vc1", ("vas", (128 * h) // GC * GC), "vas1"], [oTk])
                    first = False

        sa_qk(0)
        for t in range(TS):
            if t + 1 < TS:
                sa_qk(t + 1)
            sa_exp(t)
            if t >= 1:
                sa_pv(t - 1)
        sa_pv(TS - 1)
        r_ = eA[:, 0:512]
        t_ = eB[:, 0:512]
        o_ = eA[:, 512:768]
        sq_ = eA[:, 768:1024]
        rn_ = eB[:, 512:768]
        act(r_, psO[:, 512:1024], AF.Ln, [Lk], ["eA"])
        act(r_, r_, AF.Exp, ["eA"], ["eA"], scale=-1.0)
        tt("dve", t_, psO[:, 0:512], r_, ALU.mult, [oTk, "eA"], ["eB"])
        stt(o_, t_[:, 256:512], neg_lam, t_[:, 0:256], ALU.mult, ALU.add, ["eB", "lamc"], ["eA"])
        tt("dve", sq_, o_, o_, ALU.mult, ["eA"], ["eA"])
        mm(psO[:, 1024:1280], onesm_f, sq_, True, True, ["eA", "onesm_f"], [Mk])
        act(rn_, psO[:, 1024:1280], AF.Ln, [Mk], ["eB"], bias=EPS)
        act(rn_, rn_, AF.Exp, ["eB"], ["eB"], scale=-0.5)
        tt("dve", o_, o_, rn_, ALU.mult, ["eA", "eB"], ["eA"])
        tt("dve", ozTav[:, :, 1024:1056], o_.rearrange("p (h q) -> p h q", h=8), szgaTv[:, :, 1024:1056], ALU.mult,
           ["eA"] + [("szgaT", h, 1024) for h in range(8)], [("ozTa", "s")])
        P.barrier()
        AR.pop()
        AR.pop()
        AR.pop()

        AR.push()
        qbT = AR.bf16(8 * OWN)
        szb = AR.bf16(NT_OWN * 1024)
        kbT = AR.bf16(8 * 1664)
        vbb = AR.bf16(13 * 8 * 130)
        qbTv = qbT.rearrange("p (h t) -> p h t", h=8)
        szbv = szb.rearrange("p (t n) -> p t n", t=NT_OWN)
        kbTv = kbT.rearrange("p (h t) -> p h t", h=8)
        vbv = vbb.rearrange("p (t h d) -> p t h d", t=13, h=8)
        memset("pool", vbv[:, :, :, 128:130], 1.0, ["vb1"])

        AR.push()
        hTo, hTh = own_prep(AR, True)
        hTov = hTo.rearrange("p (c t) -> p c t", c=16)
        hThv = hTh.rearrange("p (c t) -> p c t", c=16)
        hokeys = [("hTo", t) for t in range(NT_OWN)]
        hhkeys = [("hTh", t) for t in range(4)]
        GC = 256
        wstf = [None, None]
        wbfs = [AR.bf16(16 * GC) for _ in range(2)]
        ost = [AR.f32(GC) for _ in range(3)]
        sgt = [AR.f32(GC) for _ in range(2)]
        qscale = float(128 ** -0.5)
        for g in range(4096 // GC):
            col0 = 4096 + g * GC
            c1 = (g * GC) % 1024
            wbf, wkeys = wgroup(col0)
            if g * GC < 1024:
                feat_major(wbf, wkeys, GC, c1,
                           lambda c, t0, n: (qbTv[:, c // 128, t0:t0 + n], ("qbT", c // 128, t0)),
                           qscale, hTov, hokeys, tok_blocks)
            elif g * GC < 2048:
                feat_major(wbf, wkeys, GC, c1,
                           lambda c, t0, n: (kbTv[:, c // 128, t0:t0 + n], ("kbT", c // 128, t0)),
                           None, hThv, hhkeys, [(0, 512)])
                feat_major(wbf, wkeys, GC, c1,
                           lambda c, t0, n: (kbTv[:, c // 128, 512 + t0:512 + t0 + n], ("kbT", c // 128, 512 + t0)),
                           None, hTov, hokeys, tok_blocks)

                def sink(t, ps, bkey, c1=c1):
                    if t < 4:
                        return
                    o = ost[ost_i[0] % 3]
                    ok = ("ost", ost_i[0] % 3)
                    ost_i[0] += 1
                    evac(o[:, 0:GC], ps, [bkey], [ok])
                    if t < 8:
                        dma("sp", o_kbp[128 * (t - 4):128 * (t - 3), c1:c1 + GC], o[:, 0:GC], [ok],
                            [("okbp", t, c1)], "outst%d" % ok[1])
                    else:
                        dma("sp", o_kbs[480:512, c1:c1 + GC], o[0:32, 0:GC], [ok], [("okbs", c1)], "outst%d" % ok[1])
                tok_major(wbf, wkeys, GC, hTov, hokeys, NT_OWN, sink)
            elif g * GC < 3072:
                def sinkh(t, ps, bkey, c1=c1):
                    evac(vbv[:, t, c1 // 128:c1 // 128 + GC // 128, 0:128], ps.rearrange("p (h d) -> p h d", h=GC // 128),
                         [bkey], [("vb", t, c1)])
                tok_major(wbf, wkeys, GC, hThv, hhkeys, 4, sinkh)

                def sink(t, ps, bkey, c1=c1):
                    o = ost[ost_i[0] % 3]
                    ok = ("ost", ost_i[0] % 3)
                    ost_i[0] += 1
                    evac(o[:, 0:GC], ps, [bkey], [ok])
                    cp("pool", vbv[:, 4 + t, c1 // 128:c1 // 128 + GC // 128, 0:128],
                       o[:, 0:GC].rearrange("p (h d) -> p h d", h=GC // 128), [ok], [("vb", 4 + t, c1)])
                    if 4 <= t < 8:
                        dma("sp", o_vbp[128 * (t - 4):128 * (t - 3), c1:c1 + GC], o[:, 0:GC], [ok],
                            [("ovbp", t, c1)], "outst%d" % ok[1])
                    elif t == 8:
                        dma("sp", o_vbs[480:512, c1:c1 + GC], o[0:32, 0:GC], [ok], [("ovbs", c1)], "outst%d" % ok[1])
                tok_major(wbf, wkeys, GC, hTov, hokeys, NT_OWN, sink)
            else:
                def sink(t, ps, bkey, c1=c1):
                    silu_to(szbv[:, t, c1:c1 + GC], ps, bkey, ("szb", t, c1))
                tok_major(wbf, wkeys, GC, hTov, hokeys, NT_OWN, sink)
        dma("pool", o_kbs[0:480, :], ckb[32:512, :], (), [("okbs_c",)], "outc")
        dma("pool", o_vbs[0:480, :], cvb[32:512, :], (), [("ovbs_c",)], "outc")
        P.barrier()
        AR.pop()

        ozTb = AR.top_bf16(8 * OWN)
        ozTbv = ozTb.rearrange("p (h t) -> p h t", h=8)
        AR.push()
        Pbufs = [AR.bf16(1024) for _ in range(4)]
        tmpb = [AR.f32(1024) for _ in range(2)]
        hkr = [AR.f32(128) for _ in range(4)]
        hki = [0]

        def load_bias_ring(dstb, dkey, src, offset, nq, mask_ap, maskkey, nk=128):
            i = hki[0]
            hki[0] += 1
            load_bias_tile(dstb, dkey, hkr[i % 4], ("hkr", i % 4), src, offset, nq, mask_ap, maskkey, nk=nk,
                           bank_idx=5 + (i % 2))
        maskB = AR.f32(5 * 128)
        maskBv = maskB.rearrange("p (s q) -> p s q", s=5)
        dma("sp", maskBv, c_maskB.rearrange("s p q -> p s q"), (), ["maskB"], "c0")
        bsBall = AR.f32(5 * 8 * 32)
        bsBv = bsBall.rearrange("p (s h q) -> p s h q", s=5, h=8)
        KbTc = AR.bf16(8 * 512)
        Vbc = AR.bf16(4 * 8 * 130)
        AR.push()
        cst = [AR.f32(1024) for _ in range(2)]
        KbTcv = KbTc.rearrange("p (h n) -> p h n", h=8)
        Vbcv = Vbc.rearrange("p (t h d) -> p t h d", t=4, h=8)
        memset("pool", Vbcv[:, :, :, 128:130], 1.0, ["Vbc1"])
        for t in range(4):
            sl = t % 2
            dma("sp", cst[sl], ckb[128 * t:128 * (t + 1), :], (), [("cst", sl)], "cst%d" % sl)
            for hh in range(2):
                bk, bkey = next_bank()
                for j in range(4):
                    h = 4 * hh + j
                    tr(bk[:, 128 * j:128 * (j + 1)], cst[sl][:, 128 * h:128 * (h + 1)], ident_f, [("cst", sl), "ident_f"],
                       [bkey])
                evac(KbTcv[:, 4 * hh:4 * hh + 4, 128 * t:128 * (t + 1)], bk.rearrange("p (h n) -> p h n", h=4),
                     [bkey], [("KbTc", t)])
        for t in range(4):
            sl = t % 2
            dma("sp", cst[sl], cvb[128 * t:128 * (t + 1), :], (), [("cst", sl)], "cst%d" % sl)
            cp("pool", Vbcv[:, t, :, 0:128], cst[sl].rearrange("p (h d) -> p h d", h=8), [("cst", sl)], [("Vbc", t)])
        P.barrier()
        AR.pop()

        def kb_keys(h, c0, n):
            ks = []
            for (a, b_) in ((0, 512), (512, 1024), (1024, 1536), (1536, 1664)):
                if c0 < b_ and c0 + n > a:
                    ks.append(("kbT", h, a))
            return ks

        biasBall = AR.f32(5 * 8 * 128)
        bBv = biasBall.rearrange("p (s h q) -> p s h q", s=5, h=8)
        o1all = AR.f32(8 * 130)
        oball = AR.bf16(8 * 128)
        rall = AR.f32(8)
        o1v = o1all.rearrange("p (h d) -> p h d", h=8)
        obv = oball.rearrange("p (h d) -> p h d", h=8)
        for h in range(8):
            for s_ in range(5):
                load_bias_ring(bBv[:, s_, h, :], ("biasBall", s_), gb_s[h:h + 1, :], 512 - 128 * s_, 128,
                               maskBv[:, s_, :], "maskB")
        SpsB = [(psA, [("ps", "A", 0), ("ps", "A", 1)]), (psB, [("ps", "B", 0), ("ps", "B", 1)])]
        OkB = [("ps", "O", 0), ("ps", "O", 1), ("ps", "O", 2)]
        steps = [(p, s_) for p in range(8) for s_ in range(5)]

        def b_qk(i):
            p, s_ = steps[i]
            w = p + s_
            sp, skeys = SpsB[i % 2]
            for h in range(8):
                mm(sp[:, 128 * h:128 * (h + 1)], kbTv[:, h, 128 * w:128 * (w + 1)], qbTv[:, h, 128 * p:128 * (p + 1)],
                   True, True, [], [skeys[h // 4]])

        def b_exp(i):
            p, s_ = steps[i]
            w = p + s_
            sp, skeys = SpsB[i % 2]
            tb = tmpb[i % 2]
            tk = ("tmpb", i % 2)
            pb = Pbufs[i % 4]
            pk = ("Pb", i % 4)
            tt("dve", tb, sp[:, :], bBv[:, s_, :, :].rearrange("p h q -> p (h q)"), ALU.add,
               skeys + [("biasBall", s_)], [tk])
            act(pb, tb, AF.Exp, [tk, "visB"], [pk], bias=visB[:, w:w + 1])

        def b_pv(i):
            p, s_ = steps[i]
            w = p + s_
            pb = Pbufs[i % 4]
            pk = ("Pb", i % 4)
            for h in range(8):
                mm(acc_ap(h, 128), pb[:, 128 * h:128 * (h + 1)], vbv[:, w, h, :], s_ == 0 and h % 3 == 0, s_ == 4,
                   [pk], [OkB[h // 3]])
            if s_ == 4:
                b_epilogue(p)

        def b_epilogue(p, r0=None, n=128):
            if r0 is None:
                r0 = 128 * p
            k = "o1all"
            cp("act", o1all[0:n, 0:390], psO[0:n, 0:390], [OkB[0]], [k])
            cp("act", o1all[0:n, 390:780], psO[0:n, 512:902], [OkB[1]], [k])
            cp("act", o1all[0:n, 780:1040], psO[0:n, 1024:1284], [OkB[2]], [k])
            P.op("dve", lambda E: E.reciprocal(rall[0:n, 0:8], o1v[0:n, :, 128]), [k], ["rall"])
            for h in range(8):
                stt(obv[0:n, h, :], o1v[0:n, h, 0:128], rall[0:n, h:h + 1], szbv[0:n, p, 128 * h:128 * (h + 1)],
                    ALU.mult, ALU.mult, [k, "rall"], ["oball"])
            for h in range(8):
                tr(psT[:, 128 * h:128 * h + n], obv[0:n, h, :], ident_b[0:n, 0:n], ["oball", "ident_b"], [KT_T])
            cp("act", ozTbv[:, :, r0:r0 + n], psT.rearrange("p (h t) -> p h t", h=8)[:, :, 0:n], [KT_T],
               [("ozTb", "p", p)])

        b_qk(0)
        for i in range(len(steps)):
            if i + 1 < len(steps):
                b_qk(i + 1)
            b_exp(i)
            if i >= 1:
                b_pv(i - 1)
        b_pv(len(steps) - 1)

        for h in range(8):
            for s_ in range(4):
                load_bias_ring(bsBv[:, s_, h, :], ("bsBall", s_), gb_s[h:h + 1, :], 512 - 128 * s_, 32, None, None)
            load_bias_ring(bsBv[:, 4, h, :], ("bsBall", 4), gb_s[h:h + 1, :], 0, 32, None, None, nk=32)
        i0 = len(steps)

        def sb_nk(t):
            return 128 if t < 4 else 32

        def sb_qk(t):
            i = i0 + t
            sp, skeys = SpsB[i % 2]
            nk = sb_nk(t)
            for h in range(8):
                kT = KbTcv[:, h, 128 * t:128 * (t + 1)] if t < 4 else kbTv[:, h, 512 + 1024:512 + 1056]
                mm(sp[0:nk, 32 * h:32 * h + 32], kT, qbTv[:, h, 1024:1056], True, True, [("KbTc", t)], [skeys[0]])

        def sb_exp(t):
            i = i0 + t
            sp, skeys = SpsB[i % 2]
            nk = sb_nk(t)
            tb = tmpb[i % 2]
            tk = ("tmpb", i % 2)
            pb = Pbufs[i % 4]
            pk = ("Pb", i % 4)
            tt("dve", tb[0:nk, 0:256], sp[0:nk, 0:256], bsBv[0:nk, t, :, :].rearrange("p h q -> p (h q)"), ALU.add,
               [skeys[0], ("bsBall", t)], [tk])
            act(pb[0:nk, 0:256], tb[0:nk, 0:256], AF.Exp, [tk, "visB"], [pk], bias=visB[0:nk, 15:16])

        def sb_pv(t):
            i = i0 + t
            nk = sb_nk(t)
            pb = Pbufs[i % 4]
            pk = ("Pb", i % 4)
            for h in range(8):
                v = Vbcv[:, t, h, :] if t < 4 else vbv[0:32, 12, h, :]
                mm(acc_ap(h, 32), pb[0:nk, 32 * h:32 * h + 32], v, t == 0 and h % 3 == 0, t == 4,
                   [pk, ("Vbc", t), "Vbc1"], [OkB[h // 3]])

        sb_qk(0)
        for t in range(5):
            if t + 1 < 5:
                sb_qk(t + 1)
            sb_exp(t)
            if t >= 1:
                sb_pv(t - 1)
        sb_pv(4)
        b_epilogue(8, r0=1024, n=32)
        P.barrier()
        AR.pop()
        AR.pop()

        AR.push()
        hTo, _ = own_prep(AR, False)
        hTov = hTo.rearrange("p (c t) -> p c t", c=16)
        hokeys = [("hTo", t) for t in range(NT_OWN)]
        dma("sp", gvec, post.partition_broadcast(128), (), ["gvec"], "c0")
        mT = AR.bf16(16 * OWN)
        mTv = mT.rearrange("p (c t) -> p c t", c=16)
        AR.push()
        wgst = [AR.f32(16 * 128) for _ in range(2)]
        wgbf = [AR.bf16(16 * 128) for _ in range(4)]
        wost = [AR.f32(8 * 128) for _ in range(2)]
        wobf = [AR.bf16(8 * 128) for _ in range(4)]
        sga = [AR.f32(512) for _ in range(2)]
        sgb = [AR.f32(512) for _ in range(2)]
        ya = [AR.f32(512) for _ in range(2)]
        k5 = [0]
        for cc in range(16):
            wk = {}
            for gi_, (c0, nm) in enumerate(((8192, "ga"), (10240, "gb"))):
                sl = (2 * cc + gi_) % 2
                sl4 = (2 * cc + gi_) % 4
                wk[nm] = (wgbf[sl4], load_wgroup(w_in[:, c0 + 128 * cc:c0 + 128 * (cc + 1)], 128, wgst[sl], wgbf[sl4],
                                                 ("wg5", sl4), "wg5%d" % sl4))
            for gi_, (wsrc, nm) in enumerate(((w_oa, "oa"), (w_ob, "ob"))):
                sl = (2 * cc + gi_) % 2
                sl4 = (2 * cc + gi_) % 4
                wk[nm] = (wobf[sl4], load_wgroup(wsrc[:, 128 * cc:128 * (cc + 1)], 128, wost[sl], wobf[sl4],
                                                 ("wo5", sl4), "wo5%d" % sl4, nk=8))
            for (t0, n) in tok_blocks:
                i2 = k5[0] % 2
                k5[0] += 1
                sig = {}
                for nm, sbuf_ in (("ga", sga[i2]), ("gb", sgb[i2])):
                    wbf, wkeys = wk[nm]
                    wv = wbf.rearrange("p (c n) -> p c n", c=16)
                    bk, bkey = next_bank()
                    for dc in range(16):
                        mm(bk[:, 0:n], wv[:, dc, :], hTov[:, dc, t0:t0 + n], dc == 0, dc == 15, wkeys + hokeys, [bkey])
                    sk = ("sg", nm, i2)
                    act(sbuf_[:, 0:n], bk[:, 0:n], AF.Sigmoid, [bkey], [sk])
                    sig[nm] = (sbuf_, sk)
                yk = ("ya", i2)
                for nm, ozv, oznm, gnm in (("oa", ozTav, "ozTa", "ga"), ("ob", ozTbv, "ozTb", "gb")):
                    wbf, wkeys = wk[nm]
                    wv = wbf.rearrange("p (c n) -> p c n", c=8)
                    bk, bkey = next_bank()
                    for h in range(8):
                        mm(bk[:, 0:n], wv[:, h, :], ozv[:, h, t0:t0 + n], h == 0, h == 7, wkeys, [bkey])
                    sb_, sk = sig[gnm]
                    if nm == "oa":
                        tt("dve", ya[i2][:, 0:n], bk[:, 0:n], sb_[:, 0:n], ALU.mult, [bkey, sk], [yk])
                    else:
                        tt("dve", sb_[:, 0:n], bk[:, 0:n], sb_[:, 0:n], ALU.mult, [bkey, sk], [sk])
                        tt("dve", mTv[:, cc, t0:t0 + n], sb_[:, 0:n], ya[i2][:, 0:n], ALU.add, [sk, yk], [("mT", cc, t0)])
        P.barrier()
        AR.pop()
        AR.n = ARENA_WORDS
        wout_bf_full = AR.bf16(16 * 2048)
        wost2 = [AR.f32(2048) for _ in range(2)]
        woutv = wout_bf_full.rearrange("p (c n) -> p c n", c=16)
        for dc in range(16):
            sl = dc % 2
            dma("pool", woutv[:, dc, :], w_out[128 * dc:128 * (dc + 1), :], (), [("wout", dc)], "wout")
        wokeys = [("wout", dc) for dc in range(16)]
        stage = hTo.bitcast(F32)
        yrow = [stage[:, 0:2048], stage[:, 2048:4096]]
        xrow = [stage[:, 4096:6144], stage[:, 6144:8192]]
        sq5 = stage[:, 8192:8200]
        junk5 = AR.f32(1024)
        for t in range(NT_OWN):
            i2 = t % 2
            nrow = min(128, 1056 - 128 * t)
            yk = ("yrow", i2)
            xk = ("xrow", i2)
            dma("sp", xrow[i2], xo[128 * t:128 * (t + 1), :], (), [xk], "xr%d" % i2)
            for cg in range(4):
                bk, bkey = next_bank()
                for kc in range(16):
                    mm(bk, mTv[:, kc, 128 * t:128 * (t + 1)], woutv[:, kc, 512 * cg:512 * (cg + 1)], kc == 0, kc == 15,
                       wokeys, [bkey])
                evac(yrow[i2][:, 512 * cg:512 * (cg + 1)], bk, [bkey], [yk])
            sq = sq5[:, 4 * i2:4 * i2 + 2]
            sk = ("sq5", i2)
            for hh in range(2):
                P.op("dve", lambda E, i2=i2, hh=hh: E.scalar_tensor_tensor(
                    junk5, yrow[i2][:, 1024 * hh:1024 * (hh + 1)], 1.0, yrow[i2][:, 1024 * hh:1024 * (hh + 1)],
                    ALU.mult, ALU.mult, accum_out=sq5[:, 4 * i2 + 2 + hh:4 * i2 + 3 + hh]), [yk], ["junk5", sk])
            tt("dve", sq[:, 0:1], sq5[:, 4 * i2 + 2:4 * i2 + 3], sq5[:, 4 * i2 + 3:4 * i2 + 4], ALU.add, [sk], [sk])
            act(sq[:, 0:1], sq[:, 0:1], AF.Ln, [sk], [sk], bias=EPS, scale=1.0 / D)
            act(sq[:, 1:2], sq[:, 0:1], AF.Exp, [sk], [sk], scale=-0.5)
            stt(yrow[i2], yrow[i2], sq[:, 1:2], gvec, ALU.mult, ALU.mult, [yk, sk, "gvec"], [yk])
            tt("pool", yrow[i2], yrow[i2], xrow[i2], ALU.add, [yk, xk], [yk])
            dma("sp", o_y[128 * t:128 * t + nrow, :], yrow[i2][0:nrow, :], [yk], [("oy", t)], "outy")
        P.barrier()
        AR.pop()

        P.barrier()
        sem_names = P.sem_names()
        sem_ctx = [nc.semaphore("s%d" % i) for i in range(len(sem_names))]
        sems = {}
        import contextlib
        with contextlib.ExitStack() as stack:
            for nm, c in zip(sem_names, sem_ctx):
                sems[nm] = stack.enter_context(c)
            block = stack.enter_context(nc.Block())

            def replay(E, eng):
                for waits, fn, tok in P.ops[eng]:
                    for s, v in waits:
                        E.wait_ge(sems[s], v)
                    if fn is None:
                        continue
                    ins = fn(E)
                    ins.then_inc(sems[tok[0]], 16 if tok[0].startswith("dma:") else 1)

            @block.tensor
            def _(E):
                replay(E, "pe")

            @block.scalar
            def _(E):
                replay(E, "act")

            @block.vector
            def _(E):
                replay(E, "dve")

            @block.gpsimd
            def _(E):
                replay(E, "pool")

            @block.sync
            def _(E):
                replay(E, "sp")
    return nc


def _constants():
    c = {}
    c["c_ident"] = np.eye(128, dtype=np.float32)
    c["c_antij"] = np.ascontiguousarray(np.eye(128, dtype=np.float32)[::-1])
    u = np.arange(FA_LEN)
    d = u - 511
    bk = _t5_bucket_np(-d)
    oh = np.zeros((32, FA_LEN), np.float32)
    oh[bk, u] = 1.0
    oh[15, :] -= 1.0
    c["c_oha"] = oh
    v = np.arange(GB_LEN)
    idx = np.clip(127 - v, -128, 128) + 128
    ohb = np.zeros((384, GB_LEN), np.float32)
    ohb[idx, v] = 1.0
    c["c_ohb"] = ohb
    i = np.arange(128)[:, None]
    j = np.arange(512)[None, :]
    mA = np.zeros((5, 128, 512), np.float32)
    for s in range(5):
        krel = (s - 1) * 128 + i
        mA[s] = np.where(np.floor_divide(krel, 64) > (j // 64), NEG, 0.0)
    c["c_maskA"] = mA
    j2 = np.arange(128)[None, :]
    mB = np.zeros((5, 128, 128), np.float32)
    for s in range(5):
        kc = (128 * s + i) // 64 - 8
        qc = j2 // 64
        ok = (kc <= qc) & (kc >= qc - 8)
        mB[s] = np.where(ok, 0.0, NEG)
    c["c_maskB"] = mB
    return c


def _vis_tables(core):
    visA = np.zeros((128, 128), np.float32)
    for qb in range(2):
        T0 = 8 * core + 4 * qb
        for s in range(64):
            tile = T0 - 1 + s
            if tile >= 64:
                visible = True
            else:
                visible = (0 <= tile <= T0 + 3)
            visA[:, 64 * qb + s] = 0.0 if visible else NEG
    visB = np.zeros((128, 16), np.float32)
    if core == 0:
        visB[:, 0:4] = NEG
    return visA, visB


_NC_CACHE = {}


def kernel(x_prompt, x_sample, cache_k_a, cache_v_a, cache_k_b, cache_v_b, t5_bias, pre_norm, post_norm,
           w_in, lambda_q1, lambda_k1, lambda_q2, lambda_k2, subln_a, rel_bias_b, w_o_a, w_o_b, w_out):
    f = lambda a: np.ascontiguousarray(np.asarray(a, dtype=np.float32))
    xp = f(x_prompt)[0]
    xs = f(x_sample)
    consts = _constants()
    relbT = np.zeros((384, 8), np.float32)
    relbT[:257] = f(rel_bias_b)[0].T
    lam4 = np.concatenate([f(lambda_q1)[0], f(lambda_k1)[0], f(lambda_q2)[0], f(lambda_k2)[0]])[None, :]
    shared = {
        "w_in": f(w_in)[0], "w_oa": f(w_o_a)[0], "w_ob": f(w_o_b)[0], "w_out": f(w_out)[0],
        "pre": f(pre_norm), "post": f(post_norm), "subln": f(subln_a), "lam4": np.ascontiguousarray(lam4),
        "t5": f(t5_bias), "relbT": relbT,
    }
    shared.update(consts)
    in_maps = []
    for c in range(NCORES):
        xo = np.zeros((OWN, D), np.float32)
        xo[:ROWS] = xp[ROWS * c:ROWS * (c + 1)]
        xo[ROWS:ROWS + NS] = xs[c]
        xh = np.zeros((512, D), np.float32)
        if c > 0:
            xh[:] = xp[ROWS * c - 512:ROWS * c]
        visA, visB = _vis_tables(c)
        m = dict(shared)
        m.update({
            "xf": np.ascontiguousarray(np.roll(xp, -128 * (8 * c - 1), axis=0)),
            "xo": xo, "xh": xh,
            "cka": f(cache_k_a)[0, c].reshape(PAST, 1024), "cva": f(cache_v_a)[0, c].reshape(PAST, 1024),
            "ckb": f(cache_k_b)[0, c].reshape(512, 1024), "cvb": f(cache_v_b)[0, c].reshape(512, 1024),
            "c_visA": visA, "c_visB": visB,
        })
        in_maps.append(m)
    if "nc" not in _NC_CACHE:
        _NC_CACHE["nc"] = build()
    nc = _NC_CACHE["nc"]
    res = run_bass_kernel_spmd(nc, in_maps, core_ids=list(range(NCORES)))
    R = res.results
    y_prompt = np.concatenate([R[c]["y"][:ROWS] for c in range(NCORES)], 0)[None]
    y_sample = np.stack([R[c]["y"][ROWS:ROWS + NS] for c in range(NCORES)], 0)
    kap = np.concatenate([R[c]["ka"][:ROWS] for c in range(NCORES)], 0).reshape(1, 1, SEQ, 16, 64)
    vap = np.concatenate([R[c]["va"][:ROWS] for c in range(NCORES)], 0).reshape(1, 1, SEQ, 8, 128)
    kbp = R[NCORES - 1]["kbp"].reshape(1, 1, 512, 8, 128)
    vbp = R[NCORES - 1]["vbp"].reshape(1, 1, 512, 8, 128)
    kas = np.stack([R[c]["ka"][ROWS:ROWS + NS] for c in range(NCORES)], 0).reshape(1, NCORES, NS, 16, 64)
    vas = np.stack([R[c]["va"][ROWS:ROWS + NS] for c in range(NCORES)], 0).reshape(1, NCORES, NS, 8, 128)
    kbs = np.stack([R[c]["kbs"] for c in range(NCORES)], 0).reshape(1, NCORES, 512, 8, 128)
    vbs = np.stack([R[c]["vbs"] for c in range(NCORES)], 0).reshape(1, NCORES, 512, 8, 128)
    out = (y_prompt, y_sample, kap, vap, kbp, vbp, kas, vas, kbs, vbs)
    return tuple(np.ascontiguousarray(o.astype(np.float32)) for o in out)
```

```python
import numpy as np
import ml_dtypes
import concourse.bass as bass
import concourse.mybir as mybir
from concourse.bass_utils import run_bass_kernel_spmd

F32 = mybir.dt.float32
BF16 = mybir.dt.bfloat16
AF = mybir.ActivationFunctionType
ALU = mybir.AluOpType

NCORES = 8
D = 2048
SEQ = 8192
ROWS = 1024
NS = 32
OWN = 1152
NT_OWN = 9
PAST = 2048
WIN = 12288
NEG = -30000.0
EPS = 1e-6
LAM_INIT = 0.2
FA_LEN = 1152
GB_LEN = 768
ENGS = ("pe", "act", "dve", "pool", "sp")


class Prog:
    def __init__(self):
        self.ops = {e: [] for e in ENGS}
        self.cnt = {e: 0 for e in ENGS}
        self.last_w = {}
        self.readers = {}
        self.waited = {e: {} for e in ENGS}
        self.dma_cnt = {}

    def _deps(self, eng, reads, writes):
        deps = {}
        def add(tok):
            if tok is None:
                return
            s, v = tok
            if s == eng and eng == "pe":
                return
            if s.startswith("dma:"):
                v = 16 * self.dma_cnt[s[4:]]
            if deps.get(s, 0) < v:
                deps[s] = v
        for k in reads:
            add(self.last_w.get(k))
        for k in writes:
            add(self.last_w.get(k))
            for t in self.readers.get(k, ()):
                add(t)
        waits = []
        for s, v in deps.items():
            if self.waited[eng].get(s, 0) < v:
                self.waited[eng][s] = v
                waits.append((s, v))
        return waits

    def op(self, eng, fn, reads=(), writes=(), tag=None):
        waits = self._deps(eng, reads, writes)
        if tag is not None:
            self.dma_cnt[tag] = self.dma_cnt.get(tag, 0) + 1
            tok = ("dma:" + tag, 16 * self.dma_cnt[tag])
        else:
            self.cnt[eng] += 1
            tok = (eng, self.cnt[eng])
        self.ops[eng].append((waits, fn, tok))
        for k in writes:
            self.last_w[k] = tok
            self.readers[k] = []
        for k in reads:
            self.readers.setdefault(k, []).append(tok)
        return tok

    def barrier(self):
        allt = [(e, self.cnt[e]) for e in ENGS if self.cnt[e] > 0]
        allt += [("dma:" + t, 16 * c) for t, c in self.dma_cnt.items()]
        for e in ENGS:
            waits = []
            for s, v in allt:
                if self.waited[e].get(s, 0) < v:
                    self.waited[e][s] = v
                    waits.append((s, v))
            if waits:
                self.ops[e].append((waits, None, None))
        self.last_w = {}
        self.readers = {}

    def sem_names(self):
        names = [e for e in ENGS if e != "sp"]
        names += ["dma:" + t for t in self.dma_cnt]
        return names


class Arena:
    def __init__(self, ap_f32, nwords):
        self.ap = ap_f32
        self.n = nwords
        self.off = 0
        self.marks = []

    def push(self):
        self.marks.append(self.off)

    def pop(self):
        self.off = self.marks.pop()

    def f32(self, n):
        assert self.off + n <= self.n, ("arena overflow", self.off, n, self.n)
        a = self.ap[:, self.off:self.off + n]
        self.off += n
        return a

    def bf16(self, n):
        w = (n + 1) // 2
        return self.f32(w).bitcast(BF16)

    def top_bf16(self, n):
        w = (n + 1) // 2
        assert self.n - w >= self.off
        self.n -= w
        return self.ap[:, self.n:self.n + w].bitcast(BF16)


def _t5_bucket_np(rel):
    rel = np.asarray(rel, np.int64)
    half = 16
    max_exact = 8
    ret = np.where(rel > 0, half, 0)
    n = np.abs(rel)
    nf = np.maximum(n, 1).astype(np.float32)
    large = max_exact + (np.log(nf / np.float32(max_exact)) / np.float32(np.log(128 / max_exact))
                         * np.float32(half - max_exact)).astype(np.int32)
    large = np.minimum(large, half - 1)
    return ret + np.where(n < max_exact, n, large)


def build():
    nc = bass.Bass("TRN2", target_bir_lowering=False)

    def din(name, shape, dt=F32):
        return nc.dram_tensor(name, list(shape), dt, kind="ExternalInput").ap()

    def dout(name, shape):
        return nc.dram_tensor(name, list(shape), F32, kind="ExternalOutput").ap()

    xf = din("xf", [SEQ, D])
    xo = din("xo", [OWN, D])
    xh = din("xh", [512, D])
    w_in = din("w_in", [D, WIN])
    w_oa = din("w_oa", [1024, D])
    w_ob = din("w_ob", [1024, D])
    w_out = din("w_out", [D, D])
    cka = din("cka", [PAST, 1024])
    cva = din("cva", [PAST, 1024])
    ckb = din("ckb", [512, 1024])
    cvb = din("cvb", [512, 1024])
    pre = din("pre", [1, D])
    post = din("post", [1, D])
    subln = din("subln", [1, 128])
    lam4 = din("lam4", [1, 256])
    t5 = din("t5", [32, 8])
    relbT = din("relbT", [384, 8])
    c_ident = din("c_ident", [128, 128])
    c_antij = din("c_antij", [128, 128])
    c_oha = din("c_oha", [32, FA_LEN])
    c_ohb = din("c_ohb", [384, GB_LEN])
    c_maskA = din("c_maskA", [5, 128, 512])
    c_maskB = din("c_maskB", [5, 128, 128])
    c_visA = din("c_visA", [128, 128])
    c_visB = din("c_visB", [128, 16])

    o_y = dout("y", [1056, D])
    o_ka = dout("ka", [1056, 1024])
    o_va = dout("va", [1056, 1024])
    o_kbp = dout("kbp", [512, 1024])
    o_vbp = dout("vbp", [512, 1024])
    o_kbs = dout("kbs", [512, 1024])
    o_vbs = dout("vbs", [512, 1024])

    KTs = nc.dram_tensor("KTs", [8, 128, 2 * SEQ], BF16).ap()
    Vs = nc.dram_tensor("Vs", [8, 128, 128, 128], BF16).ap()
    fa_s = nc.dram_tensor("fa_s", [8, FA_LEN], F32).ap()
    gb_s = nc.dram_tensor("gb_s", [8, GB_LEN], F32).ap()

    P = Prog()
    pid_holder = {}

    ARENA_WORDS = 53000
    with (
        nc.sbuf_tensor("arena", [128, ARENA_WORDS], F32) as arena_t,
        nc.psum_tensor("psA", [128, 1024], F32) as psA,
        nc.psum_tensor("psB", [128, 1024], F32) as psB,
        nc.psum_tensor("psO", [128, 1536], F32) as psO,
        nc.psum_tensor("psT", [128, 1024], BF16) as psT,
    ):
        AR = Arena(arena_t[:, :], ARENA_WORDS)
        banks = []
        for nm, t, nb in (("A", psA, 2), ("B", psB, 2), ("O", psO, 3)):
            for i in range(nb):
                banks.append((t[:, 512 * i:512 * (i + 1)], ("ps", nm, i)))
        bank_rr = [0]

        def next_bank():
            b = banks[bank_rr[0] % len(banks)]
            bank_rr[0] += 1
            return b
        KT_T = ("ps", "T", 0)

        def dma(q, out, in_, reads, writes, tag):
            P.op(q, lambda E: E.dma_start(out=out, in_=in_), reads, writes, tag=tag)

        def mm(out, lhsT, rhs, start, stop, reads, writes):
            P.op("pe", lambda E: E.matmul(out, lhsT, rhs, start=start, stop=stop,
                                          skip_group_check=True), reads, writes)

        def tr(out, in_, ident, reads, writes):
            P.op("pe", lambda E: E.transpose(out, in_, ident), reads, writes)

        def act(out, in_, func, reads, writes, bias=None, scale=None, accum=None):
            kw = {}
            if bias is not None:
                kw["bias"] = bias
            if scale is not None:
                kw["scale"] = scale
            if accum is not None:
                kw["accum_out"] = accum
            P.op("act", lambda E: E.activation(out, in_, func, **kw), reads, writes)

        def ts(eng, out, in0, s1, s2, op0, op1, reads, writes, accum=None):
            if accum is None:
                P.op(eng, lambda E: E.tensor_scalar(out, in0, s1, s2, op0, op1), reads, writes)
            else:
                P.op(eng, lambda E: E.tensor_scalar(out, in0, s1, s2, op0, op1, accum_out=accum),
                     reads, writes)

        def tt(eng, out, in0, in1, op, reads, writes):
            P.op(eng, lambda E: E.tensor_tensor(out, in0, in1, op), reads, writes)

        def stt(out, in0, scalar, in1, op0, op1, reads, writes):
            P.op("dve", lambda E: E.scalar_tensor_tensor(out, in0, scalar, in1, op0, op1), reads, writes)

        def cp(eng, out, in_, reads, writes):
            if eng == "act":
                P.op("act", lambda E: E.copy(out, in_), reads, writes)
            else:
                P.op(eng, lambda E: E.tensor_copy(out, in_), reads, writes)

        def memset(eng, ap, val, writes):
            P.op(eng, lambda E: E.memset(ap, val), (), writes)

        evac_rr = [0]

        def evac(out, in_, reads, writes):
            e = ("act", "dve")[evac_rr[0] % 2]
            evac_rr[0] += 1
            cp(e, out, in_, reads, writes)

        ident_f = AR.f32(128)
        antij = AR.f32(128)
        ident_b = AR.bf16(128)
        gvec = AR.f32(2048)
        sublnb = AR.f32(128)
        lamt = AR.f32(256)
        lamtmp = AR.f32(64)
        lamc = AR.f32(8)
        visA = AR.f32(128)
        visB = AR.f32(16)
        sublnc = AR.f32(2)
        ones_f = AR.f32(128)
        onesm_f = AR.f32(128)
        ones_b = AR.bf16(128)

        dma("sp", ident_f, c_ident, (), ["ident_f"], "c0")
        dma("sp", antij, c_antij, (), ["antij"], "c0")
        dma("sp", visA, c_visA, (), ["visA"], "c0")
        dma("sp", visB, c_visB, (), ["visB"], "c0")
        dma("sp", gvec, pre.partition_broadcast(128), (), ["gvec"], "c0")
        dma("sp", sublnb, subln.partition_broadcast(128), (), ["sublnb"], "c0")
        dma("sp", lamt, lam4.partition_broadcast(128), (), ["lamt"], "c0")
        dma("sp", sublnc[:, 0:1], subln.rearrange("o d -> d o"), (), ["sublnc"], "c0")
        AR.push()
        t5_sb = AR.f32(8)
        oha_sb = AR.f32(FA_LEN)
        rb_sb = AR.f32(24)
        ohb_sb = AR.f32(3 * GB_LEN)
        vec_sb = AR.f32(FA_LEN)
        dma("sp", t5_sb[0:32, :], t5, (), ["t5_sb"], "c0")
        dma("sp", oha_sb[0:32, :], c_oha, (), ["oha_sb"], "c0")
        dma("sp", rb_sb.rearrange("p (c h) -> p c h", c=3), relbT.rearrange("(c p) h -> p c h", p=128),
            (), ["rb_sb"], "c0")
        dma("sp", ohb_sb.rearrange("p (c n) -> p c n", c=3), c_ohb.rearrange("(c p) n -> p c n", p=128),
            (), ["ohb_sb"], "c0")
        P.barrier()
        cp("dve", ident_b, ident_f, ["ident_f"], ["ident_b"])
        ts("dve", sublnb, sublnb, 1.0 - LAM_INIT, None, ALU.mult, ALU.bypass, ["sublnb"], ["sublnb"])
        ts("dve", sublnc[:, 0:1], sublnc[:, 0:1], 1.0 - LAM_INIT, None, ALU.mult, ALU.bypass, ["sublnc"], ["sublnc"])
        memset("dve", ones_f, 1.0, ["ones_f"])
        memset("dve", onesm_f, 1.0 / 128, ["onesm_f"])
        memset("dve", ones_b, 1.0, ["ones_b"])
        for i in range(2):
            P.op("dve", (lambda i: lambda E: E.scalar_tensor_tensor(
                lamtmp, lamt[:, 128 * i:128 * i + 64], 1.0, lamt[:, 128 * i + 64:128 * i + 128],
                ALU.mult, ALU.mult, accum_out=lamc[:, i:i + 1]))(i),
                ["lamt"], ["lamtmp", "lamc"])
        act(lamc[:, 2:4], lamc[:, 0:2], AF.Exp, ["lamc"], ["lamc"])
        tt("dve", lamc[:, 4:5], lamc[:, 2:3], lamc[:, 3:4], ALU.subtract, ["lamc"], ["lamc"])
        ts("dve", lamc[:, 5:6], lamc[:, 4:5], LAM_INIT, -1.0, ALU.add, ALU.mult, ["lamc"], ["lamc"])
        neg_lam = lamc[:, 5:6]

        for j in range(3):
            bk, bkey = next_bank()
            mm(bk[0:8, 0:384], t5_sb[0:32, 0:8], oha_sb[0:32, 384 * j:384 * (j + 1)], True, True,
               ["t5_sb", "oha_sb"], [bkey])
            cp("dve", vec_sb[0:8, 384 * j:384 * (j + 1)], bk[0:8, 0:384], [bkey], ["vec_sb"])
        dma("sp", fa_s, vec_sb[0:8, 0:FA_LEN], ["vec_sb"], ["fa_s"], "c1")
        for j in range(2):
            bk, bkey = next_bank()
            for c in range(3):
                mm(bk[0:8, 0:384], rb_sb[:, 8 * c:8 * c + 8], ohb_sb[:, GB_LEN * c + 384 * j:GB_LEN * c + 384 * (j + 1)],
                   c == 0, c == 2, ["rb_sb", "ohb_sb"], [bkey])
            cp("dve", vec_sb[0:8, 384 * j:384 * (j + 1)], bk[0:8, 0:384], [bkey, "fa_s"], ["vec_sb"])
        dma("sp", gb_s, vec_sb[0:8, 0:GB_LEN], ["vec_sb"], ["gb_s"], "c1")
        P.barrier()
        AR.pop()

        def norm_jobs(src, ntiles, hT, hT_key, tagbase, xs, hb, ssq, junk):
            hTv = hT.rearrange("p (c t) -> p c t", c=16)
            jobs = []

            def mk(t, part):
                def job():
                    sl = t % len(xs)
                    xk = ("xs", tagbase, sl)
                    hk = ("hb", tagbase, t % 2)
                    sk = ("ssq", tagbase, t % 2)
                    sq = ssq[:, 2 * (t % 2):2 * (t % 2) + 2]
                    if part == "norm":
                        dma("sp", xs[sl], src[128 * t:128 * (t + 1), :], (), [xk], "x%s%d" % (tagbase, sl))
                        P.op("dve", lambda E: E.scalar_tensor_tensor(
                            junk, xs[sl], 1.0, xs[sl], ALU.mult, ALU.mult, accum_out=sq[:, 0:1]),
                            [xk], [("junk", tagbase), sk])
                        act(sq[:, 0:1], sq[:, 0:1], AF.Ln, [sk], [sk], bias=EPS, scale=1.0 / D)
                        act(sq[:, 1:2], sq[:, 0:1], AF.Exp, [sk], [sk], scale=-0.5)
                        stt(hb[t % 2], xs[sl], sq[:, 1:2], gvec, ALU.mult, ALU.mult, [xk, sk, "gvec"], [hk])
                        return
                    half = part
                    for j in range(8):
                        dc = 8 * half + j
                        tr(psT[:, 128 * j:128 * (j + 1)], hb[t % 2][:, 128 * dc:128 * (dc + 1)], ident_b,
                           [hk, "ident_b"], [KT_T])
                    evac(hTv[:, 8 * half:8 * half + 8, 128 * t:128 * (t + 1)],
                         psT.rearrange("p (c t) -> p c t", c=8), [KT_T], [(hT_key, t)])
                return job
            for t in range(ntiles):
                if t == 0:
                    jobs.append(mk(0, "norm"))
                if t + 1 < ntiles:
                    jobs.append(mk(t + 1, "norm"))
                jobs.append(mk(t, 0))
                jobs.append(mk(t, 1))
            return jobs

        def norm_tiles(src, ntiles, hT, hT_key, tagbase, xs, hb, ssq, junk):
            for j in norm_jobs(src, ntiles, hT, hT_key, tagbase, xs, hb, ssq, junk):
                j()

        AR.push()
        Wkv = AR.bf16(16 * 2048)
        wst = [AR.f32(2048) for _ in range(2)]
        xs1 = [AR.f32(2048) for _ in range(3)]
        hb1 = [AR.bf16(2048) for _ in range(2)]
        ssq1 = AR.f32(4)
        junk1 = AR.bf16(2048)
        hTb = [AR.bf16(16 * 512) for _ in range(2)]
        ktb = [AR.bf16(8 * 512) for _ in range(2)]
        vbk = [AR.bf16(4 * 1024) for _ in range(2)]
        Wkvv = Wkv.rearrange("p (c n) -> p c n", c=16)
        for dc in range(16):
            sl = dc % 2
            dma("pool", Wkvv[:, dc, :], w_in[128 * dc:128 * (dc + 1), 1024:3072], (), [("Wkv", dc)], "wkv")
        Wkeys = [("Wkv", dc) for dc in range(16)]
        KTsv = KTs.rearrange("h p n -> p h n")
        def p1_jobs(b):
            return norm_jobs(xf[512 * b:512 * (b + 1), :], 4, hTb[b % 2], ("hTb", b % 2), "p1", xs1, hb1, ssq1, junk1)
        for j in p1_jobs(0):
            j()
        for b in range(16):
            hT = hTb[b % 2]
            hTkey = ("hTb", b % 2)
            pending = p1_jobs(b + 1) if b + 1 < 16 else []
            hTv = hT.rearrange("p (c t) -> p c t", c=16)
            hkeys = [(hTkey, t) for t in range(4)]
            kt = ktb[b % 2].rearrange("p (h t) -> p h t", h=8)
            ktk = ("ktb", b % 2)
            vb_ = vbk[b % 2].rearrange("p (t n) -> p t n", t=4)
            vk = ("vbk", b % 2)
            gcount = [0]

            def after_group():
                gcount[0] += 1
                if pending:
                    pending.pop(0)()
            for h in range(8):
                bk, bkey = next_bank()
                for dc in range(16):
                    mm(bk, Wkvv[:, dc, 128 * h:128 * (h + 1)], hTv[:, dc, :], dc == 0, dc == 15,
                       hkeys + [Wkeys[dc]], [bkey])
                evac(kt[:, h, :], bk, [bkey], [(ktk, h)])
                after_group()
            for rep in range(2):
                dma("pool", KTsv[:, :, rep * SEQ + 512 * b:rep * SEQ + 512 * (b + 1)], kt,
                    [(ktk, h) for h in range(8)], [("KTs", b, rep)], "kts%d" % (b % 2))
            for t in range(4):
                for half in range(2):
                    bk, bkey = next_bank()
                    for dc in range(16):
                        mm(bk, hTv[:, dc, 128 * t:128 * (t + 1)], Wkvv[:, dc, 1024 + 512 * half:1024 + 512 * (half + 1)],
                           dc == 0, dc == 15, hkeys + [Wkeys[dc]], [bkey])
                    evac(vb_[:, t, 512 * half:512 * (half + 1)], bk, [bkey], [(vk, t, half)])
                    after_group()
                for rep in range(2):
                    dma("pool", Vs[:, :, 64 * rep + 4 * b + t, :].rearrange("h p d -> p h d"),
                        vb_[:, t, :].rearrange("p (h d) -> p h d", h=8),
                        [(vk, t, 0), (vk, t, 1)], [("Vs", b, t, rep)], "vs%d" % (b % 2))
            while pending:
                pending.pop(0)()
        P.barrier()
        AR.pop()

        ozTa = AR.bf16(8 * OWN)
        ozTav = ozTa.rearrange("p (h t) -> p h t", h=8)

        def load_wgroup(wsrc_cols, ncol, wst_f, wbf, key, tag, nk=16, stkey=None):
            half = nk // 2
            for hh in range(2):
                dma("pool", wbf[:, hh * half * ncol:(hh + 1) * half * ncol].rearrange("p (c n) -> p c n", c=half),
                    wsrc_cols[128 * half * hh:128 * half * (hh + 1), :].rearrange("(c p) n -> p c n", p=128),
                    (), [(key, "bf", hh)], tag)
            return [(key, "bf", 0), (key, "bf", 1)]

        def own_prep(AR, with_halo):
            hTo = AR.bf16(16 * OWN)
            hTh = AR.bf16(16 * 512) if with_halo else None
            AR.push()
            xs = [AR.f32(2048) for _ in range(2)]
            hb = [AR.bf16(2048) for _ in range(2)]
            ssq = AR.f32(4)
            junk = AR.bf16(2048)
            norm_tiles(xo, NT_OWN, hTo, "hTo", "own", xs, hb, ssq, junk)
            if with_halo:
                norm_tiles(xh, 4, hTh, "hTh", "own", xs, hb, ssq, junk)
            P.barrier()
            AR.pop()
            return hTo, hTh

        tok_blocks = [(0, 512), (512, 512), (1024, 128)]

        AR.push()
        qaT = AR.bf16(8 * OWN)
        szga = AR.bf16(8 * OWN)
        kaTs = AR.bf16(8 * 32)
        vas = AR.bf16(8 * 130)
        qaTv = qaT.rearrange("p (h t) -> p h t", h=8)
        szgaTv = szga.rearrange("p (h t) -> p h t", h=8)
        kaTsv = kaTs.rearrange("p (h t) -> p h t", h=8)
        vasv = vas.rearrange("p (h d) -> p h d", h=8)
        memset("pool", vasv[:, :, 128:130], 1.0, ["vas1"])

        AR.push()
        hTo, _ = own_prep(AR, False)
        hTov = hTo.rearrange("p (c t) -> p c t", c=16)
        hokeys = [("hTo", t) for t in range(NT_OWN)]
        GC = 256
        wstf = [None, None]
        wbfs = [AR.bf16(16 * GC) for _ in range(2)]
        ost = [AR.f32(GC) for _ in range(3)]
        sgt = [AR.f32(GC) for _ in range(2)]
        ost_i = [0]

        def feat_major(wbf, wkeys, ncol, col0, dst_fn, scale, tokv, tokkeys, blocks, evac_fn=None):
            wv = wbf.rearrange("p (c n) -> p c n", c=16)
            for cc in range(ncol // 128):
                for (t0, n) in blocks:
                    bk, bkey = next_bank()
                    for dc in range(16):
                        mm(bk[:, 0:n], wv[:, dc, 128 * cc:128 * (cc + 1)], tokv[:, dc, t0:t0 + n], dc == 0, dc == 15,
                           wkeys + tokkeys, [bkey])
                    dst, dkey = dst_fn(col0 + 128 * cc, t0, n)
                    if evac_fn is not None:
                        evac_fn(dst, dkey, bk[:, 0:n], bkey, n)
                    elif scale is None:
                        evac(dst, bk[:, 0:n], [bkey], [dkey])
                    else:
                        ts("dve", dst, bk[:, 0:n], scale, None, ALU.mult, ALU.bypass, [bkey], [dkey])

        def tok_major(wbf, wkeys, ncol, tokv, tokkeys, ntiles, sink):
            wv = wbf.rearrange("p (c n) -> p c n", c=16)
            for t in range(ntiles):
                bk, bkey = next_bank()
                for dc in range(16):
                    mm(bk[:, 0:ncol], tokv[:, dc, 128 * t:128 * (t + 1)], wv[:, dc, :], dc == 0, dc == 15,
                       wkeys + tokkeys, [bkey])
                sink(t, bk[:, 0:ncol], bkey)

        def out_rows(dst, col0, ncol, src_ap, skey, t, nrows_total):
            r0 = 128 * t
            n = min(128, nrows_total - r0)
            if n <= 0:
                return
            dma("sp", dst[r0:r0 + n, col0:col0 + ncol], src_ap[0:n, :], [skey], [("out", id(dst), t, col0)], "outst%d" % skey[1])

        def silu_to(dst, psum, bkey, dkey, mulb=None, mkey=None):
            if mulb is None:
                act(dst, psum, AF.Silu, [bkey], [dkey])
                return
            s_ = sgt[ost_i[0] % 2]
            sk = ("sgt", ost_i[0] % 2)
            ost_i[0] += 1
            n = psum.shape[-1]
            act(s_[:, 0:n], psum, AF.Silu, [bkey], [sk])
            tt("dve", dst, s_[:, 0:n], mulb, ALU.mult, [sk, mkey], [dkey])

        gi = [0]

        def wgroup(col0):
            sl = gi[0] % 2
            gi[0] += 1
            keys = load_wgroup(w_in[:, col0:col0 + GC], GC, wstf[sl], wbfs[sl], ("wg", sl), "wg%d" % sl)
            return wbfs[sl], keys

        sgt2 = [AR.f32(512) for _ in range(2)]

        for g in range(4096 // GC):
            col0 = g * GC
            wbf, wkeys = wgroup(col0)
            if col0 < 1024:
                feat_major(wbf, wkeys, GC, col0,
                           lambda c, t0, n: (qaTv[:, c // 128, t0:t0 + n], ("qaT", c // 128, t0)),
                           0.125, hTov, hokeys, tok_blocks)
            elif col0 < 2048:
                c1 = col0 - 1024

                def sink(t, ps, bkey, c1=c1):
                    o = ost[ost_i[0] % 3]
                    ok = ("ost", ost_i[0] % 3)
                    ost_i[0] += 1
                    evac(o[:, 0:GC], ps, [bkey], [ok])
                    out_rows(o_ka, c1, GC, o, ok, t, 1056)
                tok_major(wbf, wkeys, GC, hTov, hokeys, NT_OWN, sink)
                feat_major(wbf, wkeys, GC, c1,
                           lambda c, t0, n: (kaTsv[:, c // 128, 0:32], ("kaTs", c // 128)),
                           None, hTov, hokeys, [(1024, 32)])
            elif col0 < 3072:
                c1 = col0 - 2048

                def sink(t, ps, bkey, c1=c1):
                    o = ost[ost_i[0] % 3]
                    ok = ("ost", ost_i[0] % 3)
                    ost_i[0] += 1
                    evac(o[:, 0:GC], ps, [bkey], [ok])
                    out_rows(o_va, c1, GC, o, ok, t, 1056)
                    if t == 8:
                        cp("pool", vasv[0:32, c1 // 128:c1 // 128 + 2, 0:128],
                           o[0:32, 0:GC].rearrange("p (h d) -> p h d", h=2), [ok], [("vas", c1)])
                tok_major(wbf, wkeys, GC, hTov, hokeys, NT_OWN, sink)
            else:
                c1 = col0 - 3072

                def zevac(dst, dkey, ps, bkey, n):
                    s_ = sgt2[ost_i[0] % 2]
                    sk = ("sgt2", ost_i[0] % 2)
                    ost_i[0] += 1
                    act(s_[:, 0:n], ps, AF.Silu, [bkey], [sk])
                    ts("dve", dst, s_[:, 0:n], sublnc[:, 0:1], None, ALU.mult, ALU.bypass, [sk, "sublnc"], [dkey])
                feat_major(wbf, wkeys, GC, c1,
                           lambda c, t0, n: (szgaTv[:, c // 128, t0:t0 + n], ("szgaT", c // 128, t0)),
                           None, hTov, hokeys, tok_blocks, evac_fn=zevac)
        P.barrier()
        AR.pop()

        def attention_unit(nmaps, nq, qT_fn, tiles, Pbufs, tmpb, epilogue, ukey, abase=0):
            nsb = (nq + 127) // 128
            T = len(tiles)
            Sps = [(psA, [("ps", "A", 0), ("ps", "A", 1)]), (psB, [("ps", "B", 0), ("ps", "B", 1)])]
            Okeys = [("ps", "O", 0), ("ps", "O", 1), ("ps", "O", 2)]

            def qk(t):
                tl = tiles[t]
                sp, skeys = Sps[t % 2]
                for m in range(nmaps):
                    mm(sp[0:tl["nk"], 512 * m:512 * m + nq], tl["kT"](m), qT_fn(m), True, True,
                       tl["keys"], [skeys[m]])

            def softmax_exp(t):
                tl = tiles[t]
                nk = tl["nk"]
                sp, skeys = Sps[t % 2]
                pb = Pbufs[t % len(Pbufs)]
                pk = ("Pb", t % len(Pbufs))
                pbv = pb.rearrange("p (m q) -> p m q", m=2)
                spv = sp.rearrange("p (m q) -> p m q", m=2)
                if tl["bias"] is not None:
                    tb = tmpb[t % len(tmpb)]
                    tk = ("tmpb", t % len(tmpb))
                    tbv = tb.rearrange("p (m q) -> p m q", m=2)
                    for m in range(nmaps):
                        tt("dve", tbv[0:nk, m, 0:nq], spv[0:nk, m, 0:nq], tl["bias"][0:nk, 0:nq], ALU.add,
                           [skeys[m], tl["biaskey"]], [tk])
                    act(pbv[0:nk, 0:nmaps, 0:nq], tbv[0:nk, 0:nmaps, 0:nq], AF.Exp, [tk, "visA", "visB"], [pk],
                        bias=tl["vis"])
                else:
                    act(pbv[0:nk, 0:nmaps, 0:nq], spv[0:nk, 0:nmaps, 0:nq], AF.Exp,
                        skeys[0:nmaps] + ["visA", "visB"], [pk], bias=tl["vis"])

            def pv(t):
                tl = tiles[t]
                nk = tl["nk"]
                pb = Pbufs[t % len(Pbufs)]
                pk = ("Pb", t % len(Pbufs))
                pbv = pb.rearrange("p (m q) -> p m q", m=2)
                for m in range(nmaps):
                    for sb in range(nsb):
                        a = abase + m * nsb + sb
                        nqs = min(128, nq - 128 * sb)
                        first_in_bank = (t == 0 and a % 3 == 0)
                        mm(psO[0:nqs, 512 * (a // 3) + 130 * (a % 3):512 * (a // 3) + 130 * (a % 3) + 130],
                           pbv[0:nk, m, 128 * sb:128 * sb + nqs], tl["v"], first_in_bank, t == T - 1,
                           [pk] + tl["keys"], [Okeys[a // 3]])

            qk(0)
            for t in range(T):
                if t + 1 < T:
                    qk(t + 1)
                softmax_exp(t)
                if t >= 1:
                    pv(t - 1)
            pv(T - 1)
            epilogue(nsb, Okeys, abase)

        def attention_unit_T(nq, qT_fn, tiles, Pbufs, tmpb, lacc, eA, eB, h, row0):
            T = len(tiles)
            Sps = [(psA, [("ps", "A", 0), ("ps", "A", 1)]), (psB, [("ps", "B", 0), ("ps", "B", 1)])]
            Okeys = [("ps", "O", 0), ("ps", "O", 1)]
            laccv = lacc.rearrange("p (m q) -> p m q", m=2)
            lk = "lacc"
            Lps = [psO[:, 1024:1536], psT.bitcast(F32)]
            Lkeys = [("ps", "O", 2), KT_T]
            dve_started = [False]

            def qk(t):
                tl = tiles[t]
                sp, skeys = Sps[t % 2]
                for m in range(2):
                    mm(sp[0:tl["nk"], 512 * m:512 * m + nq], tl["kT"](m), qT_fn(m), True, True,
                       tl["keys"], [skeys[m]])

            def softmax_exp(t):
                tl = tiles[t]
                nk = tl["nk"]
                sp, skeys = Sps[t % 2]
                pb = Pbufs[t % len(Pbufs)]
                pk = ("Pb", t % len(Pbufs))
                pbv = pb.rearrange("p (m q) -> p m q", m=2)
                spv = sp.rearrange("p (m q) -> p m q", m=2)
                if tl["bias"] is not None:
                    tb = tmpb[t % len(tmpb)]
                    tk = ("tmpb", t % len(tmpb))
                    tbv = tb.rearrange("p (m q) -> p m q", m=2)
                    for m in range(2):
                        tt("dve", tbv[0:nk, m, 0:nq], spv[0:nk, m, 0:nq], tl["bias"][0:nk, 0:nq], ALU.add,
                           [skeys[m], tl["biaskey"]], [tk])
                    act(pbv[0:nk, :, 0:nq], tbv[0:nk, :, 0:nq], AF.Exp, [tk, "visA", "visB"], [pk], bias=tl["vis"])
                else:
                    act(pbv[0:nk, :, 0:nq], spv[0:nk, :, 0:nq], AF.Exp, skeys + ["visA", "visB"], [pk], bias=tl["vis"])

            def pv(t):
                tl = tiles[t]
                nk = tl["nk"]
                pb = Pbufs[t % len(Pbufs)]
                pk = ("Pb", t % len(Pbufs))
                pbv = pb.rearrange("p (m q) -> p m q", m=2)
                on_pe = (t % 3 == 0)
                if on_pe:
                    for m in range(2):
                        mm(Lps[m][:, 0:nq], ones_b[0:nk, :], pbv[0:nk, m, 0:nq], t == 0, False,
                           [pk, "ones_b"], [Lkeys[m]])
                elif not dve_started[0]:
                    dve_started[0] = True
                    cp("dve", laccv[0:nk, :, 0:nq], pbv[0:nk, :, 0:nq], [pk], [lk])
                else:
                    tt("dve", laccv[0:nk, :, 0:nq], laccv[0:nk, :, 0:nq], pbv[0:nk, :, 0:nq], ALU.add, [pk, lk], [lk])
                for m in range(2):
                    mm(psO[:, 512 * m:512 * m + nq], tl["v"][0:nk, 0:128], pbv[0:nk, m, 0:nq], t == 0, t == T - 1,
                       [pk] + tl["keys"], [Okeys[m]])

            qk(0)
            for t in range(T):
                if t + 1 < T:
                    qk(t + 1)
                softmax_exp(t)
                if t >= 1:
                    pv(t - 1)
            pv(T - 1)
            sp, skeys = Sps[T % 2]
            spv = sp.rearrange("p (m q) -> p m q", m=2)
            eAv = eA.rearrange("p (m q) -> p m q", m=2)
            eBv = eB.rearrange("p (m q) -> p m q", m=2)
            for m in range(2):
                mm(Lps[m][:, 0:nq], ones_f, laccv[:, m, 0:nq], False, True, [lk, "ones_f"], [Lkeys[m]])
            for m in range(2):
                act(eAv[:, m, 0:nq], Lps[m][:, 0:nq], AF.Ln, [Lkeys[m]], ["eA"])
            act(eAv[:, :, 0:nq], eAv[:, :, 0:nq], AF.Exp, ["eA"], ["eA"], scale=-1.0)
            tt("dve", eBv[:, 1, 0:nq], psO[:, 512:512 + nq], eAv[:, 1, 0:nq], ALU.mult, [Okeys[1], "eA"], ["eB"])
            tt("dve", eBv[:, 0, 0:nq], psO[:, 0:nq], eAv[:, 0, 0:nq], ALU.mult, [Okeys[0], "eA"], ["eB"])
            stt(eBv[:, 0, 0:nq], eBv[:, 1, 0:nq], neg_lam, eBv[:, 0, 0:nq], ALU.mult, ALU.add, ["eB", "lamc"], ["eB"])
            tt("dve", eBv[:, 1, 0:nq], eBv[:, 0, 0:nq], eBv[:, 0, 0:nq], ALU.mult, ["eB"], ["eB"])
            mm(sp[:, 0:nq], onesm_f, eBv[:, 1, 0:nq], True, True, ["eB", "onesm_f"], [skeys[0]])
            act(eAv[:, 0, 0:nq], sp[:, 0:nq], AF.Ln, [skeys[0]], ["eA"], bias=EPS)
            act(eAv[:, 0, 0:nq], eAv[:, 0, 0:nq], AF.Exp, ["eA"], ["eA"], scale=-0.5)
            tt("dve", eBv[:, 0, 0:nq], eBv[:, 0, 0:nq], eAv[:, 0, 0:nq], ALU.mult, ["eB", "eA"], ["eB"])
            szk = [("szgaT", h, b0) for b0 in (0, 512, 1024) if b0 < row0 + nq and b0 + 512 > row0]
            tt("dve", ozTav[:, h, row0:row0 + nq], eBv[:, 0, 0:nq], szgaTv[:, h, row0:row0 + nq], ALU.mult,
               ["eB"] + szk, [("ozTa", h, row0)])

        def acc_ap(a, n):
            return psO[0:n, 512 * (a // 3) + 130 * (a % 3):512 * (a // 3) + 130 * (a % 3) + 130]

        def make_epilogue_A(h, row0, nq, osb, ep):
            def epilogue(nsb, Okeys):
                for sb in range(nsb):
                    n = min(128, nq - 128 * sb)
                    a0, a1 = sb, nsb + sb
                    k = ("ep", sb % 2)
                    e = ep[sb % 2]
                    o1 = osb[sb % 2]
                    cp("act", o1[0:n, 0:130], acc_ap(a0, n), [Okeys[a0 // 3]], [k])
                    cp("act", o1[0:n, 130:260], acc_ap(a1, n), [Okeys[a1 // 3]], [k])
                    P.op("dve", lambda E, e=e, o1=o1, n=n: E.reciprocal(e[0:n, 0:1], o1[0:n, 128:129]), [k], [k])
                    P.op("dve", lambda E, e=e, o1=o1, n=n: E.reciprocal(e[0:n, 1:2], o1[0:n, 258:259]), [k], [k])
                    tt("dve", e[0:n, 1:2], e[0:n, 1:2], neg_lam[0:n, :], ALU.mult, [k, "lamc"], [k])
                    ts("dve", o1[0:n, 130:258], o1[0:n, 130:258], e[0:n, 1:2], None, ALU.mult, ALU.bypass, [k], [k])
                    stt(o1[0:n, 0:128], o1[0:n, 0:128], e[0:n, 0:1], o1[0:n, 130:258], ALU.mult, ALU.add, [k], [k])
                    P.op("dve", lambda E, e=e, o1=o1, n=n: E.scalar_tensor_tensor(
                        o1[0:n, 130:258], o1[0:n, 0:128], 1.0, o1[0:n, 0:128], ALU.mult, ALU.mult,
                        accum_out=e[0:n, 2:3]), [k], [k])
                    act(e[0:n, 2:3], e[0:n, 2:3], AF.Ln, [k], [k], bias=EPS, scale=1.0 / 128)
                    act(e[0:n, 3:4], e[0:n, 2:3], AF.Exp, [k], [k], scale=-0.5)
                    r = row0 + 128 * sb
                    tix, rr = r // 128, r % 128
                    ob = o1[:, 260:324].bitcast(BF16)
                    stt(ob[0:n, :], o1[0:n, 0:128], e[0:n, 3:4], szgav[rr:rr + n, tix, 128 * h:128 * (h + 1)],
                        ALU.mult, ALU.mult, [k, ("szga", tix, (128 * h) // GC * GC)], [k])
                    tr(psT[:, 0:n], ob[0:n, :], ident_b[0:n, 0:n], [k, "ident_b"], [KT_T])
                    cp("act", ozTav[:, h, r:r + n], psT[:, 0:n], [KT_T], [("ozTa", h, r)])
            return epilogue

        def hankel_dma(hk, hkkey, src, offset, nq):
            hsrc = bass.AP(tensor=src.tensor, offset=src.offset + offset, ap=[[1, 128], [1, nq]])
            dma("sp", hk[:, 0:nq], hsrc, ["fa_s", "gb_s"], [hkkey], "hk_" + str(hkkey))

        def hankel_flip(dstb, dkey, hk, hkkey, nq, mask_ap, maskkey, nk=128, bank_idx=6):
            bk, bkey = banks[bank_idx]
            mm(bk[:, 0:nq], antij, hk[:, 0:nq], True, True, [hkkey, "antij"], [bkey])
            if mask_ap is None:
                cp("dve", dstb[0:nk, 0:nq], bk[0:nk, 0:nq], [bkey], [dkey])
            else:
                tt("dve", dstb[0:nk, 0:nq], bk[0:nk, 0:nq], mask_ap[0:nk, 0:nq], ALU.add, [bkey, maskkey], [dkey])

        def load_bias_tile(dstb, dkey, hk, hkkey, src, offset, nq, mask_ap, maskkey, nk=128, bank_idx=6):
            hankel_dma(hk, hkkey, src, offset, nq)
            hankel_flip(dstb, dkey, hk, hkkey, nq, mask_ap, maskkey, nk=nk, bank_idx=bank_idx)

        AR.push()
        Pbufs = [AR.bf16(1024) for _ in range(4)]
        tmpb = [AR.f32(1024) for _ in range(2)]
        lacc = AR.f32(1024)
        eA = AR.f32(1024)
        eB = AR.f32(1024)
        hkb = AR.f32(512)
        hkA = [hkb] + [AR.f32(512) for _ in range(4)]
        maskA = AR.f32(5 * 512)
        maskAv = maskA.rearrange("p (s q) -> p s q", s=5)
        dma("sp", maskAv, c_maskA.rearrange("s p q -> p s q"), (), ["maskA"], "c0")
        AR.push()
        biasA = [AR.f32(5 * 512) for _ in range(2)]
        KTw = [AR.bf16(64 * 128) for _ in range(2)]
        Vw = [AR.bf16(64 * 130) for _ in range(2)]
        for i in range(2):
            memset("pool", Vw[i].rearrange("p (t d) -> p t d", t=64)[:, :, 128:130], 1.0, [("Vw1", i)])

        def p3_dyn(E):
            if "pid" not in pid_holder:
                pid_holder["pid"] = E.partition_id()
            return pid_holder["pid"]

        def biasA_dma(h):
            for s_ in range(5):
                hankel_dma(hkA[s_], ("hkA", s_), fa_s[h:h + 1, :], 512 - 128 * s_, 512)

        def biasA_flip(h):
            bAv_ = biasA[h % 2].rearrange("p (s q) -> p s q", s=5)
            for s_ in range(5):
                hankel_flip(bAv_[:, s_, :], ("biasA", h % 2, s_), hkA[s_], ("hkA", s_), 512, maskAv[:, s_, :], "maskA")

        biasA_dma(0)
        biasA_flip(0)
        u = 0
        for h in range(8):
            bA = biasA[h % 2]
            bAv = bA.rearrange("p (s q) -> p s q", s=5)
            for qb in range(2):
                sl = u % 2
                ktw = KTw[sl]
                vw = Vw[sl].rearrange("p (t d) -> p t d", t=64)

                dma("pool", ktw, KTs[h, :, 512 * qb:512 * qb + 64 * 128], (), [("KTw", sl)], "ktw%d" % sl)
                for half in range(2):
                    dma("pool", vw[:, 32 * half:32 * half + 32, 0:128],
                        Vs[h, :, 4 * qb + 32 * half:4 * qb + 32 * half + 32, :], (), [("Vw", sl, half)],
                        "vw%d" % sl)
                tiles = []
                for s in range(64):
                    tiles.append(dict(
                        kT=(lambda m, s=s, ktw=ktw: ktw[64 * m:64 * m + 64, 128 * s:128 * (s + 1)]),
                        v=vw[:, s, :], nk=128,
                        bias=(bAv[:, s, :] if s < 5 else None), biaskey=("biasA", h % 2, s),
                        vis=visA[:, 64 * qb + s:64 * qb + s + 1],
                        keys=[("KTw", sl), ("Vw", sl, s // 32), ("Vw1", sl)]))
                r0 = 512 * qb
                ukey = ("qaT", h, r0)
                attention_unit_T(512, (lambda m, h=h, r0=r0: qaTv[64 * m:64 * m + 64, h, r0:r0 + 512]),
                                 tiles, Pbufs, tmpb, lacc, eA, eB, h, r0)
                u += 1
                if h + 1 < 8:
                    if qb == 0:
                        biasA_dma(h + 1)
                    else:
                        biasA_flip(h + 1)
        P.barrier()
        AR.pop()

        AR.push()
        KTc = AR.bf16(8 * PAST)
        Vc = AR.bf16(16 * 8 * 130)
        cst = [AR.f32(1024) for _ in range(2)]
        KTcv = KTc.rearrange("p (h n) -> p h n", h=8)
        Vcv = Vc.rearrange("p (t h d) -> p t h d", t=16, h=8)
        memset("pool", Vcv[:, :, :, 128:130], 1.0, ["Vc1"])
        psTf = [b for b in banks]
        for t in range(16):
            sl = t % 2
            dma("sp", cst[sl], cka[128 * t:128 * (t + 1), :], (), [("cst", sl)], "cst%d" % sl)
            for hh in range(2):
                bk, bkey = next_bank()
                for j in range(4):
                    h = 4 * hh + j
                    tr(bk[:, 128 * j:128 * (j + 1)], cst[sl][:, 128 * h:128 * (h + 1)], ident_f, [("cst", sl), "ident_f"],
                       [bkey])
                evac(KTcv[:, 4 * hh:4 * hh + 4, 128 * t:128 * (t + 1)], bk.rearrange("p (h n) -> p h n", h=4),
                     [bkey], [("KTc", t)])
        for t in range(16):
            sl = t % 2
            dma("sp", cst[sl], cva[128 * t:128 * (t + 1), :], (), [("cst", sl)], "cst%d" % sl)
            cp("pool", Vcv[:, t, :, 0:128], cst[sl].rearrange("p (h d) -> p h d", h=8), [("cst", sl)], [("Vc", t)])
        bsAll = AR.f32(2 * 8 * 32)
        bsAllv = bsAll.rearrange("p (j h q) -> p j h q", j=2, h=8)
        for h in range(8):
            load_bias_tile(bsAllv[:, 0, h, :], ("bsAll", 0), hkb, "hkb", fa_s[h:h + 1, :], 512, 32, None, None)
            load_bias_tile(bsAllv[:, 1, h, :], ("bsAll", 1), hkb, "hkb", fa_s[h:h + 1, :], 384, 32, None, None, nk=32)
        SpsS = [(psA, [("ps", "A", 0), ("ps", "A", 1)]), (psB, [("ps", "B", 0), ("ps", "B", 1)])]
        oTk, Lk, Mk = ("ps", "O", 0), ("ps", "O", 1), ("ps", "O", 2)
        TS = 17

        def sa_nk(t):
            return 128 if t < 16 else 32

        def sa_qk(t):
            sp, skeys = SpsS[t % 2]
            nk = sa_nk(t)
            for m in range(2):
                for h in range(8):
                    kT = KTcv[64 * m:64 * m + 64, h, 128 * t:128 * (t + 1)] if t < 16 else kaTsv[64 * m:64 * m + 64, h, 0:32]
                    mm(sp[0:nk, 512 * m + 32 * h:512 * m + 32 * h + 32], kT, qaTv[64 * m:64 * m + 64, h, 1024:1056],
                       True, True, [("KTc", t), ("kaTs", h)], [skeys[m]])

        def sa_exp(t):
            sp, skeys = SpsS[t % 2]
            nk = sa_nk(t)
            spv = sp.rearrange("p (m c) -> p m c", m=2)[:, :, 0:256]
            pb = Pbufs[t % len(Pbufs)][:, 0:512]
            pbv = pb.rearrange("p (m c) -> p m c", m=2)
            pk = ("Pb", t % len(Pbufs))
            if t >= 15:
                tb = tmpb[t % 2][:, 0:512]
                tk = ("tmpb", t % 2)
                tbv = tb.rearrange("p (m c) -> p m c", m=2)
                for m in range(2):
                    tt("dve", tbv[0:nk, m, :], spv[0:nk, m, :],
                       bsAllv[0:nk, t - 15, :, :].rearrange("p h q -> p (h q)"), ALU.add,
                       [skeys[m], ("bsAll", t - 15)], [tk])
                act(pbv[0:nk, :, :], tbv[0:nk, :, :], AF.Exp, [tk, "visB"], [pk], bias=visB[0:nk, 15:16])
            else:
                act(pbv[0:nk, :, :], spv[0:nk, :, :], AF.Exp, skeys + ["visB"], [pk], bias=visB[0:nk, 15:16])

        def sa_pv(t):
            nk = sa_nk(t)
            pb = Pbufs[t % len(Pbufs)][:, 0:512]
            pk = ("Pb", t % len(Pbufs))
            mm(psO[:, 512:1024], ones_b[0:nk, :], pb[0:nk, :], t == 0, t == TS - 1, [pk, "ones_b"], [Lk])
            first = True
            for h in range(8):
                v = Vcv[:, t, h, 0:128] if t < 16 else vasv[0:32, h, 0:128]
                for m in range(2):
                    c0 = 256 * m + 32 * h
                    mm(psO[:, c0:c0 + 32], v, pb[0:nk, c0:c0 + 32], t == 0 and first, t == TS - 1,
                       [pk, ("Vc", t), "Vc1", ("vas", (128 * h) // GC * GC), "vas1"], [oTk])
                    first = False

        sa_qk(0)
        for t in range(TS):
            if t + 1 < TS:
                sa_qk(t + 1)
            sa_exp(t)
            if t >= 1:
                sa_pv(t - 1)
        sa_pv(TS - 1)
        r_ = eA[:, 0:512]
        t_ = eB[:, 0:512]
        o_ = eA[:, 512:768]
        sq_ = eA[:, 768:1024]
        rn_ = eB[:, 512:768]
        act(r_, psO[:, 512:1024], AF.Ln, [Lk], ["eA"])
        act(r_, r_, AF.Exp, ["eA"], ["eA"], scale=-1.0)
        tt("dve", t_, psO[:, 0:512], r_, ALU.mult, [oTk, "eA"], ["eB"])
        stt(o_, t_[:, 256:512], neg_lam, t_[:, 0:256], ALU.mult, ALU.add, ["eB", "lamc"], ["eA"])
        tt("dve", sq_, o_, o_, ALU.mult, ["eA"], ["eA"])
        mm(psO[:, 1024:1280], onesm_f, sq_, True, True, ["eA", "onesm_f"], [Mk])
        act(rn_, psO[:, 1024:1280], AF.Ln, [Mk], ["eB"], bias=EPS)
        act(rn_, rn_, AF.Exp, ["eB"], ["eB"], scale=-0.5)
        tt("dve", o_, o_, rn_, ALU.mult, ["eA", "eB"], ["eA"])
        tt("dve", ozTav[:, :, 1024:1056], o_.rearrange("p (h q) -> p h q", h=8), szgaTv[:, :, 1024:1056], ALU.mult,
           ["eA"] + [("szgaT", h, 1024) for h in range(8)], [("ozTa", "s")])
        P.barrier()
        AR.pop()
        AR.pop()
        AR.pop()

        AR.push()
        qbT = AR.bf16(8 * OWN)
        szb = AR.bf16(NT_OWN * 1024)
        kbT = AR.bf16(8 * 1664)
        vbb = AR.bf16(13 * 8 * 130)
        qbTv = qbT.rearrange("p (h t) -> p h t", h=8)
        szbv = szb.rearrange("p (t n) -> p t n", t=NT_OWN)
        kbTv = kbT.rearrange("p (h t) -> p h t", h=8)
        vbv = vbb.rearrange("p (t h d) -> p t h d", t=13, h=8)
        memset("pool", vbv[:, :, :, 128:130], 1.0, ["vb1"])

        AR.push()
        hTo, hTh = own_prep(AR, True)
        hTov = hTo.rearrange("p (c t) -> p c t", c=16)
        hThv = hTh.rearrange("p (c t) -> p c t", c=16)
        hokeys = [("hTo", t) for t in range(NT_OWN)]
        hhkeys = [("hTh", t) for t in range(4)]
        GC = 256
        wstf = [None, None]
        wbfs = [AR.bf16(16 * GC) for _ in range(2)]
        ost = [AR.f32(GC) for _ in range(3)]
        sgt = [AR.f32(GC) for _ in range(2)]
        qscale = float(128 ** -0.5)
        for g in range(4096 // GC):
            col0 = 4096 + g * GC
            c1 = (g * GC) % 1024
            wbf, wkeys = wgroup(col0)
            if g * GC < 1024:
                feat_major(wbf, wkeys, GC, c1,
                           lambda c, t0, n: (qbTv[:, c // 128, t0:t0 + n], ("qbT", c // 128, t0)),
                           qscale, hTov, hokeys, tok_blocks)
            elif g * GC < 2048:
                feat_major(wbf, wkeys, GC, c1,
                           lambda c, t0, n: (kbTv[:, c // 128, t0:t0 + n], ("kbT", c // 128, t0)),
                           None, hThv, hhkeys, [(0, 512)])
                feat_major(wbf, wkeys, GC, c1,
                           lambda c, t0, n: (kbTv[:, c // 128, 512 + t0:512 + t0 + n], ("kbT", c // 128, 512 + t0)),
                           None, hTov, hokeys, tok_blocks)

                def sink(t, ps, bkey, c1=c1):
                    if t < 4:
                        return
                    o = ost[ost_i[0] % 3]
                    ok = ("ost", ost_i[0] % 3)
                    ost_i[0] += 1
                    evac(o[:, 0:GC], ps, [bkey], [ok])
                    if t < 8:
                        dma("sp", o_kbp[128 * (t - 4):128 * (t - 3), c1:c1 + GC], o[:, 0:GC], [ok],
                            [("okbp", t, c1)], "outst%d" % ok[1])
                    else:
                        dma("sp", o_kbs[480:512, c1:c1 + GC], o[0:32, 0:GC], [ok], [("okbs", c1)], "outst%d" % ok[1])
                tok_major(wbf, wkeys, GC, hTov, hokeys, NT_OWN, sink)
            elif g * GC < 3072:
                def sinkh(t, ps, bkey, c1=c1):
                    evac(vbv[:, t, c1 // 128:c1 // 128 + GC // 128, 0:128], ps.rearrange("p (h d) -> p h d", h=GC // 128),
                         [bkey], [("vb", t, c1)])
                tok_major(wbf, wkeys, GC, hThv, hhkeys, 4, sinkh)

                def sink(t, ps, bkey, c1=c1):
                    o = ost[ost_i[0] % 3]
                    ok = ("ost", ost_i[0] % 3)
                    ost_i[0] += 1
                    evac(o[:, 0:GC], ps, [bkey], [ok])
                    cp("pool", vbv[:, 4 + t, c1 // 128:c1 // 128 + GC // 128, 0:128],
                       o[:, 0:GC].rearrange("p (h d) -> p h d", h=GC // 128), [ok], [("vb", 4 + t, c1)])
                    if 4 <= t < 8:
                        dma("sp", o_vbp[128 * (t - 4):128 * (t - 3), c1:c1 + GC], o[:, 0:GC], [ok],
                            [("ovbp", t, c1)], "outst%d" % ok[1])
                    elif t == 8:
                        dma("sp", o_vbs[480:512, c1:c1 + GC], o[0:32, 0:GC], [ok], [("ovbs", c1)], "outst%d" % ok[1])
                tok_major(wbf, wkeys, GC, hTov, hokeys, NT_OWN, sink)
            else:
                def sink(t, ps, bkey, c1=c1):
                    silu_to(szbv[:, t, c1:c1 + GC], ps, bkey, ("szb", t, c1))
                tok_major(wbf, wkeys, GC, hTov, hokeys, NT_OWN, sink)
        dma("pool", o_kbs[0:480, :], ckb[32:512, :], (), [("okbs_c",)], "outc")
        dma("pool", o_vbs[0:480, :], cvb[32:512, :], (), [("ovbs_c",)], "outc")
        P.barrier()
        AR.pop()

        ozTb = AR.top_bf16(8 * OWN)
        ozTbv = ozTb.rearrange("p (h t) -> p h t", h=8)
        AR.push()
        Pbufs = [AR.bf16(1024) for _ in range(4)]
        tmpb = [AR.f32(1024) for _ in range(2)]
        hkr = [AR.f32(128) for _ in range(4)]
        hki = [0]

        def load_bias_ring(dstb, dkey, src, offset, nq, mask_ap, maskkey, nk=128):
            i = hki[0]
            hki[0] += 1
            load_bias_tile(dstb, dkey, hkr[i % 4], ("hkr", i % 4), src, offset, nq, mask_ap, maskkey, nk=nk,
                           bank_idx=5 + (i % 2))
        maskB = AR.f32(5 * 128)
        maskBv = maskB.rearrange("p (s q) -> p s q", s=5)
        dma("sp", maskBv, c_maskB.rearrange("s p q -> p s q"), (), ["maskB"], "c0")
        bsBall = AR.f32(5 * 8 * 32)
        bsBv = bsBall.rearrange("p (s h q) -> p s h q", s=5, h=8)
        KbTc = AR.bf16(8 * 512)
        Vbc = AR.bf16(4 * 8 * 130)
        AR.push()
        cst = [AR.f32(1024) for _ in range(2)]
        KbTcv = KbTc.rearrange("p (h n) -> p h n", h=8)
        Vbcv = Vbc.rearrange("p (t h d) -> p t h d", t=4, h=8)
        memset("pool", Vbcv[:, :, :, 128:130], 1.0, ["Vbc1"])
        for t in range(4):
            sl = t % 2
            dma("sp", cst[sl], ckb[128 * t:128 * (t + 1), :], (), [("cst", sl)], "cst%d" % sl)
            for hh in range(2):
                bk, bkey = next_bank()
                for j in range(4):
                    h = 4 * hh + j
                    tr(bk[:, 128 * j:128 * (j + 1)], cst[sl][:, 128 * h:128 * (h + 1)], ident_f, [("cst", sl), "ident_f"],
                       [bkey])
                evac(KbTcv[:, 4 * hh:4 * hh + 4, 128 * t:128 * (t + 1)], bk.rearrange("p (h n) -> p h n", h=4),
                     [bkey], [("KbTc", t)])
        for t in range(4):
            sl = t % 2
            dma("sp", cst[sl], cvb[128 * t:128 * (t + 1), :], (), [("cst", sl)], "cst%d" % sl)
            cp("pool", Vbcv[:, t, :, 0:128], cst[sl].rearrange("p (h d) -> p h d", h=8), [("cst", sl)], [("Vbc", t)])
        P.barrier()
        AR.pop()

        def kb_keys(h, c0, n):
            ks = []
            for (a, b_) in ((0, 512), (512, 1024), (1024, 1536), (1536, 1664)):
                if c0 < b_ and c0 + n > a:
                    ks.append(("kbT", h, a))
            return ks

        biasBall = AR.f32(5 * 8 * 128)
        bBv = biasBall.rearrange("p (s h q) -> p s h q", s=5, h=8)
        o1all = AR.f32(8 * 130)
        oball = AR.bf16(8 * 128)
        rall = AR.f32(8)
        o1v = o1all.rearrange("p (h d) -> p h d", h=8)
        obv = oball.rearrange("p (h d) -> p h d", h=8)
        for h in range(8):
            for s_ in range(5):
                load_bias_ring(bBv[:, s_, h, :], ("biasBall", s_), gb_s[h:h + 1, :], 512 - 128 * s_, 128,
                               maskBv[:, s_, :], "maskB")
        SpsB = [(psA, [("ps", "A", 0), ("ps", "A", 1)]), (psB, [("ps", "B", 0), ("ps", "B", 1)])]
        OkB = [("ps", "O", 0), ("ps", "O", 1), ("ps", "O", 2)]
        steps = [(p, s_) for p in range(8) for s_ in range(5)]

        def b_qk(i):
            p, s_ = steps[i]
            w = p + s_
            sp, skeys = SpsB[i % 2]
            for h in range(8):
                mm(sp[:, 128 * h:128 * (h + 1)], kbTv[:, h, 128 * w:128 * (w + 1)], qbTv[:, h, 128 * p:128 * (p + 1)],
                   True, True, [], [skeys[h // 4]])

        def b_exp(i):
            p, s_ = steps[i]
            w = p + s_
            sp, skeys = SpsB[i % 2]
            tb = tmpb[i % 2]
            tk = ("tmpb", i % 2)
            pb = Pbufs[i % 4]
            pk = ("Pb", i % 4)
            tt("dve", tb, sp[:, :], bBv[:, s_, :, :].rearrange("p h q -> p (h q)"), ALU.add,
               skeys + [("biasBall", s_)], [tk])
            act(pb, tb, AF.Exp, [tk, "visB"], [pk], bias=visB[:, w:w + 1])

        def b_pv(i):
            p, s_ = steps[i]
            w = p + s_
            pb = Pbufs[i % 4]
            pk = ("Pb", i % 4)
            for h in range(8):
                mm(acc_ap(h, 128), pb[:, 128 * h:128 * (h + 1)], vbv[:, w, h, :], s_ == 0 and h % 3 == 0, s_ == 4,
                   [pk], [OkB[h // 3]])
            if s_ == 4:
                b_epilogue(p)

        def b_epilogue(p, r0=None, n=128):
            if r0 is None:
                r0 = 128 * p
            k = "o1all"
            cp("act", o1all[0:n, 0:390], psO[0:n, 0:390], [OkB[0]], [k])
            cp("act", o1all[0:n, 390:780], psO[0:n, 512:902], [OkB[1]], [k])
            cp("act", o1all[0:n, 780:1040], psO[0:n, 1024:1284], [OkB[2]], [k])
            P.op("dve", lambda E: E.reciprocal(rall[0:n, 0:8], o1v[0:n, :, 128]), [k], ["rall"])
            for h in range(8):
                stt(obv[0:n, h, :], o1v[0:n, h, 0:128], rall[0:n, h:h + 1], szbv[0:n, p, 128 * h:128 * (h + 1)],
                    ALU.mult, ALU.mult, [k, "rall"], ["oball"])
            for h in range(8):
                tr(psT[:, 128 * h:128 * h + n], obv[0:n, h, :], ident_b[0:n, 0:n], ["oball", "ident_b"], [KT_T])
            cp("act", ozTbv[:, :, r0:r0 + n], psT.rearrange("p (h t) -> p h t", h=8)[:, :, 0:n], [KT_T],
               [("ozTb", "p", p)])

        b_qk(0)
        for i in range(len(steps)):
            if i + 1 < len(steps):
                b_qk(i + 1)
            b_exp(i)
            if i >= 1:
                b_pv(i - 1)
        b_pv(len(steps) - 1)

        for h in range(8):
            for s_ in range(4):
                load_bias_ring(bsBv[:, s_, h, :], ("bsBall", s_), gb_s[h:h + 1, :], 512 - 128 * s_, 32, None, None)
            load_bias_ring(bsBv[:, 4, h, :], ("bsBall", 4), gb_s[h:h + 1, :], 0, 32, None, None, nk=32)
        i0 = len(steps)

        def sb_nk(t):
            return 128 if t < 4 else 32

        def sb_qk(t):
            i = i0 + t
            sp, skeys = SpsB[i % 2]
            nk = sb_nk(t)
            for h in range(8):
                kT = KbTcv[:, h, 128 * t:128 * (t + 1)] if t < 4 else kbTv[:, h, 512 + 1024:512 + 1056]
                mm(sp[0:nk, 32 * h:32 * h + 32], kT, qbTv[:, h, 1024:1056], True, True, [("KbTc", t)], [skeys[0]])

        def sb_exp(t):
            i = i0 + t
            sp, skeys = SpsB[i % 2]
            nk = sb_nk(t)
            tb = tmpb[i % 2]
            tk = ("tmpb", i % 2)
            pb = Pbufs[i % 4]
            pk = ("Pb", i % 4)
            tt("dve", tb[0:nk, 0:256], sp[0:nk, 0:256], bsBv[0:nk, t, :, :].rearrange("p h q -> p (h q)"), ALU.add,
               [skeys[0], ("bsBall", t)], [tk])
            act(pb[0:nk, 0:256], tb[0:nk, 0:256], AF.Exp, [tk, "visB"], [pk], bias=visB[0:nk, 15:16])

        def sb_pv(t):
            i = i0 + t
            nk = sb_nk(t)
            pb = Pbufs[i % 4]
            pk = ("Pb", i % 4)
            for h in range(8):
                v = Vbcv[:, t, h, :] if t < 4 else vbv[0:32, 12, h, :]
                mm(acc_ap(h, 32), pb[0:nk, 32 * h:32 * h + 32], v, t == 0 and h % 3 == 0, t == 4,
                   [pk, ("Vbc", t), "Vbc1"], [OkB[h // 3]])

        sb_qk(0)
        for t in range(5):
            if t + 1 < 5:
                sb_qk(t + 1)
            sb_exp(t)
            if t >= 1:
                sb_pv(t - 1)
        sb_pv(4)
        b_epilogue(8, r0=1024, n=32)
        P.barrier()
        AR.pop()
        AR.pop()

        AR.push()
        hTo, _ = own_prep(AR, False)
        hTov = hTo.rearrange("p (c t) -> p c t", c=16)
        hokeys = [("hTo", t) for t in range(NT_OWN)]
        dma("sp", gvec, post.partition_broadcast(128), (), ["gvec"], "c0")
        mT = AR.bf16(16 * OWN)
        mTv = mT.rearrange("p (c t) -> p c t", c=16)
        AR.push()
        wgst = [AR.f32(16 * 128) for _ in range(2)]
        wgbf = [AR.bf16(16 * 128) for _ in range(4)]
        wost = [AR.f32(8 * 128) for _ in range(2)]
        wobf = [AR.bf16(8 * 128) for _ in range(4)]
        sga = [AR.f32(512) for _ in range(2)]
        sgb = [AR.f32(512) for _ in range(2)]
        ya = [AR.f32(512) for _ in range(2)]
        k5 = [0]
        for cc in range(16):
            wk = {}
            for gi_, (c0, nm) in enumerate(((8192, "ga"), (10240, "gb"))):
                sl = (2 * cc + gi_) % 2
                sl4 = (2 * cc + gi_) % 4
                wk[nm] = (wgbf[sl4], load_wgroup(w_in[:, c0 + 128 * cc:c0 + 128 * (cc + 1)], 128, wgst[sl], wgbf[sl4],
                                                 ("wg5", sl4), "wg5%d" % sl4))
            for gi_, (wsrc, nm) in enumerate(((w_oa, "oa"), (w_ob, "ob"))):
                sl = (2 * cc + gi_) % 2
                sl4 = (2 * cc + gi_) % 4
                wk[nm] = (wobf[sl4], load_wgroup(wsrc[:, 128 * cc:128 * (cc + 1)], 128, wost[sl], wobf[sl4],
                                                 ("wo5", sl4), "wo5%d" % sl4, nk=8))
            for (t0, n) in tok_blocks:
                i2 = k5[0] % 2
                k5[0] += 1
                sig = {}
                for nm, sbuf_ in (("ga", sga[i2]), ("gb", sgb[i2])):
                    wbf, wkeys = wk[nm]
                    wv = wbf.rearrange("p (c n) -> p c n", c=16)
                    bk, bkey = next_bank()
                    for dc in range(16):
                        mm(bk[:, 0:n], wv[:, dc, :], hTov[:, dc, t0:t0 + n], dc == 0, dc == 15, wkeys + hokeys, [bkey])
                    sk = ("sg", nm, i2)
                    act(sbuf_[:, 0:n], bk[:, 0:n], AF.Sigmoid, [bkey], [sk])
                    sig[nm] = (sbuf_, sk)
                yk = ("ya", i2)
                for nm, ozv, oznm, gnm in (("oa", ozTav, "ozTa", "ga"), ("ob", ozTbv, "ozTb", "gb")):
                    wbf, wkeys = wk[nm]
                    wv = wbf.rearrange("p (c n) -> p c n", c=8)
                    bk, bkey = next_bank()
                    for h in range(8):
                        mm(bk[:, 0:n], wv[:, h, :], ozv[:, h, t0:t0 + n], h == 0, h == 7, wkeys, [bkey])
                    sb_, sk = sig[gnm]
                    if nm == "oa":
                        tt("dve", ya[i2][:, 0:n], bk[:, 0:n], sb_[:, 0:n], ALU.mult, [bkey, sk], [yk])
                    else:
                        tt("dve", sb_[:, 0:n], bk[:, 0:n], sb_[:, 0:n], ALU.mult, [bkey, sk], [sk])
                        tt("dve", mTv[:, cc, t0:t0 + n], sb_[:, 0:n], ya[i2][:, 0:n], ALU.add, [sk, yk], [("mT", cc, t0)])
        P.barrier()
        AR.pop()
        AR.n = ARENA_WORDS
        wout_bf_full = AR.bf16(16 * 2048)
        wost2 = [AR.f32(2048) for _ in range(2)]
        woutv = wout_bf_full.rearrange("p (c n) -> p c n", c=16)
        for dc in range(16):
            sl = dc % 2
            dma("pool", woutv[:, dc, :], w_out[128 * dc:128 * (dc + 1), :], (), [("wout", dc)], "wout")
        wokeys = [("wout", dc) for dc in range(16)]
        stage = hTo.bitcast(F32)
        yrow = [stage[:, 0:2048], stage[:, 2048:4096]]
        xrow = [stage[:, 4096:6144], stage[:, 6144:8192]]
        sq5 = stage[:, 8192:8200]
        junk5 = AR.f32(1024)
        for t in range(NT_OWN):
            i2 = t % 2
            nrow = min(128, 1056 - 128 * t)
            yk = ("yrow", i2)
            xk = ("xrow", i2)
            dma("sp", xrow[i2], xo[128 * t:128 * (t + 1), :], (), [xk], "xr%d" % i2)
            for cg in range(4):
                bk, bkey = next_bank()
                for kc in range(16):
                    mm(bk, mTv[:, kc, 128 * t:128 * (t + 1)], woutv[:, kc, 512 * cg:512 * (cg + 1)], kc == 0, kc == 15,
                       wokeys, [bkey])
                evac(yrow[i2][:, 512 * cg:512 * (cg + 1)], bk, [bkey], [yk])
            sq = sq5[:, 4 * i2:4 * i2 + 2]
            sk = ("sq5", i2)
            for hh in range(2):
                P.op("dve", lambda E, i2=i2, hh=hh: E.scalar_tensor_tensor(
                    junk5, yrow[i2][:, 1024 * hh:1024 * (hh + 1)], 1.0, yrow[i2][:, 1024 * hh:1024 * (hh + 1)],
                    ALU.mult, ALU.mult, accum_out=sq5[:, 4 * i2 + 2 + hh:4 * i2 + 3 + hh]), [yk], ["junk5", sk])
            tt("dve", sq[:, 0:1], sq5[:, 4 * i2 + 2:4 * i2 + 3], sq5[:, 4 * i2 + 3:4 * i2 + 4], ALU.add, [sk], [sk])
            act(sq[:, 0:1], sq[:, 0:1], AF.Ln, [sk], [sk], bias=EPS, scale=1.0 / D)
            act(sq[:, 1:2], sq[:, 0:1], AF.Exp, [sk], [sk], scale=-0.5)
            stt(yrow[i2], yrow[i2], sq[:, 1:2], gvec, ALU.mult, ALU.mult, [yk, sk, "gvec"], [yk])
            tt("pool", yrow[i2], yrow[i2], xrow[i2], ALU.add, [yk, xk], [yk])
            dma("sp", o_y[128 * t:128 * t + nrow, :], yrow[i2][0:nrow, :], [yk], [("oy", t)], "outy")
        P.barrier()
        AR.pop()

        P.barrier()
        sem_names = P.sem_names()
        sem_ctx = [nc.semaphore("s%d" % i) for i in range(len(sem_names))]
        sems = {}
        import contextlib
        with contextlib.ExitStack() as stack:
            for nm, c in zip(sem_names, sem_ctx):
                sems[nm] = stack.enter_context(c)
            block = stack.enter_context(nc.Block())

            def replay(E, eng):
                for waits, fn, tok in P.ops[eng]:
                    for s, v in waits:
                        E.wait_ge(sems[s], v)
                    if fn is None:
                        continue
                    ins = fn(E)
                    ins.then_inc(sems[tok[0]], 16 if tok[0].startswith("dma:") else 1)

            @block.tensor
            def _(E):
                replay(E, "pe")

            @block.scalar
            def _(E):
                replay(E, "act")

            @block.vector
            def _(E):
                replay(E, "dve")

            @block.gpsimd
            def _(E):
                replay(E, "pool")

            @block.sync
            def _(E):
                replay(E, "sp")
    return nc


def _constants():
    c = {}
    c["c_ident"] = np.eye(128, dtype=np.float32)
    c["c_antij"] = np.ascontiguousarray(np.eye(128, dtype=np.float32)[::-1])
    u = np.arange(FA_LEN)
    d = u - 511
    bk = _t5_bucket_np(-d)
    oh = np.zeros((32, FA_LEN), np.float32)
    oh[bk, u] = 1.0
    oh[15, :] -= 1.0
    c["c_oha"] = oh
    v = np.arange(GB_LEN)
    idx = np.clip(127 - v, -128, 128) + 128
    ohb = np.zeros((384, GB_LEN), np.float32)
    ohb[idx, v] = 1.0
    c["c_ohb"] = ohb
    i = np.arange(128)[:, None]
    j = np.arange(512)[None, :]
    mA = np.zeros((5, 128, 512), np.float32)
    for s in range(5):
        krel = (s - 1) * 128 + i
        mA[s] = np.where(np.floor_divide(krel, 64) > (j // 64), NEG, 0.0)
    c["c_maskA"] = mA
    j2 = np.arange(128)[None, :]
    mB = np.zeros((5, 128, 128), np.float32)
    for s in range(5):
        kc = (128 * s + i) // 64 - 8
        qc = j2 // 64
        ok = (kc <= qc) & (kc >= qc - 8)
        mB[s] = np.where(ok, 0.0, NEG)
    c["c_maskB"] = mB
    return c


def _vis_tables(core):
    visA = np.zeros((128, 128), np.float32)
    for qb in range(2):
        T0 = 8 * core + 4 * qb
        for s in range(64):
            tile = T0 - 1 + s
            if tile >= 64:
                visible = True
            else:
                visible = (0 <= tile <= T0 + 3)
            visA[:, 64 * qb + s] = 0.0 if visible else NEG
    visB = np.zeros((128, 16), np.float32)
    if core == 0:
        visB[:, 0:4] = NEG
    return visA, visB


_NC_CACHE = {}


def kernel(x_prompt, x_sample, cache_k_a, cache_v_a, cache_k_b, cache_v_b, t5_bias, pre_norm, post_norm,
           w_in, lambda_q1, lambda_k1, lambda_q2, lambda_k2, subln_a, rel_bias_b, w_o_a, w_o_b, w_out):
    f = lambda a: np.ascontiguousarray(np.asarray(a, dtype=np.float32))
    xp = f(x_prompt)[0]
    xs = f(x_sample)
    consts = _constants()
    relbT = np.zeros((384, 8), np.float32)
    relbT[:257] = f(rel_bias_b)[0].T
    lam4 = np.concatenate([f(lambda_q1)[0], f(lambda_k1)[0], f(lambda_q2)[0], f(lambda_k2)[0]])[None, :]
    shared = {
        "w_in": f(w_in)[0], "w_oa": f(w_o_a)[0], "w_ob": f(w_o_b)[0], "w_out": f(w_out)[0],
        "pre": f(pre_norm), "post": f(post_norm), "subln": f(subln_a), "lam4": np.ascontiguousarray(lam4),
        "t5": f(t5_bias), "relbT": relbT,
    }
    shared.update(consts)
    in_maps = []
    for c in range(NCORES):
        xo = np.zeros((OWN, D), np.float32)
        xo[:ROWS] = xp[ROWS * c:ROWS * (c + 1)]
        xo[ROWS:ROWS + NS] = xs[c]
        xh = np.zeros((512, D), np.float32)
        if c > 0:
            xh[:] = xp[ROWS * c - 512:ROWS * c]
        visA, visB = _vis_tables(c)
        m = dict(shared)
        m.update({
            "xf": np.ascontiguousarray(np.roll(xp, -128 * (8 * c - 1), axis=0)),
            "xo": xo, "xh": xh,
            "cka": f(cache_k_a)[0, c].reshape(PAST, 1024), "cva": f(cache_v_a)[0, c].reshape(PAST, 1024),
            "ckb": f(cache_k_b)[0, c].reshape(512, 1024), "cvb": f(cache_v_b)[0, c].reshape(512, 1024),
            "c_visA": visA, "c_visB": visB,
        })
        in_maps.append(m)
    if "nc" not in _NC_CACHE:
        _NC_CACHE["nc"] = build()
    nc = _NC_CACHE["nc"]
    res = run_bass_kernel_spmd(nc, in_maps, core_ids=list(range(NCORES)))
    R = res.results
    y_prompt = np.concatenate([R[c]["y"][:ROWS] for c in range(NCORES)], 0)[None]
    y_sample = np.stack([R[c]["y"][ROWS:ROWS + NS] for c in range(NCORES)], 0)
    kap = np.concatenate([R[c]["ka"][:ROWS] for c in range(NCORES)], 0).reshape(1, 1, SEQ, 16, 64)
    vap = np.concatenate([R[c]["va"][:ROWS] for c in range(NCORES)], 0).reshape(1, 1, SEQ, 8, 128)
    kbp = R[NCORES - 1]["kbp"].reshape(1, 1, 512, 8, 128)
    vbp = R[NCORES - 1]["vbp"].reshape(1, 1, 512, 8, 128)
    kas = np.stack([R[c]["ka"][ROWS:ROWS + NS] for c in range(NCORES)], 0).reshape(1, NCORES, NS, 16, 64)
    vas = np.stack([R[c]["va"][ROWS:ROWS + NS] for c in range(NCORES)], 0).reshape(1, NCORES, NS, 8, 128)
    kbs = np.stack([R[c]["kbs"] for c in range(NCORES)], 0).reshape(1, NCORES, 512, 8, 128)
    vbs = np.stack([R[c]["vbs"] for c in range(NCORES)], 0).reshape(1, NCORES, 512, 8, 128)
    out = (y_prompt, y_sample, kap, vap, kbp, vbp, kas, vas, kbs, vbs)
    return tuple(np.ascontiguousarray(o.astype(np.float32)) for o in out)
```
